# Optimizing a Trainium2 kernel written in Bass

```python
import math
import jax
import jax.numpy as jnp
from jax import lax
import numpy as np

D_MODEL = 4096
BATCH = 2
SEQ = 4096
DEPTH = 1
DEC_BATCH = 128
DEC_SEQ = 1
PAST_LEN = 8192
PAGE_SIZE = 128

HEAD_DIM = 64
N_ATTN_HEADS = 32
N_KV_HEADS = 4
GQA_GROUP = N_ATTN_HEADS // N_KV_HEADS
WINDOW = 128
ATTN_BLOCK = 128
N_BUCKETS = 32
MAX_DISTANCE = 128
N_RET_HEADS = 8
RET_KDIM = 128
RET_VDIM = 256
RET_CHUNK = 128
ROPE_BASE = 10000.0
D_FF = 11008
CONV_W = 3
PLE_DIM = 256
EPS = 1e-6

ATTN_WIDTH = N_ATTN_HEADS * HEAD_DIM
KV_WIDTH = N_KV_HEADS * HEAD_DIM
RET_QK_WIDTH = N_RET_HEADS * RET_KDIM
RET_V_WIDTH = N_RET_HEADS * RET_VDIM
MIX_WIDTH = ATTN_WIDTH + RET_V_WIDTH
IN_SIZES = (ATTN_WIDTH, KV_WIDTH, KV_WIDTH, RET_QK_WIDTH, RET_QK_WIDTH, RET_V_WIDTH, RET_V_WIDTH)
IN_WIDTH = sum(IN_SIZES)
IN_SPLITS = tuple(int(s) for s in np.cumsum(IN_SIZES)[:-1])

kernel_name = 'hybrid_swa_retention_convffn_step'


def rmsnorm(x, g):
    xf = x.astype(jnp.float32)
    y = xf * lax.rsqrt(jnp.mean(xf * xf, axis=-1, keepdims=True) + EPS)
    return (y * g.astype(jnp.float32)).astype(x.dtype)


def t5_bucket(dist):
    n = jnp.maximum(dist, 0)
    max_exact = N_BUCKETS // 2
    nf = jnp.maximum(n, 1).astype(jnp.float32)
    large = max_exact + (jnp.log(nf / max_exact) / math.log(MAX_DISTANCE / max_exact)
                         * (N_BUCKETS - max_exact)).astype(jnp.int32)
    return jnp.where(n < max_exact, n, jnp.minimum(large, N_BUCKETS - 1))


def window_attention(q, k, v, q_pos, k_pos, rel_bias, sinks):
    n_blk, nq = q_pos.shape
    nk = k_pos.shape[1]
    dist = q_pos[:, :, None] - k_pos[:, None, :]
    valid = (dist >= 0) & (dist <= WINDOW) & (k_pos[:, None, :] >= 0)
    bias = jnp.moveaxis(rel_bias.astype(jnp.float32)[t5_bucket(dist)], -1, 1)
    bias = bias.reshape(n_blk, N_KV_HEADS, GQA_GROUP, nq, nk)
    s = jnp.einsum('bnqhgd,bnkhd->bnhgqk', q, k,
                   preferred_element_type=jnp.float32) * HEAD_DIM ** -0.5 + bias
    s = jnp.where(valid[:, None, None], s, -jnp.inf)
    sink = sinks.astype(jnp.float32).reshape(1, 1, N_KV_HEADS, GQA_GROUP, 1, 1)
    m = jnp.maximum(jnp.max(s, axis=-1, keepdims=True), sink)
    e = jnp.exp(s - m)
    probs = e / (jnp.sum(e, axis=-1, keepdims=True) + jnp.exp(sink - m))
    return jnp.einsum('bnhgqk,bnkhd->bnqhgd', probs.astype(v.dtype), v)


def attn_prompt(q, k, v, pos, rel_bias, sinks):
    b, t = q.shape[:2]
    nb = t // ATTN_BLOCK
    qb = q.reshape(b, nb, ATTN_BLOCK, N_KV_HEADS, GQA_GROUP, HEAD_DIM)

    def with_prev(a):
        a = a.reshape(b, nb, ATTN_BLOCK, N_KV_HEADS, HEAD_DIM)
        prev = jnp.concatenate([jnp.zeros_like(a[:, :1]), a[:, :-1]], axis=1)
        return jnp.concatenate([prev, a], axis=2)

    q_pos = pos.reshape(nb, ATTN_BLOCK)
    k_pos = jnp.concatenate([q_pos - ATTN_BLOCK, q_pos], axis=1)
    o = window_attention(qb, with_prev(k), with_prev(v), q_pos, k_pos, rel_bias, sinks)
    return o.reshape(b, t, ATTN_WIDTH)


def attn_sample(q, k, v, win_k, win_v, pos, rel_bias, sinks):
    b, s = q.shape[:2]
    wb = win_k.shape[1]
    kk = jnp.concatenate([win_k.astype(k.dtype), k], axis=1)
    vv = jnp.concatenate([win_v.astype(v.dtype), v], axis=1)
    q_pos = pos[None]
    k_pos = (pos[0] - wb + jnp.arange(wb + s, dtype=jnp.int32))[None]
    o = window_attention(q.reshape(b, 1, s, N_KV_HEADS, GQA_GROUP, HEAD_DIM),
                         kk[:, None], vv[:, None], q_pos, k_pos, rel_bias, sinks)
    return o.reshape(b, s, ATTN_WIDTH), kk[:, -wb:], vv[:, -wb:]


def rotary(x, pos):
    half = x.shape[-1] // 2
    inv = ROPE_BASE ** (-jnp.arange(half, dtype=jnp.float32) / half)
    ang = pos.astype(jnp.float32)[:, None] * inv[None]
    cos = jnp.cos(ang)[None, :, None, :]
    sin = jnp.sin(ang)[None, :, None, :]
    xf = x.astype(jnp.float32)
    x1, x2 = xf[..., :half], xf[..., half:]
    return jnp.concatenate([x1 * cos - x2 * sin, x2 * cos + x1 * sin], axis=-1).astype(x.dtype)


def retention_chunk(s0, q, k, v, log_decay):
    L = q.shape[1]
    idx = jnp.arange(L, dtype=jnp.float32)
    diff = idx[:, None] - idx[None, :]
    dmask = jnp.where(diff[None] >= 0, jnp.exp(diff[None] * log_decay[:, None, None]), 0.0)
    qf, kf, vf = q.astype(jnp.float32), k.astype(jnp.float32), v.astype(jnp.float32)
    scores = jnp.einsum('blhd,bmhd->bhlm', qf, kf) * dmask
    o_intra = jnp.einsum('bhlm,bmhe->blhe', scores, vf)
    q_dec = jnp.exp((idx + 1.0)[:, None] * log_decay[None])
    o_inter = jnp.einsum('blhd,bhde->blhe', qf * q_dec[None, :, :, None], s0)
    k_dec = jnp.exp((L - 1.0 - idx)[:, None] * log_decay[None])
    s1 = (jnp.exp(L * log_decay)[None, :, None, None] * s0
          + jnp.einsum('blhd,blhe->bhde', kf * k_dec[None, :, :, None], vf))
    return s1, o_intra + o_inter


def retention(s0, q, k, v, log_decay):
    b, t = q.shape[:2]
    chunk = RET_CHUNK if t % RET_CHUNK == 0 else t
    nc = t // chunk

    def to_chunks(a):
        return jnp.swapaxes(a.reshape(b, nc, chunk, *a.shape[2:]), 0, 1)

    def step(s, qkv):
        return retention_chunk(s, qkv[0], qkv[1], qkv[2], log_decay)

    s1, o = lax.scan(step, s0, (to_chunks(q), to_chunks(k), to_chunks(v)))
    return s1, jnp.swapaxes(o, 0, 1).reshape(b, t, N_RET_HEADS, RET_VDIM)


def conv_ffn(h, prev, w_up, conv_w, conv_b, w_down):
    u = h @ w_up
    t = u.shape[1]
    up = jnp.concatenate([prev.astype(u.dtype), u], axis=1)
    c = conv_b.astype(u.dtype)
    for j in range(CONV_W):
        c = c + conv_w[j] * up[:, j:j + t]
    g, val = jnp.split(c, 2, axis=-1)
    y = (jax.nn.gelu(g, approximate=False) * val) @ w_down
    return y, up[:, -(CONV_W - 1):]


def decoder_layer(x, p, pos, past, lp, rel_bias, log_decay):
    (g_mix, w_in, g_q, g_k, sinks, w_out, g_ffn, w_up, conv_w, conv_b, w_down,
     g_ple, w_ple_gate, w_ple_proj) = lp
    b, t, _ = x.shape
    h = rmsnorm(x, g_mix)
    aq, ak, av, rq, rk, rv, rg = jnp.split(h @ w_in, IN_SPLITS, axis=-1)
    aq = rmsnorm(aq.reshape(b, t, N_ATTN_HEADS, HEAD_DIM), g_q)
    ak = rmsnorm(ak.reshape(b, t, N_KV_HEADS, HEAD_DIM), g_k)
    av = av.reshape(b, t, N_KV_HEADS, HEAD_DIM)
    rq = rotary(rq.reshape(b, t, N_RET_HEADS, RET_KDIM), pos)
    rk = rotary(rk.reshape(b, t, N_RET_HEADS, RET_KDIM), pos) * RET_KDIM ** -0.5
    rv = rv.reshape(b, t, N_RET_HEADS, RET_VDIM)
    if past is None:
        attn = attn_prompt(aq, ak, av, pos, rel_bias, sinks)
        new_k, new_v = ak[:, -WINDOW:], av[:, -WINDOW:]
        s0 = jnp.zeros((b, N_RET_HEADS, RET_KDIM, RET_VDIM), jnp.float32)
        conv_prev = jnp.zeros((b, CONV_W - 1, 2 * D_FF), x.dtype)
    else:
        win_k, win_v, s_past, conv_prev = past
        attn, new_k, new_v = attn_sample(aq, ak, av, win_k, win_v, pos, rel_bias, sinks)
        s0 = s_past.astype(jnp.float32)
    s1, ro = retention(s0, rq, rk, rv, log_decay)
    ro = ro * lax.rsqrt(jnp.mean(ro * ro, axis=-1, keepdims=True) + EPS)
    ret = (jax.nn.silu(rg.astype(jnp.float32)) * ro.reshape(b, t, RET_V_WIDTH)).astype(x.dtype)
    x = x + jnp.concatenate([attn.astype(x.dtype), ret], axis=-1) @ w_out
    f, conv_new = conv_ffn(rmsnorm(x, g_ffn), conv_prev, w_up, conv_w, conv_b, w_down)
    x = x + f
    gate = jax.nn.sigmoid(rmsnorm(x, g_ple) @ w_ple_gate)
    x = x + gate * (p @ w_ple_proj)
    return x, new_k, new_v, s1, conv_new


def setup_inputs(seed: int = 0) -> dict:
    key = jax.random.key(seed)
    ks = jax.random.split(key, 24)
    f32 = jnp.float32
    win = min(WINDOW, PAST_LEN)

    def nrm(k, shape, scale):
        return jax.random.normal(k, shape, f32) * scale

    return {
        'x_prompt': nrm(ks[0], (BATCH, SEQ, D_MODEL), 1.0),
        'x_sample': nrm(ks[1], (DEC_BATCH, DEC_SEQ, D_MODEL), 1.0),
        'p_prompt': nrm(ks[2], (DEPTH, BATCH, SEQ, PLE_DIM), 1.0),
        'p_sample': nrm(ks[3], (DEPTH, DEC_BATCH, DEC_SEQ, PLE_DIM), 1.0),
        'cache_win_k': nrm(ks[4], (DEPTH, DEC_BATCH, win, N_KV_HEADS, HEAD_DIM), 1.0),
        'cache_win_v': nrm(ks[5], (DEPTH, DEC_BATCH, win, N_KV_HEADS, HEAD_DIM), 1.0),
        'state_ret': nrm(ks[6], (DEPTH, DEC_BATCH, N_RET_HEADS, RET_KDIM, RET_VDIM), 1.0),
        'state_conv': nrm(ks[7], (DEPTH, DEC_BATCH, CONV_W - 1, 2 * D_FF), 1.0),
        'rel_bias': nrm(ks[8], (N_BUCKETS, N_ATTN_HEADS), 0.5),
        'g_mix': 1.0 + nrm(ks[9], (DEPTH, D_MODEL), 0.02),
        'w_in': nrm(ks[10], (DEPTH, D_MODEL, IN_WIDTH), D_MODEL ** -0.5),
        'g_q': 1.0 + nrm(ks[11], (DEPTH, HEAD_DIM), 0.02),
        'g_k': 1.0 + nrm(ks[12], (DEPTH, HEAD_DIM), 0.02),
        'sinks': nrm(ks[13], (DEPTH, N_ATTN_HEADS), 0.5),
        'w_out': nrm(ks[14], (DEPTH, MIX_WIDTH, D_MODEL), MIX_WIDTH ** -0.5),
        'g_ffn': 1.0 + nrm(ks[15], (DEPTH, D_MODEL), 0.02),
        'w_up': nrm(ks[16], (DEPTH, D_MODEL, 2 * D_FF), D_MODEL ** -0.5),
        'conv_w': nrm(ks[17], (DEPTH, CONV_W, 2 * D_FF), CONV_W ** -0.5),
        'conv_b': nrm(ks[18], (DEPTH, 2 * D_FF), 0.01),
        'w_down': nrm(ks[19], (DEPTH, D_FF, D_MODEL), D_FF ** -0.5),
        'g_ple': 1.0 + nrm(ks[20], (DEPTH, D_MODEL), 0.02),
        'w_ple_gate': nrm(ks[21], (DEPTH, D_MODEL, D_MODEL), D_MODEL ** -0.5),
        'w_ple_proj': nrm(ks[22], (DEPTH, PLE_DIM, D_MODEL), PLE_DIM ** -0.5),
    }


def reference(x_prompt, x_sample, p_prompt, p_sample, cache_win_k, cache_win_v, state_ret, state_conv,
              rel_bias, g_mix, w_in, g_q, g_k, sinks, w_out, g_ffn, w_up, conv_w, conv_b, w_down,
              g_ple, w_ple_gate, w_ple_proj):
    log_decay = jnp.log(1.0 - 2.0 ** (-5.0 - jnp.arange(N_RET_HEADS, dtype=jnp.float32)))
    pos_prompt = jnp.arange(x_prompt.shape[1], dtype=jnp.int32)
    pos_sample = PAST_LEN + jnp.arange(x_sample.shape[1], dtype=jnp.int32)
    yp, ys = x_prompt, x_sample
    kp_l, vp_l, rp_l, cp_l, ks_l, vs_l, rs_l, cs_l = [], [], [], [], [], [], [], []
    for l in range(DEPTH):
        lp = (g_mix[l], w_in[l], g_q[l], g_k[l], sinks[l], w_out[l], g_ffn[l], w_up[l], conv_w[l],
              conv_b[l], w_down[l], g_ple[l], w_ple_gate[l], w_ple_proj[l])
        yp, kp, vp, rp, cp = decoder_layer(yp, p_prompt[l], pos_prompt, None, lp, rel_bias, log_decay)
        past = (cache_win_k[l], cache_win_v[l], state_ret[l], state_conv[l])
        ys, ksn, vsn, rsn, csn = decoder_layer(ys, p_sample[l], pos_sample, past, lp, rel_bias, log_decay)
        kp_l.append(kp.astype(cache_win_k.dtype))
        vp_l.append(vp.astype(cache_win_v.dtype))
        rp_l.append(rp.astype(state_ret.dtype))
        cp_l.append(cp.astype(state_conv.dtype))
        ks_l.append(ksn.astype(cache_win_k.dtype))
        vs_l.append(vsn.astype(cache_win_v.dtype))
        rs_l.append(rsn.astype(state_ret.dtype))
        cs_l.append(csn.astype(state_conv.dtype))
    return (yp, ys, jnp.stack(kp_l), jnp.stack(vp_l), jnp.stack(rp_l), jnp.stack(cp_l),
            jnp.stack(ks_l), jnp.stack(vs_l), jnp.stack(rs_l), jnp.stack(cs_l))
```

```python
import os
import math
import numpy as np
import concourse.bass as bass
import concourse.mybir as mybir
from concourse.bass_utils import run_bass_kernel_spmd

F32 = mybir.dt.float32
BF16 = mybir.dt.bfloat16
U8 = mybir.dt.uint8
AF = mybir.ActivationFunctionType
ALU = mybir.AluOpType

STAGE = float(os.environ.get("MK_STAGE", "99"))
SES = os.environ.get("MK_SES", "1") == "1"
SKIP = set(os.environ.get("MK_SKIP", "").split(","))

D = 4096
NKC = 32
AQ0, AK0, AV0, RQ0, RK0, RV0, RG0 = 0, 2048, 2304, 2560, 3584, 4608, 6656
INW = 8704
FF = 11008
NFC = 86
EPS = 1e-6
NPREV = 23
PREV_PASS = [6, 6, 6, 5]
GROUPS = [
    dict(name="A", row0=0, ncol=768, blocks=[0, 1, 2, 3, 4, 5], q0=128, f0=254, nsmp=0),
    dict(name="B", row0=768, ncol=528, blocks=[6, 7, 8, 9], q0=0, f0=0, nsmp=16),
]
FCP = 8


class Tok:
    __slots__ = ("name", "w", "r")

    def __init__(self, name=""):
        self.name = name
        self.w = None
        self.r = []


class Op:
    __slots__ = ("eng", "fn", "deps", "dma", "mark", "cnt", "sem", "semval", "prev_same_sem")

    def __init__(self, eng, fn, deps, dma):
        self.eng = eng
        self.fn = fn
        self.deps = deps
        self.dma = dma
        self.mark = False
        self.cnt = 0
        self.sem = None
        self.semval = 0
        self.prev_same_sem = None


class Prog:
    ENGS = ("pe", "act", "dve", "pool", "sp")

    def __init__(self, nc, n_dma_sems=8, same_engine_sync=True):
        self.nc = nc
        self.ops = []
        self.same_engine_sync = same_engine_sync
        self.n_dma_sems = n_dma_sems
        self.last = {e: None for e in self.ENGS}
        self.dma_since = []

    def op(self, eng, fn, reads=(), writes=(), dma=False, extra=()):
        idx = len(self.ops)
        deps = set(extra)
        for t in reads:
            if t.w is not None:
                deps.add(t.w)
        for t in writes:
            if t.w is not None:
                deps.add(t.w)
            deps.update(t.r)
        for t in reads:
            t.r.append(idx)
        for t in writes:
            t.w = idx
            t.r = []
        deps.discard(idx)
        self.ops.append(Op(eng, fn, deps, dma))
        if os.environ.get("MK_SITES"):
            import sys as _sys
            f = _sys._getframe(1)
            st = []
            while f is not None and len(st) < 5:
                st.append(f.f_lineno)
                f = f.f_back
            self.sites = getattr(self, "sites", {})
            self.sites[idx] = st
        self.last[eng] = idx
        if dma:
            self.dma_since.append(idx)
        return idx

    def barrier(self):
        deps = [v for v in self.last.values() if v is not None] + list(self.dma_since)
        self.dma_since = []
        for e in self.ENGS:
            self.op(e, lambda en: en.nop(), extra=deps)

    def emit(self):
        nc = self.nc
        ops = self.ops
        esem = {e: nc.alloc_semaphore(name=f"es_{e}") for e in self.ENGS}
        dsems = {e: [nc.alloc_semaphore(name=f"ds_{e}{i}") for i in range(self.n_dma_sems)]
                 for e in ("sp", "pool", "act")}
        dcount = {e: [0] * self.n_dma_sems for e in dsems}
        dlast = {e: [None] * self.n_dma_sems for e in dsems}
        dk = {e: 0 for e in dsems}
        for i, o in enumerate(ops):
            if o.dma:
                k = dk[o.eng] % self.n_dma_sems
                dk[o.eng] += 1
                dcount[o.eng][k] += 1
                o.sem = dsems[o.eng][k]
                o.semval = 16 * dcount[o.eng][k]
                o.prev_same_sem = dlast[o.eng][k]
                dlast[o.eng][k] = i
        waited = {}
        plan = []
        for i, o in enumerate(ops):
            wl = []
            byeng = {}
            dmaw = {}
            deps = set(o.deps)
            if o.dma and o.prev_same_sem is not None:
                deps.add(o.prev_same_sem)
            for d in deps:
                od = ops[d]
                if od.dma:
                    key = id(od.sem)
                    if key not in dmaw or dmaw[key][1] < od.semval:
                        dmaw[key] = (od.sem, od.semval)
                else:
                    if od.eng == o.eng and (o.eng == "pe" or not self.same_engine_sync):
                        continue
                    if od.eng not in byeng or byeng[od.eng] < d:
                        byeng[od.eng] = d
            for e, d in byeng.items():
                k = (o.eng, e)
                if waited.get(k, -1) >= d:
                    continue
                waited[k] = d
                ops[d].mark = True
                wl.append(("e", e, d))
            for key, (sem, val) in dmaw.items():
                k = (o.eng, key)
                if waited.get(k, -1) >= val:
                    continue
                waited[k] = val
                wl.append(("d", sem, val))
            plan.append(wl)
        cnt = {e: 0 for e in self.ENGS}
        for o in ops:
            if o.mark and not o.dma:
                cnt[o.eng] += 1
                o.cnt = cnt[o.eng]
        self.stats = dict(n_ops=len(ops), marks=dict(cnt),
                          per_eng={e: sum(1 for o in ops if o.eng == e) for e in self.ENGS})
        with nc.Block() as block:
            def run(engname):
                def body(e):
                    for i, o in enumerate(ops):
                        if o.eng != engname:
                            continue
                        for w in plan[i]:
                            if w[0] == "e":
                                e.wait_ge(esem[w[1]], ops[w[2]].cnt)
                            else:
                                e.wait_ge(w[1], w[2])
                        ins = o.fn(e)
                        if os.environ.get("MK_SITES"):
                            try:
                                self.names = getattr(self, "names", {})
                                self.names[ins.ins.name] = self.sites.get(i)
                            except Exception as ex:
                                pass
                        if o.dma:
                            ins.then_inc(o.sem, 16)
                        elif o.mark:
                            ins.then_inc(esem[engname], 1)
                return body
            block.tensor(run("pe"))
            block.scalar(run("act"))
            block.vector(run("dve"))
            block.gpsimd(run("pool"))
            block.sync(run("sp"))


class EarlyExit(Exception):
    pass


def pieces(n):
    if n <= 512:
        return [(0, n)]
    mid = ((n // 2) // 128) * 128
    if n - mid > 512:
        mid += 128
    assert mid <= 512 and n - mid <= 512
    return [(0, mid), (mid, n)]


def host_consts():
    c = {}
    c["identf"] = np.eye(128, dtype=np.float32)
    c["jrev"] = np.eye(128, dtype=np.float32)[::-1].copy()
    bd = np.zeros((128, 128), np.float32)
    bd[:64, :64] = 1.0
    bd[64:, 64:] = 1.0
    c["bd64"] = bd / 64.0
    c["bd1"] = bd.copy()
    c["ones4096"] = np.full((128, 128), 1.0 / 4096.0, np.float32)
    c["ones256"] = np.full((128, 128), 1.0 / 256.0, np.float32)
    c["ones64"] = np.ones((128, 64), np.float32)
    h = np.arange(8, dtype=np.float32)
    log_decay = np.log(1.0 - 2.0 ** (-5.0 - h)).astype(np.float32)
    idx = np.arange(128, dtype=np.float32)
    cs = np.float32(128.0 ** -0.5)
    diff = idx[None, :] - idx[:, None]
    dm = np.where(diff[None] >= 0, np.exp(diff[None] * log_decay[:, None, None]), 0.0).astype(np.float32)
    c["dmaskT"] = np.ascontiguousarray(np.transpose(dm, (1, 0, 2)) * cs).astype(np.float32)
    qd = np.exp((idx + 1.0)[None, :] * log_decay[:, None]).astype(np.float32)
    c["qdec"] = np.ascontiguousarray(np.broadcast_to(qd[None], (128, 8, 128))).astype(np.float32)
    kd = np.exp((127.0 - idx)[:, None] * log_decay[None]).astype(np.float32) * cs
    c["kdec"] = np.ascontiguousarray(kd).astype(np.float32)
    g128 = np.exp(np.float32(128.0) * log_decay).astype(np.float32)
    gam = np.exp(log_decay).astype(np.float32)
    r = np.arange(NPREV * 128, dtype=np.float32)
    kp = (np.exp((NPREV * 128 - 1.0 - r)[:, None] * log_decay[None]) * cs).astype(np.float32)
    c["kdecp"] = np.ascontiguousarray(kp.reshape(NPREV, 128, 8).transpose(1, 0, 2))
    def bucket(dist):
        n = np.maximum(dist, 0)
        nf = np.maximum(n, 1).astype(np.float32)
        large = 16 + (np.log(nf / np.float32(16)) / np.float32(math.log(128 / 16)) * np.float32(16)).astype(np.int32)
        return np.where(n < 16, n, np.minimum(large, 31))
    oh = np.zeros((32, 384), np.float32)
    for i in range(383):
        dist = i - 127
        if 0 <= dist <= 128:
            oh[bucket(np.array(dist)), i] = 1.0
    c["oh"] = oh
    ohs = np.zeros((32, 128), np.float32)
    for j in range(128):
        ohs[bucket(np.array(128 - j)), j] = 1.0
    c["ohs"] = ohs
    return c, g128, gam, log_decay


def rope_tables(pos):
    half = 64
    inv = (np.float32(10000.0) ** (-np.arange(half, dtype=np.float32) / np.float32(half))).astype(np.float32)
    ang = pos.astype(np.float32)[:, None] * inv[None]
    cos = np.cos(ang).astype(np.float32).T
    sin = np.sin(ang).astype(np.float32).T
    return (np.ascontiguousarray(np.concatenate([cos, cos], 0)),
            np.ascontiguousarray(np.concatenate([sin, sin], 0)))


def build_nc(g128, gam):
    nc = bass.Bass("TRN2", target_bir_lowering=False)
    P = Prog(nc, same_engine_sync=SES)

    def din(name, shape, dt=F32):
        return nc.dram_tensor(name, list(shape), dt, kind="ExternalInput").ap()

    def dout(name, shape, dt=F32):
        return nc.dram_tensor(name, list(shape), dt, kind="ExternalOutput").ap()

    def MM(out, lhsT, rhs, start, stop, reads, writes):
        P.op("pe", lambda e: e.matmul(out, lhsT=lhsT, rhs=rhs, start=start, stop=stop), reads, writes)

    def TR(out, in_, ident, reads, writes):
        P.op("pe", lambda e: e.transpose(out, in_, ident), reads, writes)

    def ACT(out, in_, func, reads, writes, **kw):
        P.op("act", lambda e: e.activation(out=out, in_=in_, func=func, **kw), reads, writes)

    def DVE(name, reads, writes, **kw):
        P.op("dve", lambda e: getattr(e, name)(**kw), reads, writes)

    def DMA(q, out, in_, reads=(), writes=()):
        P.op(q, lambda e: e.dma_start(out=out, in_=in_), reads, writes, dma=True)

    evac_rr = [0]

    def COPY(out, in_, reads, writes, eng=None):
        if eng is None:
            eng = "act" if evac_rr[0] % 2 == 0 else "dve"
            evac_rr[0] += 1
        if eng == "act":
            ACT(out, in_, AF.Copy, reads, writes)
        else:
            DVE("tensor_copy", reads, writes, out=out, in_=in_)

    xm = din("xm", [1408, D])
    xprev = din("xprev", [NPREV * 128, D])
    pm = din("pm", [1042, 256])
    w_in = din("w_in", [INW // 128, 128, 32 * 128])
    w_out = din("w_out", [32, 128, 32 * 128])
    w_up = din("w_up", [2 * NFC, 128, 32 * 128])
    w_down = din("w_down", [8, 128, NFC * 512])
    w_gate = din("w_gate", [32, 128, 32 * 128])
    w_proj = din("w_proj", [32, 128, 2 * 128])
    ck = din("ck", [16, 128, 256])
    cv = din("cv", [16, 128, 256])
    sret = din("sret", [16, 8, 128, 256])
    sconv = din("sconv", [32, 2 * FF])
    cosP_d = din("cosP", [128, NPREV * 128])
    sinP_d = din("sinP", [128, NPREV * 128])
    cosG_d = [din("cosA", [128, 640]), din("cosB", [128, 528])]
    sinG_d = [din("sinA", [128, 640]), din("sinB", [128, 528])]
    escr = nc.dram_tensor("escr", [32, 384], F32, kind="Internal").ap()

    y_own = dout("y_own", [1024, D])
    y_smp = dout("y_smp", [16, D])
    o_kp = dout("o_kp", [128, 256])
    o_vp = dout("o_vp", [128, 256])
    o_rp = dout("o_rp", [8, 128, 256])
    o_cp = dout("o_cp", [2, 2 * FF])
    o_ks = dout("o_ks", [16, 128, 256])
    o_vs = dout("o_vs", [16, 128, 256])
    o_rs = dout("o_rs", [16, 8, 128, 256])
    o_cs = dout("o_cs", [16, 2, 2 * FF])
    t_outs = []

    def OUT(out, in_, reads):
        t = Tok()
        t_outs.append(t)
        DMA("sp", out, in_, reads=reads, writes=[t])

    consts = {}

    def const(name, shape, dt=F32, src=None):
        d = src if src is not None else din(name, shape)
        t = nc.alloc_sbuf_tensor("c_" + name, list(shape), dt)
        tok = Tok(name)
        DMA("pool" if dt != F32 else "sp", t[:], d, writes=[tok])
        consts[name] = (t, tok)
        return t, tok

    identf, t_identf = const("identf", [128, 128])
    identb, t_identb = const("identb", [128, 128], BF16, src=din("identf2", [128, 128]))
    jrev, t_jrev = const("jrev", [128, 128])
    bd64, t_bd64 = const("bd64", [128, 128], BF16)
    bd1, t_bd1 = const("bd1", [128, 128], BF16)
    ones4096, t_o4096 = const("ones4096", [128, 128], BF16)
    ones256, t_o256 = const("ones256", [128, 128], BF16)
    ones64, t_o64 = const("ones64", [128, 64], BF16)
    hvalid, t_hvalid = const("hvalid", [128, 64], BF16)
    g_mix, t_gmix = const("g_mix", [128, 32])
    g_ffn, t_gffn = const("g_ffn", [128, 32])
    g_ple, t_gple = const("g_ple", [128, 32])
    gq2, t_gq2 = const("gq2", [128, 1])
    gk2, t_gk2 = const("gk2", [128, 1])
    convw, t_convw = const("convw", [128, 3, 2 * NFC])
    convb, t_convb = const("convb", [128, 2 * NFC])
    dmaskT, t_dmask = const("dmaskT", [128, 8, 128])
    qdec, t_qdec = const("qdec", [128, 8, 128])
    kdec, t_kdec = const("kdec", [128, 8])
    kdecp, t_kdecp = const("kdecp", [128, NPREV, 8])
    relb, t_relb = const("relb", [32, 32])
    oh, t_oh = const("oh", [32, 384])
    ohs, t_ohs = const("ohs", [32, 128])
    sinkl, t_sinkl = const("sinkl", [128, 16])
    relb0l, t_relb0l = const("relb0l", [128, 16])

    NPS = 3
    PS = [nc.alloc_psum_tensor(f"ps{i}", [128, 1024], F32) for i in range(NPS)]
    PST = [Tok(f"ps{i}") for i in range(NPS)]
    PH = [nc.alloc_psum_tensor(f"ph{i}", [128, 512], F32) for i in range(2)]
    PHT = [Tok(f"ph{i}") for i in range(2)]
    ps_rr = [0]

    def next_ps():
        i = ps_rr[0] % NPS
        ps_rr[0] += 1
        return PS[i], PST[i]

    RA_BYTES = 50688
    RB_BYTES = 50688
    RS_BYTES = 24576
    Ra = nc.alloc_sbuf_tensor("Ra", [128, RA_BYTES], U8)
    Rb = nc.alloc_sbuf_tensor("Rb", [128, RB_BYTES], U8)
    Rs = nc.alloc_sbuf_tensor("Rs", [128, RS_BYTES], U8)

    def view(arena, off, shape, dt):
        esz = 4 if dt == F32 else 2
        n = 1
        for s in shape[1:]:
            n *= s
        v = arena[:, off:off + n * esz].bitcast(dt)
        if len(shape) == 3:
            v = v.rearrange("p (a b) -> p a b", a=shape[1])
        elif len(shape) == 4:
            v = v.rearrange("p (a b c) -> p a b c", a=shape[1], b=shape[2])
        return v

    NWB = 3
    WB = [nc.alloc_sbuf_tensor(f"wb{i}", [128, 32, 128], BF16) for i in range(NWB)]
    WBT = [Tok(f"wb{i}") for i in range(NWB)]
    wb_rr = [0]

    def next_wb():
        i = wb_rr[0] % NWB
        wb_rr[0] += 1
        return WB[i], WBT[i]

    w_in_v, w_out_v, w_up_v, w_down_v, w_gate_v, w_proj_v = w_in, w_out, w_up, w_down, w_gate, w_proj

    def load_w(wv, col0, ncols=128, nk=32, k0=0, dup64=False):
        wb, wt = next_wb()
        b, off = col0 // 128, col0 % 128
        if dup64:
            src = wv[b].rearrange("p (k c) -> p k c", c=128)[:, 0:nk, off:off + 64]
            DMA("pool", wb[:, 0:nk, 0:64], src, writes=[wt])
            DMA("pool", wb[:, 0:nk, 64:128], src, writes=[wt])
        else:
            assert off == 0 and ncols == 128 and k0 == 0
            DMA("pool", wb[:].rearrange("p k c -> p (k c)")[:, 0:nk * 128], wv[b][:, 0:nk * 128], writes=[wt])
        return wb, wt

    kT_dup = nc.alloc_sbuf_tensor("kT_dup", [128, 4, 1296], BF16)
    t_kT = [Tok(f"kT{g}") for g in range(4)]
    V_tm = nc.alloc_sbuf_tensor("V_tm", [128, 10, 256], BF16)
    t_V = Tok("V_tm")
    S = nc.alloc_sbuf_tensor("S", [128, 8, 256], F32)
    S_bf = nc.alloc_sbuf_tensor("S_bf", [128, 8, 256], BF16)
    t_S = [Tok(f"S{h}") for h in range(8)]
    t_Sbf = [Tok(f"Sbf{h}") for h in range(8)]
    cosT = nc.alloc_sbuf_tensor("cosT", [128, 640], F32)
    sinT = nc.alloc_sbuf_tensor("sinT", [128, 640], F32)
    t_cs = Tok("cossin")
    Eg = nc.alloc_sbuf_tensor("Eg", [128, 8, 2, 128], BF16)
    t_Eg = Tok("Eg")
    Es = nc.alloc_sbuf_tensor("Es", [128, 32], F32)
    t_Es = Tok("Es")
    esl = nc.alloc_sbuf_tensor("esl", [128, 16], F32)
    t_esl = Tok("esl")
    e0l = nc.alloc_sbuf_tensor("e0l", [128, 16], F32)
    t_e0l = Tok("e0l")
    expb = nc.alloc_sbuf_tensor("expb", [32, 32], F32)
    t_expb = Tok("expb")
    carry = nc.alloc_sbuf_tensor("carry", [128, 2 * NFC, 2], F32)
    t_carry = Tok("carry")
    t_kp = Tok("kp_tm")
    t_ksn = Tok("ksn")
    t_vp = Tok("vp_tm")
    t_vsn = Tok("vsn")

    ACT(expb[:], relb[:], AF.Exp, [t_relb], [t_expb])
    ACT(esl[:], sinkl[:], AF.Exp, [t_sinkl], [t_esl])
    ACT(e0l[:], relb0l[:], AF.Exp, [t_relb0l], [t_e0l])
    ps, pt = next_ps()
    MM(ps[0:32, 0:384], expb[:], oh[:], True, True, [t_expb, t_oh], [pt])
    u_sb0 = view(Rs, 0, [128, 384], F32)
    t_u0 = Tok("u0")
    COPY(u_sb0[0:32, :], ps[0:32, 0:384], [pt], [t_u0], eng="dve")
    t_escr = Tok("escr")
    DMA("sp", escr, u_sb0[0:32, :], reads=[t_u0], writes=[t_escr])
    ps, pt = next_ps()
    MM(ps[:, 0:32], ohs[:], expb[:], True, True, [t_ohs, t_expb], [pt])
    COPY(Es[:], ps[:, 0:32], [pt], [t_Es], eng="dve")

    def RSTD(out, in_, reads, tok):
        ACT(out, in_, AF.Sqrt, reads, [tok], bias=EPS, scale=1.0)
        DVE("reciprocal", [tok], [tok], out=out, in_=out)

    def load_xT(src, n, x_tm, t_xtm, dstT, t_dst):
        DMA("sp", x_tm[0:n, :], src, writes=[t_xtm])
        for c4 in range(8):
            ps, pt = next_ps()
            for i in range(4):
                k = c4 * 4 + i
                TR(ps[:, i * 128:i * 128 + n], x_tm[0:n, k * 128:(k + 1) * 128], identf[0:n, 0:n],
                   [t_xtm, t_identf], [pt])
            COPY(dstT[:, c4 * 4:(c4 + 1) * 4, 0:n],
                 ps[:, 0:512].rearrange("p (c t) -> p c t", c=4)[:, :, 0:n], [pt], [t_dst])

    def rmsnorm_fm(xT_of, t_x, n, gvec, t_g, out_of, t_out, sq, t_sq, rstd, t_rstd):
        for c0 in range(0, n, 128):
            m = min(128, n - c0)
            for k in range(32):
                ACT(sq[:, k, 0:m], xT_of(k)[:, c0:c0 + m], AF.Square, [t_x], [t_sq])
            ps, pt = next_ps()
            for k in range(32):
                MM(ps[:, 0:m], ones4096[:], sq[:, k, 0:m], k == 0, k == 31, [t_sq, t_o4096], [pt])
            RSTD(rstd[:, c0:c0 + m], ps[:, 0:m], [pt], t_rstd)
        for k in range(32):
            DVE("scalar_tensor_tensor", [t_x, t_rstd, t_g], [t_out], out=out_of(k), in0=xT_of(k)[:, 0:n],
                scalar=gvec[:, k:k + 1], in1=rstd[:, 0:n], op0=ALU.mult, op1=ALU.mult)

    def proj_fm(wb, wt, nk, rhs_of, t_rhs, lo, hi):
        ps, pt = next_ps()
        pcs = pieces(hi - lo)
        for k in range(nk):
            for pi, (a, b) in enumerate(pcs):
                MM(ps[:, pi * 512:pi * 512 + (b - a)], wb[:, k, :], rhs_of(k)[:, lo + a:lo + b],
                   k == 0, k == nk - 1, [wt, t_rhs], [pt])
        return ps, pt, pcs

    def rotary(ps, pt, pcs, cos, sin, t_tab, tc0, out, t_out, X, B, t_X, t_B):
        n = pcs[-1][1]
        for pi, (a, b) in enumerate(pcs):
            ACT(X[:, a:b], ps[:, pi * 512:pi * 512 + (b - a)], AF.Copy, [pt], [t_X])
        DVE("tensor_tensor", [t_X, t_tab], [t_B], out=B[0:64, 0:n], in0=X[64:128, 0:n],
            in1=sin[64:128, tc0:tc0 + n], op=ALU.mult)
        DVE("tensor_tensor", [t_X, t_tab], [t_B], out=B[64:128, 0:n], in0=X[0:64, 0:n],
            in1=sin[0:64, tc0:tc0 + n], op=ALU.mult)
        DVE("tensor_tensor", [t_X, t_tab], [t_X], out=X[:, 0:n], in0=X[:, 0:n],
            in1=cos[:, tc0:tc0 + n], op=ALU.mult)
        DVE("tensor_tensor", [t_X, t_B], [t_out], out=out[0:64, 0:n], in0=X[0:64, 0:n],
            in1=B[0:64, 0:n], op=ALU.subtract)
        DVE("tensor_tensor", [t_X, t_B], [t_out], out=out[64:128, 0:n], in0=X[64:128, 0:n],
            in1=B[64:128, 0:n], op=ALU.add)

    P.barrier()
    hTp = view(Rb, 0, [128, 32, 768], BF16)
    t_hTp = Tok("hTp")
    x_tm = view(Ra, 0, [128, 4096], F32)
    t_xtm = Tok("x_tm")
    xT_tmp = view(Ra, 16384, [128, 32, 128], F32)
    t_xTt = Tok("xT_tmp")
    cosP = view(Ra, 32768, [128, 768], F32)
    sinP = view(Ra, 32768 + 3072, [128, 768], F32)
    t_csP = Tok("csP")
    Xr = view(Ra, 32768 + 6144, [128, 768], F32)
    Br = view(Ra, 32768 + 9216, [128, 768], F32)
    t_Xr, t_Br = Tok("Xr"), Tok("Br")
    sq = view(Rs, 0, [128, 32, 128], BF16)
    t_sq = Tok("sq")
    rstd = view(Rs, 8192, [128, 528], F32)
    t_rstd = Tok("rstd")
    rkT = view(Rs, 10304, [128, 768], BF16)
    rvT = view(Rs, 11840, [128, 2, 768], BF16)
    t_rkT, t_rvT = Tok("rkT"), Tok("rvT")
    ktm = [view(Rs, 14912 + i * 256, [128, 128], BF16) for i in range(2)]
    vtm = [view(Rs, 15424 + i * 512, [128, 256], BF16) for i in range(2)]
    t_ktm = [Tok(), Tok()]
    t_vtm = [Tok(), Tok()]
    Sps_rr = [0]

    tile0 = 0
    for pi_, ntile in enumerate(PREV_PASS):
        ncolp = ntile * 128
        DMA("sp", cosP[:, 0:ncolp], cosP_d[:, tile0 * 128:tile0 * 128 + ncolp], writes=[t_csP])
        DMA("sp", sinP[:, 0:ncolp], sinP_d[:, tile0 * 128:tile0 * 128 + ncolp], writes=[t_csP])
        for ti in range(ntile):
            r0 = (tile0 + ti) * 128
            load_xT(xprev[r0:r0 + 128, :], 128, x_tm, t_xtm, xT_tmp, t_xTt)
            rmsnorm_fm(lambda k: xT_tmp[:, k, :], t_xTt, 128, g_mix, t_gmix,
                       lambda k, ti=ti: hTp[:, k, ti * 128:(ti + 1) * 128], t_hTp, sq, t_sq, rstd, t_rstd)
        for h in range(8):
            wb, wt = load_w(w_in_v, RK0 + h * 128)
            ps, pt, pcs = proj_fm(wb, wt, 32, lambda k: hTp[:, k, :], t_hTp, 0, ncolp)
            rotary(ps, pt, pcs, cosP, sinP, t_csP, 0, rkT, t_rkT, Xr, Br, t_Xr, t_Br)
            for dvc in range(2):
                wb, wt = load_w(w_in_v, RV0 + h * 256 + dvc * 128)
                ps, pt, pcs = proj_fm(wb, wt, 32, lambda k: hTp[:, k, :], t_hTp, 0, ncolp)
                for pj, (a, b) in enumerate(pcs):
                    COPY(rvT[:, dvc, a:b], ps[:, pj * 512:pj * 512 + (b - a)], [pt], [t_rvT])
            psS, ptS = PH[h % 2], PHT[h % 2]
            for ti in range(ntile):
                bi = ti % 2
                ps, pt = next_ps()
                psb = ps[:, 0:512].bitcast(BF16)
                TR(psb[:, 0:128], rkT[:, ti * 128:(ti + 1) * 128], identb[:], [t_rkT, t_identb], [pt])
                ACT(ktm[bi][:], psb[:, 0:128], AF.Copy, [pt, t_kdecp], [t_ktm[bi]],
                    scale=kdecp[:, tile0 + ti, h:h + 1])
                for dvc in range(2):
                    TR(psb[:, 128 + dvc * 128:256 + dvc * 128], rvT[:, dvc, ti * 128:(ti + 1) * 128], identb[:],
                       [t_rvT, t_identb], [pt])
                COPY(vtm[bi][:], psb[:, 128:384], [pt], [t_vtm[bi]], eng="act")
                MM(psS[:, 0:256], ktm[bi][:], vtm[bi][:], ti == 0, ti == ntile - 1,
                   [t_ktm[bi], t_vtm[bi]], [ptS])
            if pi_ == 0:
                COPY(S[:, h, :], psS[:, 0:256], [ptS], [t_S[h]], eng="dve")
            else:
                DVE("tensor_tensor", [ptS, t_S[h]], [t_S[h]], out=S[:, h, :], in0=S[:, h, :],
                    in1=psS[:, 0:256], op=ALU.add)
        tile0 += ntile
    for h in range(8):
        ACT(S_bf[:, h, :], S[:, h, :], AF.Copy, [t_S[h]], [t_Sbf[h]])

    if STAGE <= 2:
        OUT(o_rp.rearrange("h k v -> k h v"), S[:], t_S)
        P.op("sp", lambda e: e.nop(), reads=t_outs)
        P.emit()
        return nc, P


    t_qns = Tok("qn_s")
    t_vnT = Tok("vnT")
    ucp = [nc.alloc_sbuf_tensor(f"ucp{i}", [18, 512], F32) for i in range(2)]
    t_ucp = [Tok("ucp0"), Tok("ucp1")]
    dbg_mix = [dout(f"dbg_mix{gi}", [128, 32, 528], BF16) for gi in range(2)] if os.environ.get("MK_DBG") else None

    def xT_of(k):
        if k < 24:
            return view(Rb, k * 2112, [128, 528], F32)
        return view(Ra, 33792 + (k - 24) * 2112, [128, 528], F32)
    t_xT = Tok("xT")

    def chk(code, cond=True):
        if cond and STAGE <= code:
            raise EarlyExit()

    def _groups():
      for gi, G in enumerate(GROUPS):
          ncol, q0, f0, nsmp, row0 = G["ncol"], G["q0"], G["f0"], G["nsmp"], G["row0"]
          nq = ncol - q0
          nf = ncol - f0
          gcol0 = row0
          blocks = G["blocks"]
          P.barrier()
          hT = view(Rb, 0, [128, 32, 768], BF16)
          t_hT = Tok("hT")
          x_tm2 = [view(Ra, 0, [128, 4096], F32), view(Ra, 16384, [128, 4096], F32)]
          t_xtm2 = [Tok(), Tok()]
          xT_tmp = view(Ra, 32768, [128, 32, 128], F32)
          t_xTt = Tok()
          sq = view(Rs, 0, [128, 32, 128], BF16)
          t_sq = Tok()
          rstd = view(Rs, 8192, [128, 528], F32)
          t_rstd = Tok()
          DMA("sp", cosT[:, 0:nq], cosG_d[gi], writes=[t_cs])
          DMA("sp", sinT[:, 0:nq], sinG_d[gi], writes=[t_cs])
          ntile = (ncol + 127) // 128
          if os.environ.get('MK_SKIP16'):
              ntile = ncol // 128
          for ti in range(ntile):
              n = 128
              load_xT(xm[row0 + ti * 128:row0 + ti * 128 + n, :], n, x_tm2[ti % 2], t_xtm2[ti % 2], xT_tmp, t_xTt)
              rmsnorm_fm(lambda k: xT_tmp[:, k, :], t_xTt, n, g_mix, t_gmix,
                         lambda k, ti=ti, n=n: hT[:, k, ti * 128:ti * 128 + n], t_hT, sq, t_sq, rstd, t_rstd)
          if STAGE <= 6.5 and gi == 1:
              raise EarlyExit()
          P.barrier()
          mixT = view(Ra, 0, [128, 32, 528], BF16)
          t_mix = Tok("mixT")
          kp_tm = view(Ra, 33792, [128, 256], F32)
          ksn_tm = view(Ra, 34816, [128, 256], F32)
          vp_tm = view(Ra, 35840, [128, 256], F32)
          vsn_tm = view(Ra, 36864, [128, 256], F32)
          qn_s = view(Ra, 37888, [128, 16, 16], BF16)
          vnT = view(Ra, 38400, [128, 4, 16], F32)
          hT_of = lambda k: hT[:, k, :]

          tm_blocks = [(bi * 128, 128, gb) for bi, gb in enumerate(blocks)]
          if nsmp and "tmv16" not in SKIP:
              tm_blocks.append((512, 16, None))
          for half in range(2):
              wb, wt = load_w(w_in_v, AV0 + half * 128)
              for (c0, n, gb) in tm_blocks:
                  ps, pt = next_ps()
                  for k in range(32):
                      MM(ps[0:n, 0:128], hT[:, k, c0:c0 + n], wb[:, k, :], k == 0, k == 31, [t_hT, wt], [pt])
                  if gb is not None:
                      ce = "act" if gb % 2 == 0 else "dve"
                      COPY(V_tm[0:n, gb, half * 128:(half + 1) * 128], ps[0:n, 0:128], [pt], [t_V], eng=ce)
                      if gb == 9:
                          COPY(vp_tm[:, half * 128:(half + 1) * 128], ps[:, 0:128], [pt], [t_vp], eng=ce)
                  else:
                      COPY(vsn_tm[0:16, half * 128:(half + 1) * 128], ps[0:16, 0:128], [pt], [t_vsn])

          chk(6.51, gi == 1)
          qn = view(Rs, 0, [128, 4, 640], BF16)
          t_qn = Tok("qn")
          sqn = view(Rs, 5120, [128, 768], BF16)
          t_sqn = Tok()
          rst2 = view(Rs, 6656, [128, 768], F32)
          t_rst2 = Tok()
          ex = [view(Rs, 9728 + i * 2048, [128, 512], F32) for i in range(2)]
          t_ex = [Tok(), Tok()]
          PT = view(Rs, 13824, [128, 2, 2, 512], BF16)
          t_PT = Tok("PT")
          rden = view(Rs, 17920, [128, 512], F32)
          t_rden = Tok()
          Hk = [view(Rs, 19968 + i * 1024, [128, 256], F32) for i in range(2)]
          t_Hk = [Tok(), Tok()]
          knf = view(Rs, 22016, [128, 144], F32)
          t_knf = Tok()

          def norm_qk(ps, pt, pcs, gvec, t_gv, out, t_out, extra=None):
              for pi, (a, b) in enumerate(pcs):
                  ACT(sqn[:, a:b], ps[:, pi * 512:pi * 512 + (b - a)], AF.Square, [pt], [t_sqn])
              ps2, pt2 = next_ps()
              for pi, (a, b) in enumerate(pcs):
                  MM(ps2[:, pi * 512:pi * 512 + (b - a)], bd64[:], sqn[:, a:b], True, True, [t_sqn, t_bd64], [pt2])
              for pi, (a, b) in enumerate(pcs):
                  RSTD(rst2[:, a:b], ps2[:, pi * 512:pi * 512 + (b - a)], [pt2], t_rst2)
              for pi, (a, b) in enumerate(pcs):
                  DVE("scalar_tensor_tensor", [pt, t_rst2, t_gv], [t_out], out=out[:, a:b],
                      in0=ps[:, pi * 512:pi * 512 + (b - a)], scalar=gvec[:, 0:1], in1=rst2[:, a:b],
                      op0=ALU.mult, op1=ALU.mult)
                  if extra is not None:
                      for (ea, eb, dst, t_dst) in extra:
                          lo, hi = max(a, ea), min(b, eb)
                          if lo < hi:
                              DVE("scalar_tensor_tensor", [pt, t_rst2, t_gv], [t_dst], out=dst[:, lo - ea:hi - ea],
                                  in0=ps[:, pi * 512 + lo - a:pi * 512 + hi - a], scalar=gvec[:, 0:1],
                                  in1=rst2[:, lo:hi], op0=ALU.mult, op1=ALU.mult)

          for g in range(4):
              wb, wt = load_w(w_in_v, AK0 + g * 64, dup64=True)
              ps, pt, pcs = proj_fm(wb, wt, 32, hT_of, t_hT, 0, ncol)
              extra = None
              if gi == 1:
                  extra = [(384, 528, knf, t_knf)]
              norm_qk(ps, pt, pcs, gk2, t_gk2, kT_dup[:, g, gcol0:gcol0 + ncol], t_kT[g], extra)
              if gi == 1 and "knf" not in SKIP:
                  ps, pt = next_ps()
                  TR(ps[:, 0:64], knf[0:64, 0:128], identf[0:64, 0:64], [t_knf, t_identf], [pt])
                  TR(ps[0:16, 64:128], knf[0:64, 128:144], identf[0:64, 0:64], [t_knf, t_identf], [pt])
                  COPY(kp_tm[:, g * 64:(g + 1) * 64], ps[:, 0:64], [pt], [t_kp], eng="act")
                  COPY(ksn_tm[0:16, g * 64:(g + 1) * 64], ps[0:16, 64:128], [pt], [t_ksn], eng="act")
              chk(6.52, gi == 1)
              for j in range(4):
                  wb, wt = load_w(w_in_v, AQ0 + (4 * g + j) * 128)
                  ps, pt, pcs = proj_fm(wb, wt, 32, hT_of, t_hT, q0, ncol)
                  norm_qk(ps, pt, pcs, gq2, t_gq2, qn[:, j, 0:nq], t_qn)
                  if nsmp and "qns" not in SKIP:
                      DVE("tensor_copy", [t_qn], [t_qns], out=qn_s[:, 4 * g + j, :], in_=qn[:, j, 512:528])
              if nsmp and "vnT" not in SKIP:
                  wb, wt = load_w(w_in_v, AV0 + g * 64, dup64=True)
                  ps, pt = next_ps()
                  for k in range(32):
                      MM(ps[:, 0:16], wb[:, k, :], hT[:, k, 512:528], k == 0, k == 31, [wt, t_hT], [pt])
                  COPY(vnT[:, g, :], ps[:, 0:16], [pt], [t_vnT])
              chk(6.53, gi == 1)
              for hh in range(8):
                  head = 8 * g + hh
                  hk, thk = Hk[hh % 2], t_Hk[hh % 2]
                  src = bass.AP(escr.tensor, head * 384, [[1, 128], [128, 2], [1, 128]])
                  DMA("sp", hk[:].rearrange("p (a b) -> p a b", a=2), src, reads=[t_escr], writes=[thk])
                  ps, pt = next_ps()
                  MM(ps[:, 0:256], jrev[:], hk[:], True, True, [thk, t_jrev], [pt])
                  ce = "act" if hh % 2 == 0 else "dve"
                  COPY(Eg[:, hh, 1, :], ps[:, 0:128], [pt], [t_Eg], eng=ce)
                  COPY(Eg[:, hh, 0, :], ps[:, 128:256], [pt], [t_Eg], eng=ce)
              chk(6.54, gi == 1)
              for bi, gb in enumerate(blocks):
                  c0 = bi * 128
                  if c0 < q0:
                      continue
                  qi0 = c0 - q0
                  kcols = (gcol0 + c0 - 128, gcol0 + c0)
                  ei = 0
                  for kbi in range(2):
                      for half in range(2):
                          ps, pt = next_ps()
                          MM(ps[:, 0:512].rearrange("p (j q) -> p j q", j=4),
                             kT_dup[half * 64:(half + 1) * 64, g, kcols[kbi]:kcols[kbi] + 128],
                             qn[half * 64:(half + 1) * 64, :, qi0:qi0 + 128], True, True, [t_kT[g], t_qn], [pt])
                          e_, te_ = ex[ei % 2], t_ex[ei % 2]
                          ei += 1
                          ACT(e_[:], ps[:, 0:512], AF.Exp, [pt], [te_], scale=0.125)
                          DVE("tensor_tensor", [te_, t_Eg], [t_PT],
                              out=PT[:, kbi, half, :].rearrange("p (j q) -> p j q", j=4),
                              in0=e_[:].rearrange("p (j q) -> p j q", j=4),
                              in1=Eg[:, half:8:2, kbi, :], op=ALU.mult)
                  ps_o, pt_o = next_ps()
                  for half in range(2):
                      for kbi in range(2):
                          vgb = gb - 1 + kbi
                          vo, tvo = (hvalid, t_hvalid) if vgb in (0, 1) else (ones64, t_o64)
                          MM(ps_o[half * 64:(half + 1) * 64, 0:512], V_tm[:, vgb, g * 64:(g + 1) * 64],
                             PT[:, kbi, half, :], kbi == 0, kbi == 1, [t_V, t_PT], [pt_o])
                      for kbi in range(2):
                          vgb = gb - 1 + kbi
                          vo, tvo = (hvalid, t_hvalid) if vgb in (0, 1) else (ones64, t_o64)
                          MM(ps_o[half * 64:(half + 1) * 64, 512:1024], vo[:],
                             PT[:, kbi, half, :], kbi == 0, kbi == 1, [tvo, t_PT], [pt_o])
                  for j in range(4):
                      DVE("tensor_scalar", [pt_o, t_esl], [t_rden], out=rden[:, j * 128:(j + 1) * 128],
                          in0=ps_o[:, 512 + j * 128:512 + (j + 1) * 128], scalar1=esl[:, 4 * g + j:4 * g + j + 1],
                          scalar2=None, op0=ALU.add)
                  DVE("reciprocal", [t_rden], [t_rden], out=rden[:], in_=rden[:])
                  q_lo = 126 if gb == 1 else 0
                  mc0 = c0 + q_lo - f0
                  DVE("tensor_tensor", [pt_o, t_rden], [t_mix],
                      out=mixT[:, 4 * g:4 * g + 4, mc0:mc0 + 128 - q_lo],
                      in0=ps_o[:, 0:512].rearrange("p (j q) -> p j q", j=4)[:, :, q_lo:128],
                      in1=rden[:].rearrange("p (j q) -> p j q", j=4)[:, :, q_lo:128], op=ALU.mult)

          if STAGE <= 6.6 and gi == 1:
              raise EarlyExit()
          if nsmp:
              P.barrier()
              kdup_s = [view(Rs, 0 + i * 1024, [128, 4, 2, 64], BF16) for i in range(2)]
              vc_s = [view(Rs, 2048 + i * 512, [128, 256], BF16) for i in range(2)]
              t_kds = [Tok(), Tok()]
              t_vcs = [Tok(), Tok()]
              KTs = view(Rs, 3072, [128, 4, 128], BF16)
              t_KTs = Tok()
              ex_s = view(Rs, 4096, [128, 32], F32)
              t_exs = Tok()
              PTs = view(Rs, 4224, [128, 32], BF16)
              t_PTs = Tok()
              prod = view(Rs, 4352, [128, 16, 16], BF16)
              t_prod = Tok()
              pnew = view(Rs, 4864, [128, 16, 16], F32)
              t_pnew = Tok()
              vn16 = view(Rs, 5888, [128, 16, 16], F32)
              t_vn16 = Tok()
              sm_a = view(Rs, 6912, [128, 16], F32)
              sm_b = view(Rs, 6976, [128, 16], F32)
              t_sma, t_smb = Tok(), Tok()
              for c in range(16):
                  g = c // 4
                  DVE("tensor_tensor", [t_qns, t_kT[g]], [t_prod], out=prod[:, c, :], in0=qn_s[:, c, :],
                      in1=kT_dup[:, g, gcol0 + 512:gcol0 + 528], op=ALU.mult)
                  DVE("tensor_copy", [t_vnT], [t_vn16], out=vn16[:, c, :], in_=vnT[:, g, :])
              ps, pt = next_ps()
              MM(ps[:, 0:256], bd1[:], prod[:].rearrange("p a b -> p (a b)"), True, True, [t_prod, t_bd1], [pt])
              ACT(pnew[:].rearrange("p a b -> p (a b)"), ps[:, 0:256], AF.Exp, [pt], [t_pnew], scale=0.125)
              DVE("tensor_tensor", [t_pnew, t_e0l], [t_pnew], out=pnew[:], in0=pnew[:],
                  in1=e0l[:].unsqueeze(2).to_broadcast([128, 16, 16]), op=ALU.mult)
              DVE("tensor_tensor", [t_pnew, t_vn16], [t_vn16], out=vn16[:], in0=vn16[:], in1=pnew[:], op=ALU.mult)
              for s_ in range(16):
                  bi_ = s_ % 2
                  DMA("pool", kdup_s[bi_][:, :, 0, :], ck[s_].rearrange("k (g d) -> k g d", g=4), writes=[t_kds[bi_]])
                  DMA("pool", kdup_s[bi_][:, :, 1, :], ck[s_].rearrange("k (g d) -> k g d", g=4), writes=[t_kds[bi_]])
                  DMA("pool", vc_s[bi_][:], cv[s_], writes=[t_vcs[bi_]])
                  ps, pt = next_ps()
                  psb = ps[:, 0:512].bitcast(BF16)
                  for g in range(4):
                      TR(psb[:, g * 128:(g + 1) * 128], kdup_s[bi_][:, g, :, :].rearrange("p a b -> p (a b)"),
                         identb[:], [t_kds[bi_], t_identb], [pt])
                  COPY(KTs[:].rearrange("p a b -> p (a b)"), psb[:, 0:512], [pt], [t_KTs])
                  ps_sc, pt_sc = next_ps()
                  for half in range(2):
                      for g in range(4):
                          MM(ps_sc[:, half * 512 + 4 * g:half * 512 + 4 * g + 4], KTs[half * 64:(half + 1) * 64, g, :],
                             qn_s[half * 64:(half + 1) * 64, 4 * g:4 * g + 4, s_], True, True, [t_KTs, t_qns], [pt_sc])
                  for half in range(2):
                      ACT(ex_s[:, half * 16:(half + 1) * 16], ps_sc[:, half * 512:half * 512 + 16], AF.Exp,
                          [pt_sc], [t_exs], scale=0.125)
                  DVE("tensor_tensor", [t_exs, t_Es], [t_PTs], out=PTs[:].rearrange("p (h c) -> p h c", h=2),
                      in0=ex_s[:].rearrange("p (h c) -> p h c", h=2),
                      in1=Es[:].rearrange("p (c h) -> p h c", h=2), op=ALU.mult)
                  ps_os, pt_os = next_ps()
                  for g in range(4):
                      for half in range(2):
                          MM(ps_os[half * 64:(half + 1) * 64, 4 * g:4 * g + 4], vc_s[bi_][:, g * 64:(g + 1) * 64],
                             PTs[:, half * 16 + 4 * g:half * 16 + 4 * g + 4], True, True, [t_vcs[bi_], t_PTs], [pt_os])
                          MM(ps_os[half * 64:(half + 1) * 64, 16 + 4 * g:16 + 4 * g + 4], ones64[:],
                             PTs[:, half * 16 + 4 * g:half * 16 + 4 * g + 4], True, True, [t_o64, t_PTs], [pt_os])
                  DVE("tensor_tensor", [pt_os, t_vn16], [t_sma], out=sm_a[:], in0=ps_os[:, 0:16], in1=vn16[:, :, s_], op=ALU.add)
                  DVE("tensor_tensor", [pt_os, t_pnew], [t_smb], out=sm_b[:], in0=ps_os[:, 16:32], in1=pnew[:, :, s_], op=ALU.add)
                  DVE("tensor_tensor", [t_smb, t_esl], [t_smb], out=sm_b[:], in0=sm_b[:], in1=esl[:], op=ALU.add)
                  DVE("reciprocal", [t_smb], [t_smb], out=sm_b[:], in_=sm_b[:])
                  DVE("tensor_tensor", [t_sma, t_smb], [t_mix], out=mixT[:, 0:16, 512 + s_], in0=sm_a[:], in1=sm_b[:], op=ALU.mult)
              OUT(o_ks[:, 0:127, :], ck[:, 1:128, :], [])
              OUT(o_vs[:, 0:127, :], cv[:, 1:128, :], [])
              OUT(o_ks[:, 127, :], ksn_tm[0:16, :], [t_ksn])
              OUT(o_vs[:, 127, :], vsn_tm[0:16, :], [t_vsn])
              OUT(o_kp, kp_tm[:], [t_kp])
              OUT(o_vp, vp_tm[:], [t_vp])
          P.barrier()

          if STAGE <= 6.7 and gi == 1:
              raise EarlyExit()
          rq = view(Rs, 0, [128, 640], BF16)
          rk = view(Rs, 1280, [128, 640], BF16)
          t_rq, t_rk = Tok(), Tok()
          rv_tm = view(Rs, 2560, [128, 6, 256], BF16)
          t_rv = Tok()
          sg = view(Rs, 5632, [128, 2, 640], BF16)
          t_sg = Tok()
          Xr = view(Rs, 8192, [128, 640], F32)
          Br = view(Rs, 10752, [128, 640], F32)
          t_Xr, t_Br = Tok(), Tok()
          AT = [view(Rs, 13312 + i * 256, [128, 128], BF16) for i in range(2)]
          qd = [view(Rs, 13824 + i * 256, [128, 128], BF16) for i in range(2)]
          ktm = [view(Rs, 14336 + i * 256, [128, 128], BF16) for i in range(2)]
          t_AT, t_qd, t_ktm = [Tok(), Tok()], [Tok(), Tok()], [Tok(), Tok()]
          sqo = view(Rs, 14848, [128, 256], BF16)
          t_sqo = Tok()
          rstd_o = view(Rs, 15360, [128, 128], F32)
          t_rso = Tok()
          to_ = view(Rs, 15872, [128, 256], F32)
          t_to = Tok()
          s0b = [view(Rs, 16896 + i * 1024, [128, 256], F32) for i in range(2)]
          t_s0 = [Tok(), Tok()]
          s1bf = [view(Rs, 18944 + i * 512, [128, 256], BF16) for i in range(2)]
          t_s1bf = [Tok(), Tok()]
          kmask = view(Rs, 19968, [128, 16, 128], BF16)
          t_kmask = Tok()
          ktm_s = view(Rs, 24064, [128, 128], BF16)
          t_ktms = Tok()

          qblocks = [(bi * 128, 128, gb) for bi, gb in enumerate(blocks) if bi * 128 >= q0]
          rblocks = list(qblocks)
          if nsmp:
              rblocks.append((512, 16, None))
          for h in range(8):
              wb, wt = load_w(w_in_v, RQ0 + h * 128)
              ps, pt, pcs = proj_fm(wb, wt, 32, hT_of, t_hT, q0, ncol)
              rotary(ps, pt, pcs, cosT, sinT, t_cs, 0, rq, t_rq, Xr, Br, t_Xr, t_Br)
              wb, wt = load_w(w_in_v, RK0 + h * 128)
              ps, pt, pcs = proj_fm(wb, wt, 32, hT_of, t_hT, q0, ncol)
              rotary(ps, pt, pcs, cosT, sinT, t_cs, 0, rk, t_rk, Xr, Br, t_Xr, t_Br)
              for dvc in range(2):
                  wb, wt = load_w(w_in_v, RV0 + h * 256 + dvc * 128)
                  for ri, (c0, n, gb) in enumerate(rblocks):
                      ps, pt = next_ps()
                      for k in range(32):
                          MM(ps[0:n, 0:128], hT[:, k, c0:c0 + n], wb[:, k, :], k == 0, k == 31, [t_hT, wt], [pt])
                      COPY(rv_tm[0:n, ri, dvc * 128:(dvc + 1) * 128], ps[0:n, 0:128], [pt], [t_rv])
              for dvc in range(2):
                  wb, wt = load_w(w_in_v, RG0 + h * 256 + dvc * 128)
                  ps, pt, pcs = proj_fm(wb, wt, 32, hT_of, t_hT, q0, ncol)
                  for pi, (a, b) in enumerate(pcs):
                      ACT(sg[:, dvc, a:b], ps[:, pi * 512:pi * 512 + (b - a)], AF.Silu, [pt], [t_sg])
              for ri, (c0, n, gb) in enumerate(qblocks):
                  qi0 = c0 - q0
                  bi_ = ri % 2
                  ps1, pt1 = next_ps()
                  MM(ps1[:, 0:128], rk[:, qi0:qi0 + 128], rq[:, qi0:qi0 + 128], True, True, [t_rk, t_rq], [pt1])
                  DVE("tensor_tensor", [pt1, t_dmask], [t_AT[bi_]], out=AT[bi_][:], in0=ps1[:, 0:128],
                      in1=dmaskT[:, h, :], op=ALU.mult)
                  DVE("tensor_tensor", [t_rq, t_qdec], [t_qd[bi_]], out=qd[bi_][:], in0=rq[:, qi0:qi0 + 128],
                      in1=qdec[:, h, :], op=ALU.mult)
                  psb = ps1[:, 512:1024].bitcast(BF16)
                  TR(psb[:, 0:128], rk[:, qi0:qi0 + 128], identb[:], [t_rk, t_identb], [pt1])
                  ACT(ktm[bi_][:], psb[:, 0:128], AF.Copy, [pt1, t_kdec], [t_ktm[bi_]], scale=kdec[:, h:h + 1])
                  ps_o, pt_o = next_ps()
                  for dvc in range(2):
                      MM(ps_o[:, dvc * 128:(dvc + 1) * 128], rv_tm[:, ri, dvc * 128:(dvc + 1) * 128], AT[bi_][:],
                         True, False, [t_rv, t_AT[bi_]], [pt_o])
                      MM(ps_o[:, dvc * 128:(dvc + 1) * 128], S_bf[:, h, dvc * 128:(dvc + 1) * 128], qd[bi_][:],
                         False, True, [t_Sbf[h], t_qd[bi_]], [pt_o])
                  ACT(sqo[:], ps_o[:, 0:256], AF.Square, [pt_o], [t_sqo])
                  MM(ps_o[:, 512:640], ones256[:], sqo[:, 0:128], True, False, [t_sqo, t_o256], [pt_o])
                  MM(ps_o[:, 512:640], ones256[:], sqo[:, 128:256], False, True, [t_sqo, t_o256], [pt_o])
                  RSTD(rstd_o[:], ps_o[:, 512:640], [pt_o], t_rso)
                  for dvc in range(2):
                      DVE("tensor_tensor", [pt_o, t_rso], [t_to], out=to_[:, dvc * 128:(dvc + 1) * 128],
                          in0=ps_o[:, dvc * 128:(dvc + 1) * 128], in1=rstd_o[:], op=ALU.mult)
                  q_lo = 126 if gb == 1 else 0
                  mc0 = c0 + q_lo - f0
                  DVE("tensor_tensor", [t_to, t_sg], [t_mix],
                      out=mixT[:, 16 + 2 * h:16 + 2 * h + 2, mc0:mc0 + 128 - q_lo],
                      in0=to_[:].rearrange("p (a b) -> p a b", a=2)[:, :, q_lo:128],
                      in1=sg[:, :, qi0 + q_lo:qi0 + 128], op=ALU.mult)
                  psS, ptS = PH[ri % 2], PHT[ri % 2]
                  MM(psS[:, 0:256], ktm[bi_][:], rv_tm[:, ri, :], True, True, [t_ktm[bi_], t_rv], [ptS])
                  DVE("scalar_tensor_tensor", [ptS, t_S[h]], [t_S[h]], out=S[:, h, :], in0=S[:, h, :],
                      scalar=float(g128[h]), in1=psS[:, 0:256], op0=ALU.mult, op1=ALU.add)
                  ACT(S_bf[:, h, :], S[:, h, :], AF.Copy, [t_S[h]], [t_Sbf[h]])
              if nsmp:
                  ri = len(qblocks)
                  qs0 = 512 - q0
                  cs_ = float(128.0 ** -0.5)
                  ps1, pt1 = next_ps()
                  psb = ps1[:, 0:512].bitcast(BF16)
                  TR(psb[0:16, 0:128], rk[:, qs0:qs0 + 16], identb[:], [t_rk, t_identb], [pt1])
                  ACT(ktm_s[0:16, :], psb[0:16, 0:128], AF.Copy, [pt1], [t_ktms], scale=cs_)
                  for s_ in range(16):
                      DVE("tensor_scalar", [t_ktms, t_identf], [t_kmask], out=kmask[0:16, s_, :], in0=ktm_s[0:16, :],
                          scalar1=identf[0:16, s_:s_ + 1], scalar2=None, op0=ALU.mult)
                  ps_os, pt_os = PH[0], PHT[0]
                  for s_ in range(16):
                      bi_ = s_ % 2
                      DMA("sp", s0b[bi_][:], sret[s_, h], writes=[t_s0[bi_]])
                      psS, ptS = next_ps()
                      MM(psS[:, 0:256], kmask[0:16, s_, :], rv_tm[0:16, ri, :], True, True, [t_kmask, t_rv], [ptS])
                      DVE("scalar_tensor_tensor", [ptS, t_s0[bi_]], [t_s0[bi_]], out=s0b[bi_][:], in0=s0b[bi_][:],
                          scalar=float(gam[h]), in1=psS[:, 0:256], op0=ALU.mult, op1=ALU.add)
                      OUT(o_rs[s_, h], s0b[bi_][:], [t_s0[bi_]])
                      ACT(s1bf[bi_][:], s0b[bi_][:], AF.Copy, [t_s0[bi_]], [t_s1bf[bi_]])
                      for dvc in range(2):
                          MM(ps_os[:, dvc * 16 + s_:dvc * 16 + s_ + 1], s1bf[bi_][:, dvc * 128:(dvc + 1) * 128],
                             rq[:, qs0 + s_:qs0 + s_ + 1], True, True, [t_s1bf[bi_], t_rq], [pt_os])
                  ACT(sqo[:, 0:32], ps_os[:, 0:32], AF.Square, [pt_os], [t_sqo])
                  MM(ps_os[:, 64:80], ones256[:], sqo[:, 0:16], True, False, [t_sqo, t_o256], [pt_os])
                  MM(ps_os[:, 64:80], ones256[:], sqo[:, 16:32], False, True, [t_sqo, t_o256], [pt_os])
                  RSTD(rstd_o[:, 0:16], ps_os[:, 64:80], [pt_os], t_rso)
                  for dvc in range(2):
                      DVE("tensor_tensor", [pt_os, t_rso], [t_to], out=to_[:, dvc * 16:(dvc + 1) * 16],
                          in0=ps_os[:, dvc * 16:(dvc + 1) * 16], in1=rstd_o[:, 0:16], op=ALU.mult)
                  DVE("tensor_tensor", [t_to, t_sg], [t_mix],
                      out=mixT[:, 16 + 2 * h:16 + 2 * h + 2, 512:528],
                      in0=to_[:, 0:32].rearrange("p (a b) -> p a b", a=2),
                      in1=sg[:, :, qs0:qs0 + 16], op=ALU.mult)
          if gi == 1:
              OUT(o_rp.rearrange("h k v -> k h v"), S[:], t_S)
          if dbg_mix is not None:
              OUT(dbg_mix[gi][:, :, 0:nf], mixT[:, :, 0:nf], [t_mix])
          def early(stage_no, g_at):
              return STAGE <= stage_no and gi == g_at
          if early(3, 0) or early(7, 1):
              raise EarlyExit()
          P.barrier()

          xs = [view(Rs, i * 2560, [128, 5, 128], F32) for i in range(2)]
          t_xs = [Tok(), Tok()]
          pcs_f = pieces(nf)
          nft = (nf + 127) // 128
          frow0 = row0 + f0
          for oc in range(32):
              wb, wt = load_w(w_out_v, oc * 128)
              xb, txb = xs[oc % 2], t_xs[oc % 2]
              nfull = nf // 128
              DMA("sp", xb[:, 0:nfull, :],
                  xm[frow0:frow0 + nfull * 128, oc * 128:(oc + 1) * 128].rearrange("(t p) c -> p t c", p=128),
                  writes=[txb])
              rem = nf - nfull * 128
              if rem:
                  DMA("sp", xb[0:rem, nfull, :], xm[frow0 + nfull * 128:frow0 + nf, oc * 128:(oc + 1) * 128],
                      writes=[txb])
              ps, pt = next_ps()

              def pcol(c):
                  for pi, (a, b) in enumerate(pcs_f):
                      if a <= c < b:
                          return pi * 512 + c - a
              for k in range(32):
                  for pi, (a, b) in enumerate(pcs_f):
                      MM(ps[:, pi * 512:pi * 512 + (b - a)], wb[:, k, :], mixT[:, k, a:b], k == 0, k == 31,
                         [wt, t_mix], [pt])
                  if k == 0:
                      for ti in range(nft):
                          n = min(128, nf - ti * 128)
                          c = pcol(ti * 128)
                          MM(ps[:, c:c + n], xb[0:n, ti, :], identf[0:n, 0:n], False, False, [txb, t_identf], [pt])
              for pi, (a, b) in enumerate(pcs_f):
                  COPY(xT_of(oc)[:, a:b], ps[:, pi * 512:pi * 512 + (b - a)], [pt], [t_xT])
          if early(4, 0):
              raise EarlyExit()
          P.barrier()

          h2T = view(Ra, 0, [128, 32, 528], BF16)
          t_h2 = Tok("h2T")
          sq = view(Rs, 0, [128, 32, 128], BF16)
          t_sq = Tok()
          rstd = view(Rs, 8192, [128, 528], F32)
          t_rstd = Tok()
          rmsnorm_fm(xT_of, t_xT, nf, g_ffn, t_gffn, lambda k: h2T[:, k, 0:nf], t_h2, sq, t_sq, rstd, t_rstd)
          P.barrier()
          aT = view(Rs, 0, [128, FCP, 528], BF16)
          t_aT = Tok("aT")
          u_sb = [view(Rs, 8448 + i * 2128, [128, 532], F32) for i in range(2)]
          t_u = [Tok(), Tok()]
          cgv = [view(Rs, 12704 + i * 2112, [128, 528], F32) for i in range(2)]
          t_c = [Tok(), Tok()]
          scs = [view(Rs, 16928 + i * 512, [128, 128], F32) for i in range(2)]
          t_scs = [Tok(), Tok()]
          scT = [view(Rs, 17952 + i * 128, [128, 32], F32) for i in range(2)]
          t_scT = [Tok(), Tok()]
          h2_of = lambda k: h2T[:, k, :]
          for typ in range(2):
              if gi == 0:
                  DVE("memset", [], [t_u[typ]], ap=u_sb[typ][:, 0:2], constant=0.0)
          fc0 = 0
          uo_cnt = 0
          while fc0 < NFC:
              npart = min(FCP, NFC - fc0)
              for fi in range(npart):
                  fc = fc0 + fi
                  for typ in range(2):
                      ch = typ * NFC + fc
                      wb, wt = load_w(w_up_v, ch * 128)
                      ps, pt, pcs = proj_fm(wb, wt, 32, h2_of, t_h2, 0, nf)
                      ub, tu = u_sb[typ], t_u[typ]
                      if gi == 1:
                          DVE("tensor_copy", [t_carry], [tu], out=ub[:, 0:2], in_=carry[:, ch, :])
                      for pi, (a, b) in enumerate(pcs):
                          ACT(ub[:, 2 + a:2 + b], ps[:, pi * 512:pi * 512 + (b - a)], AF.Copy, [pt], [tu])
                      if gi == 0:
                          DVE("tensor_copy", [tu], [t_carry], out=carry[:, ch, :], in_=ub[:, 2 + nf - 2:2 + nf])
                      cb, tcb = cgv[typ], t_c[typ]
                      ACT(cb[:, 0:nf], ub[:, 2:2 + nf], AF.Identity, [tu, t_convw, t_convb], [tcb],
                          scale=convw[:, 2, ch:ch + 1], bias=convb[:, ch:ch + 1])
                      DVE("scalar_tensor_tensor", [tu, tcb, t_convw], [tcb], out=cb[:, 0:nf], in0=ub[:, 1:1 + nf],
                          scalar=convw[:, 1, ch:ch + 1], in1=cb[:, 0:nf], op0=ALU.mult, op1=ALU.add)
                      DVE("scalar_tensor_tensor", [tu, tcb, t_convw], [tcb], out=cb[:, 0:nf], in0=ub[:, 0:nf],
                          scalar=convw[:, 0, ch:ch + 1], in1=cb[:, 0:nf], op0=ALU.mult, op1=ALU.add)
                      if gi == 1:
                          sb_, tsb = scs[typ], t_scs[typ]
                          DMA("sp", sb_[0:32, :], sconv[:, ch * 128:(ch + 1) * 128], writes=[tsb])
                          ps2, pt2 = next_ps()
                          TR(ps2[:, 0:32], sb_[0:32, :], identf[0:32, 0:32], [tsb, t_identf], [pt2])
                          COPY(scT[typ][:], ps2[:, 0:32], [pt2], [t_scT[typ]], eng="act")
                          sc3 = scT[typ][:].rearrange("p (s r) -> p s r", r=2)
                          ACT(cb[:, 512:528], ub[:, 514:530], AF.Identity, [tu, t_convw, t_convb], [tcb],
                              scale=convw[:, 2, ch:ch + 1], bias=convb[:, ch:ch + 1])
                          DVE("scalar_tensor_tensor", [t_scT[typ], tcb, t_convw], [tcb], out=cb[:, 512:528],
                              in0=sc3[:, :, 1], scalar=convw[:, 1, ch:ch + 1], in1=cb[:, 512:528],
                              op0=ALU.mult, op1=ALU.add)
                          DVE("scalar_tensor_tensor", [t_scT[typ], tcb, t_convw], [tcb], out=cb[:, 512:528],
                              in0=sc3[:, :, 0], scalar=convw[:, 0, ch:ch + 1], in1=cb[:, 512:528],
                              op0=ALU.mult, op1=ALU.add)
                          ps3, pt3 = next_ps()
                          TR(ps3[0:18, 0:128], ub[:, 2 + 510:2 + 528], identf[:], [tu, t_identf], [pt3])
                          j4 = fc % 4
                          COPY(ucp[typ][0:18, j4 * 128:(j4 + 1) * 128], ps3[0:18, 0:128], [pt3], [t_ucp[typ]], eng="act")
                          if j4 == 3 or fc == NFC - 1:
                              cbase = typ * FF + (fc // 4) * 512
                              w_ = (j4 + 1) * 128
                              OUT(o_cp[:, cbase:cbase + w_], ucp[typ][0:2, 0:w_], [t_ucp[typ]])
                              OUT(o_cs[:, 1, cbase:cbase + w_], ucp[typ][2:18, 0:w_], [t_ucp[typ]])
                  ACT(cgv[0][:, 0:nf], cgv[0][:, 0:nf], AF.Gelu, [t_c[0]], [t_c[0]])
                  DVE("tensor_tensor", [t_c[0], t_c[1]], [t_aT], out=aT[:, fi, 0:nf], in0=cgv[0][:, 0:nf],
                      in1=cgv[1][:, 0:nf], op=ALU.mult)
              for oc in range(32):
                  if oc % 4 == 0:
                      wb, wt = next_wb()
                      wbv = wb[:].rearrange("p k c -> p (k c)")[:, 0:npart * 512].rearrange("p (k c) -> p k c", c=512)
                      DMA("pool", wb[:].rearrange("p k c -> p (k c)")[:, 0:npart * 512],
                          w_down_v[oc // 4][:, fc0 * 512:(fc0 + npart) * 512], writes=[wt])
                  oj = oc % 4
                  ps, pt = next_ps()
                  for ki in range(npart):
                      for pi, (a, b) in enumerate(pcs_f):
                          MM(ps[:, pi * 512:pi * 512 + (b - a)], wbv[:, ki, oj * 128:(oj + 1) * 128], aT[:, ki, a:b],
                             ki == 0, ki == npart - 1, [wt, t_aT], [pt])
                  for pi, (a, b) in enumerate(pcs_f):
                      DVE("tensor_tensor", [pt, t_xT], [t_xT], out=xT_of(oc)[:, a:b], in0=xT_of(oc)[:, a:b],
                          in1=ps[:, pi * 512:pi * 512 + (b - a)], op=ALU.add)
              fc0 += npart
          if early(5, 0):
              raise EarlyExit()
          P.barrier()

          h3T = view(Ra, 0, [128, 32, 528], BF16)
          t_h3 = Tok("h3T")
          rmsnorm_fm(xT_of, t_xT, nf, g_ple, t_gple, lambda k: h3T[:, k, 0:nf], t_h3, sq, t_sq, rstd, t_rstd)
          P.barrier()
          pT = view(Rs, 0, [128, 2, 528], BF16)
          t_pT = Tok()
          p_tm = [view(Rs, 2112 + i * 1024, [128, 256], F32) for i in range(2)]
          t_ptm = [Tok(), Tok()]
          gate = view(Rs, 4160, [128, 528], F32)
          t_gate = Tok()
          prow0 = 0 if gi == 0 else 514
          for ti in range(nft):
              n = min(128, nf - ti * 128)
              pb, tpb = p_tm[ti % 2], t_ptm[ti % 2]
              DMA("sp", pb[0:n, :], pm[prow0 + ti * 128:prow0 + ti * 128 + n, :], writes=[tpb])
              ps, pt = next_ps()
              for c2 in range(2):
                  TR(ps[:, c2 * 128:c2 * 128 + n], pb[0:n, c2 * 128:(c2 + 1) * 128], identf[0:n, 0:n],
                     [tpb, t_identf], [pt])
              COPY(pT[:, :, ti * 128:ti * 128 + n], ps[:, 0:256].rearrange("p (c t) -> p c t", c=2)[:, :, 0:n],
                   [pt], [t_pT])
          h3_of = lambda k: h3T[:, k, :]
          for oc in range(32):
              wb, wt = load_w(w_gate_v, oc * 128)
              ps, pt, pcs = proj_fm(wb, wt, 32, h3_of, t_h3, 0, nf)
              for pi, (a, b) in enumerate(pcs):
                  ACT(gate[:, a:b], ps[:, pi * 512:pi * 512 + (b - a)], AF.Sigmoid, [pt], [t_gate])
              wb2, wt2 = load_w(w_proj_v, oc * 128, nk=2)
              ps2, pt2, pcs2 = proj_fm(wb2, wt2, 2, lambda k: pT[:, k, :], t_pT, 0, nf)
              for pi, (a, b) in enumerate(pcs2):
                  DVE("tensor_tensor", [pt2, t_gate], [t_gate], out=gate[:, a:b], in0=gate[:, a:b],
                      in1=ps2[:, pi * 512:pi * 512 + (b - a)], op=ALU.mult)
              DVE("tensor_tensor", [t_gate, t_xT], [t_xT], out=xT_of(oc)[:, 0:nf], in0=xT_of(oc)[:, 0:nf],
                  in1=gate[:, 0:nf], op=ALU.add)
          P.barrier()

          y_tm = view(Rs, 0, [128, 4096], F32)
          t_ytm = Tok()
          if gi == 0:
              otiles = [(2 + i * 128, 128, y_own[i * 128:(i + 1) * 128, :]) for i in range(4)]
          else:
              otiles = [(i * 128, 128, y_own[512 + i * 128:512 + (i + 1) * 128, :]) for i in range(4)]
              otiles.append((512, 16, y_smp))
          for (c0, n, dst) in otiles:
              for c4 in range(8):
                  ps, pt = next_ps()
                  for i in range(4):
                      k = c4 * 4 + i
                      TR(ps[0:n, i * 128:(i + 1) * 128], xT_of(k)[:, c0:c0 + n], identf[:], [t_xT, t_identf], [pt])
                  COPY(y_tm[0:n, c4 * 512:(c4 + 1) * 512], ps[0:n, 0:512], [pt], [t_ytm])
              OUT(dst, y_tm[0:n, :], [t_ytm])
          if early(6, 0):
              raise EarlyExit()

    try:
        _groups()
    except EarlyExit:
        P.op("sp", lambda e: e.nop(), reads=t_outs)
        P.emit()
        return nc, P
    OUT(o_cs[:, 0, :], sconv.rearrange("(s r) c -> s r c", r=2)[:, 1, :], [])
    P.op("sp", lambda e: e.nop(), reads=t_outs)
    P.emit()
    return nc, P


def kernel(x_prompt, x_sample, p_prompt, p_sample, cache_win_k, cache_win_v, state_ret, state_conv,
           rel_bias, g_mix, w_in, g_q, g_k, sinks, w_out, g_ffn, w_up, conv_w, conv_b, w_down,
           g_ple, w_ple_gate, w_ple_proj):
    f32 = np.float32
    consts, g128, gam, log_decay = host_consts()
    nc, P = build_nc(g128, gam)
    print("prog stats", P.stats, flush=True)
    if os.environ.get("MK_SITES"):
        import json
        json.dump({k: v for k, v in getattr(P, "names", {}).items()}, open("sites.json", "w"))

    def fm_vec(v):
        return np.ascontiguousarray(np.asarray(v, f32).reshape(32, 128).T)

    x_prompt = np.asarray(x_prompt, f32)
    shared = dict(consts)
    shared["identf2"] = consts["identf"]
    def blk(W, nk):
        W = np.asarray(W, f32)
        nb = W.shape[1] // 128
        return np.ascontiguousarray(W.reshape(nk, 128, nb, 128).transpose(2, 1, 0, 3)).reshape(nb, 128, nk * 128)
    shared["w_in"] = blk(w_in[0], 32)
    shared["w_out"] = blk(w_out[0], 32)
    shared["w_up"] = blk(w_up[0], 32)
    shared["w_gate"] = blk(w_ple_gate[0], 32)
    shared["w_proj"] = blk(w_ple_proj[0], 2)
    shared["w_down"] = np.ascontiguousarray(
        np.asarray(w_down[0], f32).reshape(NFC, 128, 8, 512).transpose(2, 1, 0, 3)).reshape(8, 128, NFC * 512)
    shared["g_mix"] = fm_vec(g_mix[0])
    shared["g_ffn"] = fm_vec(g_ffn[0])
    shared["g_ple"] = fm_vec(g_ple[0])
    shared["gq2"] = np.ascontiguousarray(np.concatenate([g_q[0], g_q[0]]).astype(f32).reshape(128, 1))
    shared["gk2"] = np.ascontiguousarray(np.concatenate([g_k[0], g_k[0]]).astype(f32).reshape(128, 1))
    cw = np.asarray(conv_w[0], f32)
    shared["convw"] = np.ascontiguousarray(cw.reshape(3, 2 * NFC, 128).transpose(2, 0, 1))
    shared["convb"] = np.ascontiguousarray(np.asarray(conv_b[0], f32).reshape(2 * NFC, 128).T)
    shared["relb"] = np.asarray(rel_bias, f32)
    sk = np.asarray(sinks[0], f32)
    sl = np.zeros((128, 16), f32)
    r0l = np.zeros((128, 16), f32)
    for g in range(4):
        for j in range(4):
            for half in range(2):
                sl[half * 64:(half + 1) * 64, 4 * g + j] = sk[8 * g + 2 * j + half]
    for c in range(16):
        for half in range(2):
            r0l[half * 64:(half + 1) * 64, c] = rel_bias[0, 2 * c + half]
    shared["sinkl"] = sl
    shared["relb0l"] = r0l

    in_maps = []
    for c in range(8):
        b, j = c // 4, c % 4
        t0 = 1024 * j
        m = dict(shared)
        xmr = np.zeros((1408, D), f32)
        if j > 0:
            xmr[0:256] = x_prompt[b, t0 - 256:t0]
        xmr[256:1280] = x_prompt[b, t0:t0 + 1024]
        xmr[1280:1296] = x_sample[16 * c:16 * c + 16, 0]
        m["xm"] = xmr
        xp = np.zeros((NPREV * 128, D), f32)
        npv = max(t0 - 128, 0)
        if npv > 0:
            xp[NPREV * 128 - npv:] = x_prompt[b, 0:npv]
        m["xprev"] = xp
        pmr = np.zeros((1042, 256), f32)
        pmr[2:1026] = p_prompt[0, b, t0:t0 + 1024]
        pmr[1026:1042] = p_sample[0, 16 * c:16 * c + 16, 0]
        m["pm"] = pmr
        m["ck"] = np.ascontiguousarray(np.asarray(cache_win_k[0, 16 * c:16 * c + 16], f32).reshape(16, 128, 256))
        m["cv"] = np.ascontiguousarray(np.asarray(cache_win_v[0, 16 * c:16 * c + 16], f32).reshape(16, 128, 256))
        m["sret"] = np.ascontiguousarray(np.asarray(state_ret[0, 16 * c:16 * c + 16], f32))
        m["sconv"] = np.ascontiguousarray(np.asarray(state_conv[0, 16 * c:16 * c + 16], f32).reshape(32, 2 * FF))
        posP = (t0 - 128 - NPREV * 128 + np.arange(NPREV * 128)).astype(np.int32)
        m["cosP"], m["sinP"] = rope_tables(posP)
        posA = (t0 - 128 + np.arange(640)).astype(np.int32)
        posB = np.concatenate([t0 + 512 + np.arange(512), np.full(16, 8192)]).astype(np.int32)
        m["cosA"], m["sinA"] = rope_tables(posA)
        m["cosB"], m["sinB"] = rope_tables(posB)
        m["hvalid"] = np.full((128, 64), 1.0 if j > 0 else 0.0, f32)
        in_maps.append(m)

    names = set()
    for alloc in nc.allocations:
        try:
            if alloc.kind == "ExternalInput":
                names.add(alloc.memorylocations[0].name)
        except Exception:
            pass
    if names:
        in_maps = [{k: v for k, v in m.items() if k in names} for m in in_maps]
    if os.environ.get("MK_CORES"):
        sel = [int(v) for v in os.environ["MK_CORES"].split(",")]
        res = run_bass_kernel_spmd(nc, [in_maps[i] for i in sel], core_ids=list(range(len(sel))))
        return dict(zip(sel, res.results))
    res = run_bass_kernel_spmd(nc, in_maps, core_ids=list(range(8)))
    R = res.results
    if STAGE < 99:
        return R

    yp = np.zeros((2, 4096, D), f32)
    ys = np.zeros((128, 1, D), f32)
    kp = np.zeros((1, 2, 128, 4, 64), f32)
    vp = np.zeros((1, 2, 128, 4, 64), f32)
    rp = np.zeros((1, 2, 8, 128, 256), f32)
    cp = np.zeros((1, 2, 2, 2 * FF), f32)
    ks = np.zeros((1, 128, 128, 4, 64), f32)
    vs = np.zeros((1, 128, 128, 4, 64), f32)
    rs = np.zeros((1, 128, 8, 128, 256), f32)
    cs = np.zeros((1, 128, 2, 2 * FF), f32)
    for c in range(8):
        b, j = c // 4, c % 4
        r = R[c]
        yp[b, 1024 * j:1024 * (j + 1)] = r["y_own"]
        ys[16 * c:16 * c + 16, 0] = r["y_smp"]
        if j == 3:
            kp[0, b] = r["o_kp"].reshape(128, 4, 64)
            vp[0, b] = r["o_vp"].reshape(128, 4, 64)
            rp[0, b] = r["o_rp"]
            cp[0, b] = r["o_cp"]
        ks[0, 16 * c:16 * c + 16] = r["o_ks"].reshape(16, 128, 4, 64)
        vs[0, 16 * c:16 * c + 16] = r["o_vs"].reshape(16, 128, 4, 64)
        rs[0, 16 * c:16 * c + 16] = r["o_rs"]
        cs[0, 16 * c:16 * c + 16] = r["o_cs"]
    return yp, ys, kp, vp, rp, cp, ks, vs, rs, cs
```

```python
import os
import math
import numpy as np
import concourse.bass as bass
import concourse.mybir as mybir
from concourse.bass_utils import run_bass_kernel_spmd

F32 = mybir.dt.float32
BF16 = mybir.dt.bfloat16
U8 = mybir.dt.uint8
AF = mybir.ActivationFunctionType
ALU = mybir.AluOpType

STAGE = float(os.environ.get("MK_STAGE", "99"))
SES = os.environ.get("MK_SES", "1") == "1"
SKIP = set(os.environ.get("MK_SKIP", "").split(","))

D = 4096
NKC = 32
AQ0, AK0, AV0, RQ0, RK0, RV0, RG0 = 0, 2048, 2304, 2560, 3584, 4608, 6656
INW = 8704
FF = 11008
NFC = 86
EPS = 1e-6
NPREV = 23
PREV_PASS = [6, 6, 6, 5]
GROUPS = [
    dict(name="A", row0=0, ncol=768, blocks=[0, 1, 2, 3, 4, 5], q0=128, f0=254, nsmp=0),
    dict(name="B", row0=768, ncol=528, blocks=[6, 7, 8, 9], q0=0, f0=0, nsmp=16),
]
FCP = 8


class Tok:
    __slots__ = ("name", "w", "r")

    def __init__(self, name=""):
        self.name = name
        self.w = None
        self.r = []


class Op:
    __slots__ = ("eng", "fn", "deps", "dma", "mark", "cnt", "sem", "semval", "prev_same_sem")

    def __init__(self, eng, fn, deps, dma):
        self.eng = eng
        self.fn = fn
        self.deps = deps
        self.dma = dma
        self.mark = False
        self.cnt = 0
        self.sem = None
        self.semval = 0
        self.prev_same_sem = None


class Prog:
    ENGS = ("pe", "act", "dve", "pool", "sp")

    def __init__(self, nc, n_dma_sems=8, same_engine_sync=True):
        self.nc = nc
        self.ops = []
        self.same_engine_sync = same_engine_sync
        self.n_dma_sems = n_dma_sems
        self.last = {e: None for e in self.ENGS}
        self.dma_since = []

    def op(self, eng, fn, reads=(), writes=(), dma=False, extra=()):
        idx = len(self.ops)
        deps = set(extra)
        for t in reads:
            if t.w is not None:
                deps.add(t.w)
        for t in writes:
            if t.w is not None:
                deps.add(t.w)
            deps.update(t.r)
        for t in reads:
            t.r.append(idx)
        for t in writes:
            t.w = idx
            t.r = []
        deps.discard(idx)
        self.ops.append(Op(eng, fn, deps, dma))
        if os.environ.get("MK_SITES"):
            import sys as _sys
            f = _sys._getframe(1)
            st = []
            while f is not None and len(st) < 5:
                st.append(f.f_lineno)
                f = f.f_back
            self.sites = getattr(self, "sites", {})
            self.sites[idx] = st
        self.last[eng] = idx
        if dma:
            self.dma_since.append(idx)
        return idx

    def barrier(self):
        deps = [v for v in self.last.values() if v is not None] + list(self.dma_since)
        self.dma_since = []
        for e in self.ENGS:
            self.op(e, lambda en: en.nop(), extra=deps)

    def emit(self):
        nc = self.nc
        ops = self.ops
        esem = {e: nc.alloc_semaphore(name=f"es_{e}") for e in self.ENGS}
        dsems = {e: [nc.alloc_semaphore(name=f"ds_{e}{i}") for i in range(self.n_dma_sems)]
                 for e in ("sp", "pool", "act")}
        dcount = {e: [0] * self.n_dma_sems for e in dsems}
        dlast = {e: [None] * self.n_dma_sems for e in dsems}
        dk = {e: 0 for e in dsems}
        for i, o in enumerate(ops):
            if o.dma:
                k = dk[o.eng] % self.n_dma_sems
                dk[o.eng] += 1
                dcount[o.eng][k] += 1
                o.sem = dsems[o.eng][k]
                o.semval = 16 * dcount[o.eng][k]
                o.prev_same_sem = dlast[o.eng][k]
                dlast[o.eng][k] = i
        waited = {}
        plan = []
        for i, o in enumerate(ops):
            wl = []
            byeng = {}
            dmaw = {}
            deps = set(o.deps)
            if o.dma and o.prev_same_sem is not None:
                deps.add(o.prev_same_sem)
            for d in deps:
                od = ops[d]
                if od.dma:
                    key = id(od.sem)
                    if key not in dmaw or dmaw[key][1] < od.semval:
                        dmaw[key] = (od.sem, od.semval)
                else:
                    if od.eng == o.eng and (o.eng == "pe" or not self.same_engine_sync):
                        continue
                    if od.eng not in byeng or byeng[od.eng] < d:
                        byeng[od.eng] = d
            for e, d in byeng.items():
                k = (o.eng, e)
                if waited.get(k, -1) >= d:
                    continue
                waited[k] = d
                ops[d].mark = True
                wl.append(("e", e, d))
            for key, (sem, val) in dmaw.items():
                k = (o.eng, key)
                if waited.get(k, -1) >= val:
                    continue
                waited[k] = val
                wl.append(("d", sem, val))
            plan.append(wl)
        cnt = {e: 0 for e in self.ENGS}
        for o in ops:
            if o.mark and not o.dma:
                cnt[o.eng] += 1
                o.cnt = cnt[o.eng]
        self.stats = dict(n_ops=len(ops), marks=dict(cnt),
                          per_eng={e: sum(1 for o in ops if o.eng == e) for e in self.ENGS})
        with nc.Block() as block:
            def run(engname):
                def body(e):
                    for i, o in enumerate(ops):
                        if o.eng != engname:
                            continue
                        for w in plan[i]:
                            if w[0] == "e":
                                e.wait_ge(esem[w[1]], ops[w[2]].cnt)
                            else:
                                e.wait_ge(w[1], w[2])
                        ins = o.fn(e)
                        if os.environ.get("MK_SITES"):
                            try:
                                self.names = getattr(self, "names", {})
                                self.names[ins.ins.name] = self.sites.get(i)
                            except Exception as ex:
                                pass
                        if o.dma:
                            ins.then_inc(o.sem, 16)
                        elif o.mark:
                            ins.then_inc(esem[engname], 1)
                return body
            block.tensor(run("pe"))
            block.scalar(run("act"))
            block.vector(run("dve"))
            block.gpsimd(run("pool"))
            block.sync(run("sp"))


class EarlyExit(Exception):
    pass


def pieces(n):
    if n <= 512:
        return [(0, n)]
    mid = ((n // 2) // 128) * 128
    if n - mid > 512:
        mid += 128
    assert mid <= 512 and n - mid <= 512
    return [(0, mid), (mid, n)]


def host_consts():
    c = {}
    c["identf"] = np.eye(128, dtype=np.float32)
    c["jrev"] = np.eye(128, dtype=np.float32)[::-1].copy()
    bd = np.zeros((128, 128), np.float32)
    bd[:64, :64] = 1.0
    bd[64:, 64:] = 1.0
    c["bd64"] = bd / 64.0
    c["bd1"] = bd.copy()
    c["ones4096"] = np.full((128, 128), 1.0 / 4096.0, np.float32)
    c["ones256"] = np.full((128, 128), 1.0 / 256.0, np.float32)
    c["ones64"] = np.ones((128, 64), np.float32)
    h = np.arange(8, dtype=np.float32)
    log_decay = np.log(1.0 - 2.0 ** (-5.0 - h)).astype(np.float32)
    idx = np.arange(128, dtype=np.float32)
    cs = np.float32(128.0 ** -0.5)
    diff = idx[None, :] - idx[:, None]
    dm = np.where(diff[None] >= 0, np.exp(diff[None] * log_decay[:, None, None]), 0.0).astype(np.float32)
    c["dmaskT"] = np.ascontiguousarray(np.transpose(dm, (1, 0, 2)) * cs).astype(np.float32)
    qd = np.exp((idx + 1.0)[None, :] * log_decay[:, None]).astype(np.float32)
    c["qdec"] = np.ascontiguousarray(np.broadcast_to(qd[None], (128, 8, 128))).astype(np.float32)
    kd = np.exp((127.0 - idx)[:, None] * log_decay[None]).astype(np.float32) * cs
    c["kdec"] = np.ascontiguousarray(kd).astype(np.float32)
    g128 = np.exp(np.float32(128.0) * log_decay).astype(np.float32)
    gam = np.exp(log_decay).astype(np.float32)
    r = np.arange(NPREV * 128, dtype=np.float32)
    kp = (np.exp((NPREV * 128 - 1.0 - r)[:, None] * log_decay[None]) * cs).astype(np.float32)
    c["kdecp"] = np.ascontiguousarray(kp.reshape(NPREV, 128, 8).transpose(1, 0, 2))
    def bucket(dist):
        n = np.maximum(dist, 0)
        nf = np.maximum(n, 1).astype(np.float32)
        large = 16 + (np.log(nf / np.float32(16)) / np.float32(math.log(128 / 16)) * np.float32(16)).astype(np.int32)
        return np.where(n < 16, n, np.minimum(large, 31))
    oh = np.zeros((32, 384), np.float32)
    for i in range(383):
        dist = i - 127
        if 0 <= dist <= 128:
            oh[bucket(np.array(dist)), i] = 1.0
    c["oh"] = oh
    ohs = np.zeros((32, 128), np.float32)
    for j in range(128):
        ohs[bucket(np.array(128 - j)), j] = 1.0
    c["ohs"] = ohs
    return c, g128, gam, log_decay


def rope_tables(pos):
    half = 64
    inv = (np.float32(10000.0) ** (-np.arange(half, dtype=np.float32) / np.float32(half))).astype(np.float32)
    ang = pos.astype(np.float32)[:, None] * inv[None]
    cos = np.cos(ang).astype(np.float32).T
    sin = np.sin(ang).astype(np.float32).T
    return (np.ascontiguousarray(np.concatenate([cos, cos], 0)),
            np.ascontiguousarray(np.concatenate([sin, sin], 0)))


def build_nc(g128, gam):
    nc = bass.Bass("TRN2", target_bir_lowering=False)
    P = Prog(nc, same_engine_sync=SES)

    def din(name, shape, dt=F32):
        return nc.dram_tensor(name, list(shape), dt, kind="ExternalInput").ap()

    def dout(name, shape, dt=F32):
        return nc.dram_tensor(name, list(shape), dt, kind="ExternalOutput").ap()

    def MM(out, lhsT, rhs, start, stop, reads, writes):
        P.op("pe", lambda e: e.matmul(out, lhsT=lhsT, rhs=rhs, start=start, stop=stop), reads, writes)

    def TR(out, in_, ident, reads, writes):
        P.op("pe", lambda e: e.transpose(out, in_, ident), reads, writes)

    def ACT(out, in_, func, reads, writes, **kw):
        P.op("act", lambda e: e.activation(out=out, in_=in_, func=func, **kw), reads, writes)

    def DVE(name, reads, writes, **kw):
        P.op("dve", lambda e: getattr(e, name)(**kw), reads, writes)

    def DMA(q, out, in_, reads=(), writes=()):
        P.op(q, lambda e: e.dma_start(out=out, in_=in_), reads, writes, dma=True)

    evac_rr = [0]

    def COPY(out, in_, reads, writes, eng=None):
        if eng is None:
            eng = "act" if evac_rr[0] % 2 == 0 else "dve"
            evac_rr[0] += 1
        if eng == "act":
            ACT(out, in_, AF.Copy, reads, writes)
        else:
            DVE("tensor_copy", reads, writes, out=out, in_=in_)

    xm = din("xm", [1408, D])
    xprev = din("xprev", [NPREV * 128, D])
    pm = din("pm", [1042, 256])
    w_in = din("w_in", [INW // 128, 128, 32 * 128])
    w_out = din("w_out", [32, 128, 32 * 128])
    w_up = din("w_up", [2 * NFC, 128, 32 * 128])
    w_down = din("w_down", [8, 128, NFC * 512])
    w_gate = din("w_gate", [32, 128, 32 * 128])
    w_proj = din("w_proj", [32, 128, 2 * 128])
    ck = din("ck", [16, 128, 256])
    cv = din("cv", [16, 128, 256])
    sret = din("sret", [16, 8, 128, 256])
    sconv = din("sconv", [32, 2 * FF])
    cosP_d = din("cosP", [128, NPREV * 128])
    sinP_d = din("sinP", [128, NPREV * 128])
    cosG_d = [din("cosA", [128, 640]), din("cosB", [128, 528])]
    sinG_d = [din("sinA", [128, 640]), din("sinB", [128, 528])]
    escr = nc.dram_tensor("escr", [32, 384], F32, kind="Internal").ap()

    y_own = dout("y_own", [1024, D])
    y_smp = dout("y_smp", [16, D])
    o_kp = dout("o_kp", [128, 256])
    o_vp = dout("o_vp", [128, 256])
    o_rp = dout("o_rp", [8, 128, 256])
    o_cp = dout("o_cp", [2, 2 * FF])
    o_ks = dout("o_ks", [16, 128, 256])
    o_vs = dout("o_vs", [16, 128, 256])
    o_rs = dout("o_rs", [16, 8, 128, 256])
    o_cs = dout("o_cs", [16, 2, 2 * FF])
    t_outs = []

    def OUT(out, in_, reads):
        t = Tok()
        t_outs.append(t)
        DMA("sp", out, in_, reads=reads, writes=[t])

    consts = {}

    def const(name, shape, dt=F32, src=None):
        d = src if src is not None else din(name, shape)
        t = nc.alloc_sbuf_tensor("c_" + name, list(shape), dt)
        tok = Tok(name)
        DMA("pool" if dt != F32 else "sp", t[:], d, writes=[tok])
        consts[name] = (t, tok)
        return t, tok

    identf, t_identf = const("identf", [128, 128])
    identb, t_identb = const("identb", [128, 128], BF16, src=din("identf2", [128, 128]))
    jrev, t_jrev = const("jrev", [128, 128])
    bd64, t_bd64 = const("bd64", [128, 128], BF16)
    bd1, t_bd1 = const("bd1", [128, 128], BF16)
    ones4096, t_o4096 = const("ones4096", [128, 128], BF16)
    ones256, t_o256 = const("ones256", [128, 128], BF16)
    ones64, t_o64 = const("ones64", [128, 64], BF16)
    hvalid, t_hvalid = const("hvalid", [128, 64], BF16)
    g_mix, t_gmix = const("g_mix", [128, 32])
    g_ffn, t_gffn = const("g_ffn", [128, 32])
    g_ple, t_gple = const("g_ple", [128, 32])
    gq2, t_gq2 = const("gq2", [128, 1])
    gk2, t_gk2 = const("gk2", [128, 1])
    convw, t_convw = const("convw", [128, 3, 2 * NFC])
    convb, t_convb = const("convb", [128, 2 * NFC])
    dmaskT, t_dmask = const("dmaskT", [128, 8, 128])
    qdec, t_qdec = const("qdec", [128, 8, 128])
    kdec, t_kdec = const("kdec", [128, 8])
    kdecp, t_kdecp = const("kdecp", [128, NPREV, 8])
    relb, t_relb = const("relb", [32, 32])
    oh, t_oh = const("oh", [32, 384])
    ohs, t_ohs = const("ohs", [32, 128])
    sinkl, t_sinkl = const("sinkl", [128, 16])
    relb0l, t_relb0l = const("relb0l", [128, 16])

    NPS = 3
    PS = [nc.alloc_psum_tensor(f"ps{i}", [128, 1024], F32) for i in range(NPS)]
    PST = [Tok(f"ps{i}") for i in range(NPS)]
    PH = [nc.alloc_psum_tensor(f"ph{i}", [128, 512], F32) for i in range(2)]
    PHT = [Tok(f"ph{i}") for i in range(2)]
    ps_rr = [0]

    def next_ps():
        i = ps_rr[0] % NPS
        ps_rr[0] += 1
        return PS[i], PST[i]

    RA_BYTES = 50688
    RB_BYTES = 50688
    RS_BYTES = 24576
    Ra = nc.alloc_sbuf_tensor("Ra", [128, RA_BYTES], U8)
    Rb = nc.alloc_sbuf_tensor("Rb", [128, RB_BYTES], U8)
    Rs = nc.alloc_sbuf_tensor("Rs", [128, RS_BYTES], U8)

    def view(arena, off, shape, dt):
        esz = 4 if dt == F32 else 2
        n = 1
        for s in shape[1:]:
            n *= s
        v = arena[:, off:off + n * esz].bitcast(dt)
        if len(shape) == 3:
            v = v.rearrange("p (a b) -> p a b", a=shape[1])
        elif len(shape) == 4:
            v = v.rearrange("p (a b c) -> p a b c", a=shape[1], b=shape[2])
        return v

    NWB = 3
    WB = [nc.alloc_sbuf_tensor(f"wb{i}", [128, 32, 128], BF16) for i in range(NWB)]
    WBT = [Tok(f"wb{i}") for i in range(NWB)]
    wb_rr = [0]

    def next_wb():
        i = wb_rr[0] % NWB
        wb_rr[0] += 1
        return WB[i], WBT[i]

    w_in_v, w_out_v, w_up_v, w_down_v, w_gate_v, w_proj_v = w_in, w_out, w_up, w_down, w_gate, w_proj

    def load_w(wv, col0, ncols=128, nk=32, k0=0, dup64=False):
        wb, wt = next_wb()
        b, off = col0 // 128, col0 % 128
        if dup64:
            src = wv[b].rearrange("p (k c) -> p k c", c=128)[:, 0:nk, off:off + 64]
            DMA("pool", wb[:, 0:nk, 0:64], src, writes=[wt])
            DMA("pool", wb[:, 0:nk, 64:128], src, writes=[wt])
        else:
            assert off == 0 and ncols == 128 and k0 == 0
            DMA("pool", wb[:].rearrange("p k c -> p (k c)")[:, 0:nk * 128], wv[b][:, 0:nk * 128], writes=[wt])
        return wb, wt

    kT_dup = nc.alloc_sbuf_tensor("kT_dup", [128, 4, 1296], BF16)
    t_kT = [Tok(f"kT{g}") for g in range(4)]
    V_tm = nc.alloc_sbuf_tensor("V_tm", [128, 10, 256], BF16)
    t_V = Tok("V_tm")
    S = nc.alloc_sbuf_tensor("S", [128, 8, 256], F32)
    S_bf = nc.alloc_sbuf_tensor("S_bf", [128, 8, 256], BF16)
    t_S = [Tok(f"S{h}") for h in range(8)]
    t_Sbf = [Tok(f"Sbf{h}") for h in range(8)]
    cosT = nc.alloc_sbuf_tensor("cosT", [128, 640], F32)
    sinT = nc.alloc_sbuf_tensor("sinT", [128, 640], F32)
    t_cs = Tok("cossin")
    Eg = nc.alloc_sbuf_tensor("Eg", [128, 8, 2, 128], BF16)
    t_Eg = Tok("Eg")
    Es = nc.alloc_sbuf_tensor("Es", [128, 32], F32)
    t_Es = Tok("Es")
    esl = nc.alloc_sbuf_tensor("esl", [128, 16], F32)
    t_esl = Tok("esl")
    e0l = nc.alloc_sbuf_tensor("e0l", [128, 16], F32)
    t_e0l = Tok("e0l")
    expb = nc.alloc_sbuf_tensor("expb", [32, 32], F32)
    t_expb = Tok("expb")
    carry = nc.alloc_sbuf_tensor("carry", [128, 2 * NFC, 2], F32)
    t_carry = Tok("carry")
    t_kp = Tok("kp_tm")
    t_ksn = Tok("ksn")
    t_vp = Tok("vp_tm")
    t_vsn = Tok("vsn")

    ACT(expb[:], relb[:], AF.Exp, [t_relb], [t_expb])
    ACT(esl[:], sinkl[:], AF.Exp, [t_sinkl], [t_esl])
    ACT(e0l[:], relb0l[:], AF.Exp, [t_relb0l], [t_e0l])
    ps, pt = next_ps()
    MM(ps[0:32, 0:384], expb[:], oh[:], True, True, [t_expb, t_oh], [pt])
    u_sb0 = view(Rs, 0, [128, 384], F32)
    t_u0 = Tok("u0")
    COPY(u_sb0[0:32, :], ps[0:32, 0:384], [pt], [t_u0], eng="dve")
    t_escr = Tok("escr")
    DMA("sp", escr, u_sb0[0:32, :], reads=[t_u0], writes=[t_escr])
    ps, pt = next_ps()
    MM(ps[:, 0:32], ohs[:], expb[:], True, True, [t_ohs, t_expb], [pt])
    COPY(Es[:], ps[:, 0:32], [pt], [t_Es], eng="dve")

    def RSTD(out, in_, reads, tok):
        ACT(out, in_, AF.Sqrt, reads, [tok], bias=EPS, scale=1.0)
        DVE("reciprocal", [tok], [tok], out=out, in_=out)

    def load_xT(src, n, x_tm, t_xtm, dstT, t_dst):
        DMA("sp", x_tm[0:n, :], src, writes=[t_xtm])
        for c4 in range(8):
            ps, pt = next_ps()
            for i in range(4):
                k = c4 * 4 + i
                TR(ps[:, i * 128:i * 128 + n], x_tm[0:n, k * 128:(k + 1) * 128], identf[0:n, 0:n],
                   [t_xtm, t_identf], [pt])
            COPY(dstT[:, c4 * 4:(c4 + 1) * 4, 0:n],
                 ps[:, 0:512].rearrange("p (c t) -> p c t", c=4)[:, :, 0:n], [pt], [t_dst])

    def rmsnorm_fm(xT_of, t_x, n, gvec, t_g, out_of, t_out, sq, t_sq, rstd, t_rstd, x3d=None, out3d=None):
        if x3d is not None:
            ACT(sq[:, :, 0:n], x3d, AF.Square, [t_x], [t_sq])
            ps, pt = next_ps()
            for k in range(32):
                MM(ps[:, 0:n], ones4096[:], sq[:, k, 0:n], k == 0, k == 31, [t_sq, t_o4096], [pt])
            RSTD(rstd[:, 0:n], ps[:, 0:n], [pt], t_rstd)
            DVE("tensor_tensor", [t_x, t_rstd], [t_x], out=x3d, in0=x3d,
                in1=rstd[:, 0:n].unsqueeze(1).to_broadcast([128, 32, n]), op=ALU.mult)
            DVE("tensor_tensor", [t_x, t_g], [t_out], out=out3d, in0=x3d,
                in1=gvec[:, 0:32].unsqueeze(2).to_broadcast([128, 32, n]), op=ALU.mult)
            return
        for c0 in range(0, n, 128):
            m = min(128, n - c0)
            for k in range(32):
                ACT(sq[:, k, 0:m], xT_of(k)[:, c0:c0 + m], AF.Square, [t_x], [t_sq])
            ps, pt = next_ps()
            for k in range(32):
                MM(ps[:, 0:m], ones4096[:], sq[:, k, 0:m], k == 0, k == 31, [t_sq, t_o4096], [pt])
            RSTD(rstd[:, c0:c0 + m], ps[:, 0:m], [pt], t_rstd)
        for k in range(32):
            DVE("scalar_tensor_tensor", [t_x, t_rstd, t_g], [t_out], out=out_of(k), in0=xT_of(k)[:, 0:n],
                scalar=gvec[:, k:k + 1], in1=rstd[:, 0:n], op0=ALU.mult, op1=ALU.mult)

    def proj_fm(wb, wt, nk, rhs_of, t_rhs, lo, hi):
        ps, pt = next_ps()
        pcs = pieces(hi - lo)
        for k in range(nk):
            for pi, (a, b) in enumerate(pcs):
                MM(ps[:, pi * 512:pi * 512 + (b - a)], wb[:, k, :], rhs_of(k)[:, lo + a:lo + b],
                   k == 0, k == nk - 1, [wt, t_rhs], [pt])
        return ps, pt, pcs

    def rotary(ps, pt, pcs, cos, sin, t_tab, tc0, out, t_out, X, B, t_X, t_B):
        n = pcs[-1][1]
        for pi, (a, b) in enumerate(pcs):
            ACT(X[:, a:b], ps[:, pi * 512:pi * 512 + (b - a)], AF.Copy, [pt], [t_X])
        DVE("tensor_tensor", [t_X, t_tab], [t_B], out=B[0:64, 0:n], in0=X[64:128, 0:n],
            in1=sin[64:128, tc0:tc0 + n], op=ALU.mult)
        DVE("tensor_tensor", [t_X, t_tab], [t_B], out=B[64:128, 0:n], in0=X[0:64, 0:n],
            in1=sin[0:64, tc0:tc0 + n], op=ALU.mult)
        DVE("tensor_tensor", [t_X, t_tab], [t_X], out=X[:, 0:n], in0=X[:, 0:n],
            in1=cos[:, tc0:tc0 + n], op=ALU.mult)
        DVE("tensor_tensor", [t_X, t_B], [t_out], out=out[0:64, 0:n], in0=X[0:64, 0:n],
            in1=B[0:64, 0:n], op=ALU.subtract)
        DVE("tensor_tensor", [t_X, t_B], [t_out], out=out[64:128, 0:n], in0=X[64:128, 0:n],
            in1=B[64:128, 0:n], op=ALU.add)

    P.barrier()
    hTp = view(Rb, 0, [128, 32, 768], BF16)
    t_hTp = Tok("hTp")
    x_tm = view(Ra, 0, [128, 4096], F32)
    t_xtm = Tok("x_tm")
    xT_tmp = view(Ra, 16384, [128, 32, 128], F32)
    t_xTt = Tok("xT_tmp")
    cosP = view(Ra, 32768, [128, 768], F32)
    sinP = view(Ra, 32768 + 3072, [128, 768], F32)
    t_csP = Tok("csP")
    Xr = view(Ra, 32768 + 6144, [128, 768], F32)
    Br = view(Ra, 32768 + 9216, [128, 768], F32)
    t_Xr, t_Br = Tok("Xr"), Tok("Br")
    sq = view(Rs, 0, [128, 32, 128], BF16)
    t_sq = Tok("sq")
    rstd = view(Rs, 8192, [128, 528], F32)
    t_rstd = Tok("rstd")
    rkT = view(Rs, 10304, [128, 768], BF16)
    rvT = view(Rs, 11840, [128, 2, 768], BF16)
    t_rkT, t_rvT = Tok("rkT"), Tok("rvT")
    ktm = [view(Rs, 14912 + i * 256, [128, 128], BF16) for i in range(2)]
    vtm = [view(Rs, 15424 + i * 512, [128, 256], BF16) for i in range(2)]
    t_ktm = [Tok(), Tok()]
    t_vtm = [Tok(), Tok()]
    Sps_rr = [0]

    tile0 = 0
    for pi_, ntile in enumerate(PREV_PASS):
        ncolp = ntile * 128
        DMA("sp", cosP[:, 0:ncolp], cosP_d[:, tile0 * 128:tile0 * 128 + ncolp], writes=[t_csP])
        DMA("sp", sinP[:, 0:ncolp], sinP_d[:, tile0 * 128:tile0 * 128 + ncolp], writes=[t_csP])
        for ti in range(ntile):
            r0 = (tile0 + ti) * 128
            load_xT(xprev[r0:r0 + 128, :], 128, x_tm, t_xtm, xT_tmp, t_xTt)
            rmsnorm_fm(lambda k: xT_tmp[:, k, :], t_xTt, 128, g_mix, t_gmix,
                       lambda k, ti=ti: hTp[:, k, ti * 128:(ti + 1) * 128], t_hTp, sq, t_sq, rstd, t_rstd,
                       x3d=xT_tmp[:, :, :], out3d=hTp[:, :, ti * 128:(ti + 1) * 128])
        for h in range(8):
            wb, wt = load_w(w_in_v, RK0 + h * 128)
            ps, pt, pcs = proj_fm(wb, wt, 32, lambda k: hTp[:, k, :], t_hTp, 0, ncolp)
            rotary(ps, pt, pcs, cosP, sinP, t_csP, 0, rkT, t_rkT, Xr, Br, t_Xr, t_Br)
            for dvc in range(2):
                wb, wt = load_w(w_in_v, RV0 + h * 256 + dvc * 128)
                ps, pt, pcs = proj_fm(wb, wt, 32, lambda k: hTp[:, k, :], t_hTp, 0, ncolp)
                for pj, (a, b) in enumerate(pcs):
                    COPY(rvT[:, dvc, a:b], ps[:, pj * 512:pj * 512 + (b - a)], [pt], [t_rvT])
            psS, ptS = PH[h % 2], PHT[h % 2]
            for ti in range(ntile):
                bi = ti % 2
                ps, pt = next_ps()
                psb = ps[:, 0:512].bitcast(BF16)
                TR(psb[:, 0:128], rkT[:, ti * 128:(ti + 1) * 128], identb[:], [t_rkT, t_identb], [pt])
                ACT(ktm[bi][:], psb[:, 0:128], AF.Copy, [pt, t_kdecp], [t_ktm[bi]],
                    scale=kdecp[:, tile0 + ti, h:h + 1])
                for dvc in range(2):
                    TR(psb[:, 128 + dvc * 128:256 + dvc * 128], rvT[:, dvc, ti * 128:(ti + 1) * 128], identb[:],
                       [t_rvT, t_identb], [pt])
                COPY(vtm[bi][:], psb[:, 128:384], [pt], [t_vtm[bi]], eng="act")
                MM(psS[:, 0:256], ktm[bi][:], vtm[bi][:], ti == 0, ti == ntile - 1,
                   [t_ktm[bi], t_vtm[bi]], [ptS])
            if pi_ == 0:
                COPY(S[:, h, :], psS[:, 0:256], [ptS], [t_S[h]], eng="dve")
            else:
                DVE("tensor_tensor", [ptS, t_S[h]], [t_S[h]], out=S[:, h, :], in0=S[:, h, :],
                    in1=psS[:, 0:256], op=ALU.add)
        tile0 += ntile
    for h in range(8):
        ACT(S_bf[:, h, :], S[:, h, :], AF.Copy, [t_S[h]], [t_Sbf[h]])

    if STAGE <= 2:
        OUT(o_rp.rearrange("h k v -> k h v"), S[:], t_S)
        P.op("sp", lambda e: e.nop(), reads=t_outs)
        P.emit()
        return nc, P


    t_qns = Tok("qn_s")
    t_vnT = Tok("vnT")
    ucp = [nc.alloc_sbuf_tensor(f"ucp{i}", [18, 512], F32) for i in range(2)]
    t_ucp = [Tok("ucp0"), Tok("ucp1")]
    dbg_mix = [dout(f"dbg_mix{gi}", [128, 32, 528], BF16) for gi in range(2)] if os.environ.get("MK_DBG") else None

    def xT_of(k):
        if k < 24:
            return view(Rb, k * 2112, [128, 528], F32)
        return view(Ra, 33792 + (k - 24) * 2112, [128, 528], F32)
    t_xT = Tok("xT")

    def chk(code, cond=True):
        if cond and STAGE <= code:
            raise EarlyExit()

    def _groups():
      for gi, G in enumerate(GROUPS):
          ncol, q0, f0, nsmp, row0 = G["ncol"], G["q0"], G["f0"], G["nsmp"], G["row0"]
          nq = ncol - q0
          nf = ncol - f0
          gcol0 = row0
          blocks = G["blocks"]
          P.barrier()
          hT = view(Rb, 0, [128, 32, 768], BF16)
          t_hT = Tok("hT")
          x_tm2 = [view(Ra, 0, [128, 4096], F32), view(Ra, 16384, [128, 4096], F32)]
          t_xtm2 = [Tok(), Tok()]
          xT_tmp = view(Ra, 32768, [128, 32, 128], F32)
          t_xTt = Tok()
          sq = view(Rs, 0, [128, 32, 128], BF16)
          t_sq = Tok()
          rstd = view(Rs, 8192, [128, 528], F32)
          t_rstd = Tok()
          DMA("sp", cosT[:, 0:nq], cosG_d[gi], writes=[t_cs])
          DMA("sp", sinT[:, 0:nq], sinG_d[gi], writes=[t_cs])
          ntile = (ncol + 127) // 128
          if os.environ.get('MK_SKIP16'):
              ntile = ncol // 128
          for ti in range(ntile):
              n = 128
              load_xT(xm[row0 + ti * 128:row0 + ti * 128 + n, :], n, x_tm2[ti % 2], t_xtm2[ti % 2], xT_tmp, t_xTt)
              rmsnorm_fm(lambda k: xT_tmp[:, k, :], t_xTt, n, g_mix, t_gmix,
                         lambda k, ti=ti, n=n: hT[:, k, ti * 128:ti * 128 + n], t_hT, sq, t_sq, rstd, t_rstd,
                         x3d=xT_tmp[:, :, :], out3d=hT[:, :, ti * 128:ti * 128 + n])
          if STAGE <= 6.5 and gi == 1:
              raise EarlyExit()
          P.barrier()
          mixT = view(Ra, 0, [128, 32, 528], BF16)
          t_mix = Tok("mixT")
          kp_tm = view(Ra, 33792, [128, 256], F32)
          ksn_tm = view(Ra, 34816, [128, 256], F32)
          vp_tm = view(Ra, 35840, [128, 256], F32)
          vsn_tm = view(Ra, 36864, [128, 256], F32)
          qn_s = view(Ra, 37888, [128, 16, 16], BF16)
          vnT = view(Ra, 38400, [128, 4, 16], F32)
          hT_of = lambda k: hT[:, k, :]

          tm_blocks = [(bi * 128, 128, gb) for bi, gb in enumerate(blocks)]
          if nsmp and "tmv16" not in SKIP:
              tm_blocks.append((512, 16, None))
          for half in range(2):
              wb, wt = load_w(w_in_v, AV0 + half * 128)
              for (c0, n, gb) in tm_blocks:
                  ps, pt = next_ps()
                  for k in range(32):
                      MM(ps[0:n, 0:128], hT[:, k, c0:c0 + n], wb[:, k, :], k == 0, k == 31, [t_hT, wt], [pt])
                  if gb is not None:
                      ce = "act" if gb % 2 == 0 else "dve"
                      COPY(V_tm[0:n, gb, half * 128:(half + 1) * 128], ps[0:n, 0:128], [pt], [t_V], eng=ce)
                      if gb == 9:
                          COPY(vp_tm[:, half * 128:(half + 1) * 128], ps[:, 0:128], [pt], [t_vp], eng=ce)
                  else:
                      COPY(vsn_tm[0:16, half * 128:(half + 1) * 128], ps[0:16, 0:128], [pt], [t_vsn])

          chk(6.51, gi == 1)
          qn = view(Rs, 0, [128, 4, 640], BF16)
          t_qn = Tok("qn")
          sqn = view(Rs, 5120, [128, 768], BF16)
          t_sqn = Tok()
          rst2 = view(Rs, 6656, [128, 768], F32)
          t_rst2 = Tok()
          ex = [view(Rs, 9728 + i * 2048, [128, 512], F32) for i in range(2)]
          t_ex = [Tok(), Tok()]
          PT = view(Rs, 13824, [128, 2, 2, 512], BF16)
          t_PT = Tok("PT")
          rden = view(Rs, 17920, [128, 512], F32)
          t_rden = Tok()
          Hk = [view(Rs, 19968 + i * 1024, [128, 256], F32) for i in range(2)]
          t_Hk = [Tok(), Tok()]
          knf = view(Rs, 22016, [128, 144], F32)
          t_knf = Tok()

          def norm_qk(ps, pt, pcs, gvec, t_gv, out, t_out, extra=None):
              for pi, (a, b) in enumerate(pcs):
                  ACT(sqn[:, a:b], ps[:, pi * 512:pi * 512 + (b - a)], AF.Square, [pt], [t_sqn])
              ps2, pt2 = next_ps()
              for pi, (a, b) in enumerate(pcs):
                  MM(ps2[:, pi * 512:pi * 512 + (b - a)], bd64[:], sqn[:, a:b], True, True, [t_sqn, t_bd64], [pt2])
              for pi, (a, b) in enumerate(pcs):
                  RSTD(rst2[:, a:b], ps2[:, pi * 512:pi * 512 + (b - a)], [pt2], t_rst2)
              for pi, (a, b) in enumerate(pcs):
                  DVE("scalar_tensor_tensor", [pt, t_rst2, t_gv], [t_out], out=out[:, a:b],
                      in0=ps[:, pi * 512:pi * 512 + (b - a)], scalar=gvec[:, 0:1], in1=rst2[:, a:b],
                      op0=ALU.mult, op1=ALU.mult)
                  if extra is not None:
                      for (ea, eb, dst, t_dst) in extra:
                          lo, hi = max(a, ea), min(b, eb)
                          if lo < hi:
                              DVE("scalar_tensor_tensor", [pt, t_rst2, t_gv], [t_dst], out=dst[:, lo - ea:hi - ea],
                                  in0=ps[:, pi * 512 + lo - a:pi * 512 + hi - a], scalar=gvec[:, 0:1],
                                  in1=rst2[:, lo:hi], op0=ALU.mult, op1=ALU.mult)

          for g in range(4):
              wb, wt = load_w(w_in_v, AK0 + g * 64, dup64=True)
              ps, pt, pcs = proj_fm(wb, wt, 32, hT_of, t_hT, 0, ncol)
              extra = None
              if gi == 1:
                  extra = [(384, 528, knf, t_knf)]
              norm_qk(ps, pt, pcs, gk2, t_gk2, kT_dup[:, g, gcol0:gcol0 + ncol], t_kT[g], extra)
              if gi == 1 and "knf" not in SKIP:
                  ps, pt = next_ps()
                  TR(ps[:, 0:64], knf[0:64, 0:128], identf[0:64, 0:64], [t_knf, t_identf], [pt])
                  TR(ps[0:16, 64:128], knf[0:64, 128:144], identf[0:64, 0:64], [t_knf, t_identf], [pt])
                  COPY(kp_tm[:, g * 64:(g + 1) * 64], ps[:, 0:64], [pt], [t_kp], eng="act")
                  COPY(ksn_tm[0:16, g * 64:(g + 1) * 64], ps[0:16, 64:128], [pt], [t_ksn], eng="act")
              chk(6.52, gi == 1)
              for j in range(4):
                  wb, wt = load_w(w_in_v, AQ0 + (4 * g + j) * 128)
                  ps, pt, pcs = proj_fm(wb, wt, 32, hT_of, t_hT, q0, ncol)
                  norm_qk(ps, pt, pcs, gq2, t_gq2, qn[:, j, 0:nq], t_qn)
                  if nsmp and "qns" not in SKIP:
                      DVE("tensor_copy", [t_qn], [t_qns], out=qn_s[:, 4 * g + j, :], in_=qn[:, j, 512:528])
              if nsmp and "vnT" not in SKIP:
                  wb, wt = load_w(w_in_v, AV0 + g * 64, dup64=True)
                  ps, pt = next_ps()
                  for k in range(32):
                      MM(ps[:, 0:16], wb[:, k, :], hT[:, k, 512:528], k == 0, k == 31, [wt, t_hT], [pt])
                  COPY(vnT[:, g, :], ps[:, 0:16], [pt], [t_vnT])
              chk(6.53, gi == 1)
              for hh in range(8):
                  head = 8 * g + hh
                  hk, thk = Hk[hh % 2], t_Hk[hh % 2]
                  src = bass.AP(escr.tensor, head * 384, [[1, 128], [128, 2], [1, 128]])
                  DMA("sp", hk[:].rearrange("p (a b) -> p a b", a=2), src, reads=[t_escr], writes=[thk])
                  ps, pt = next_ps()
                  MM(ps[:, 0:256], jrev[:], hk[:], True, True, [thk, t_jrev], [pt])
                  ce = "act" if hh % 2 == 0 else "dve"
                  COPY(Eg[:, hh, 1, :], ps[:, 0:128], [pt], [t_Eg], eng=ce)
                  COPY(Eg[:, hh, 0, :], ps[:, 128:256], [pt], [t_Eg], eng=ce)
              chk(6.54, gi == 1)
              for bi, gb in enumerate(blocks):
                  c0 = bi * 128
                  if c0 < q0:
                      continue
                  qi0 = c0 - q0
                  kcols = (gcol0 + c0 - 128, gcol0 + c0)
                  ei = 0
                  for kbi in range(2):
                      for half in range(2):
                          ps, pt = next_ps()
                          MM(ps[:, 0:512].rearrange("p (j q) -> p j q", j=4),
                             kT_dup[half * 64:(half + 1) * 64, g, kcols[kbi]:kcols[kbi] + 128],
                             qn[half * 64:(half + 1) * 64, :, qi0:qi0 + 128], True, True, [t_kT[g], t_qn], [pt])
                          e_, te_ = ex[ei % 2], t_ex[ei % 2]
                          ei += 1
                          ACT(e_[:], ps[:, 0:512], AF.Exp, [pt], [te_], scale=0.125)
                          DVE("tensor_tensor", [te_, t_Eg], [t_PT],
                              out=PT[:, kbi, half, :].rearrange("p (j q) -> p j q", j=4),
                              in0=e_[:].rearrange("p (j q) -> p j q", j=4),
                              in1=Eg[:, half:8:2, kbi, :], op=ALU.mult)
                  ps_o, pt_o = next_ps()
                  for half in range(2):
                      for kbi in range(2):
                          vgb = gb - 1 + kbi
                          vo, tvo = (hvalid, t_hvalid) if vgb in (0, 1) else (ones64, t_o64)
                          MM(ps_o[half * 64:(half + 1) * 64, 0:512], V_tm[:, vgb, g * 64:(g + 1) * 64],
                             PT[:, kbi, half, :], kbi == 0, kbi == 1, [t_V, t_PT], [pt_o])
                      for kbi in range(2):
                          vgb = gb - 1 + kbi
                          vo, tvo = (hvalid, t_hvalid) if vgb in (0, 1) else (ones64, t_o64)
                          MM(ps_o[half * 64:(half + 1) * 64, 512:1024], vo[:],
                             PT[:, kbi, half, :], kbi == 0, kbi == 1, [tvo, t_PT], [pt_o])
                  for j in range(4):
                      DVE("tensor_scalar", [pt_o, t_esl], [t_rden], out=rden[:, j * 128:(j + 1) * 128],
                          in0=ps_o[:, 512 + j * 128:512 + (j + 1) * 128], scalar1=esl[:, 4 * g + j:4 * g + j + 1],
                          scalar2=None, op0=ALU.add)
                  DVE("reciprocal", [t_rden], [t_rden], out=rden[:], in_=rden[:])
                  q_lo = 126 if gb == 1 else 0
                  mc0 = c0 + q_lo - f0
                  DVE("tensor_tensor", [pt_o, t_rden], [t_mix],
                      out=mixT[:, 4 * g:4 * g + 4, mc0:mc0 + 128 - q_lo],
                      in0=ps_o[:, 0:512].rearrange("p (j q) -> p j q", j=4)[:, :, q_lo:128],
                      in1=rden[:].rearrange("p (j q) -> p j q", j=4)[:, :, q_lo:128], op=ALU.mult)

          if STAGE <= 6.6 and gi == 1:
              raise EarlyExit()
          if nsmp:
              P.barrier()
              kdup_s = [view(Rs, 0 + i * 1024, [128, 4, 2, 64], BF16) for i in range(2)]
              vc_s = [view(Rs, 2048 + i * 512, [128, 256], BF16) for i in range(2)]
              t_kds = [Tok(), Tok()]
              t_vcs = [Tok(), Tok()]
              KTs = view(Rs, 3072, [128, 4, 128], BF16)
              t_KTs = Tok()
              ex_s = view(Rs, 4096, [128, 32], F32)
              t_exs = Tok()
              PTs = view(Rs, 4224, [128, 32], BF16)
              t_PTs = Tok()
              prod = view(Rs, 4352, [128, 16, 16], BF16)
              t_prod = Tok()
              pnew = view(Rs, 4864, [128, 16, 16], F32)
              t_pnew = Tok()
              vn16 = view(Rs, 5888, [128, 16, 16], F32)
              t_vn16 = Tok()
              sm_a = view(Rs, 6912, [128, 16], F32)
              sm_b = view(Rs, 6976, [128, 16], F32)
              t_sma, t_smb = Tok(), Tok()
              for c in range(16):
                  g = c // 4
                  DVE("tensor_tensor", [t_qns, t_kT[g]], [t_prod], out=prod[:, c, :], in0=qn_s[:, c, :],
                      in1=kT_dup[:, g, gcol0 + 512:gcol0 + 528], op=ALU.mult)
                  DVE("tensor_copy", [t_vnT], [t_vn16], out=vn16[:, c, :], in_=vnT[:, g, :])
              ps, pt = next_ps()
              MM(ps[:, 0:256], bd1[:], prod[:].rearrange("p a b -> p (a b)"), True, True, [t_prod, t_bd1], [pt])
              ACT(pnew[:].rearrange("p a b -> p (a b)"), ps[:, 0:256], AF.Exp, [pt], [t_pnew], scale=0.125)
              DVE("tensor_tensor", [t_pnew, t_e0l], [t_pnew], out=pnew[:], in0=pnew[:],
                  in1=e0l[:].unsqueeze(2).to_broadcast([128, 16, 16]), op=ALU.mult)
              DVE("tensor_tensor", [t_pnew, t_vn16], [t_vn16], out=vn16[:], in0=vn16[:], in1=pnew[:], op=ALU.mult)
              for s_ in range(16):
                  bi_ = s_ % 2
                  DMA("pool", kdup_s[bi_][:, :, 0, :], ck[s_].rearrange("k (g d) -> k g d", g=4), writes=[t_kds[bi_]])
                  DMA("pool", kdup_s[bi_][:, :, 1, :], ck[s_].rearrange("k (g d) -> k g d", g=4), writes=[t_kds[bi_]])
                  DMA("pool", vc_s[bi_][:], cv[s_], writes=[t_vcs[bi_]])
                  ps, pt = next_ps()
                  psb = ps[:, 0:512].bitcast(BF16)
                  for g in range(4):
                      TR(psb[:, g * 128:(g + 1) * 128], kdup_s[bi_][:, g, :, :].rearrange("p a b -> p (a b)"),
                         identb[:], [t_kds[bi_], t_identb], [pt])
                  COPY(KTs[:].rearrange("p a b -> p (a b)"), psb[:, 0:512], [pt], [t_KTs])
                  ps_sc, pt_sc = next_ps()
                  for half in range(2):
                      for g in range(4):
                          MM(ps_sc[:, half * 512 + 4 * g:half * 512 + 4 * g + 4], KTs[half * 64:(half + 1) * 64, g, :],
                             qn_s[half * 64:(half + 1) * 64, 4 * g:4 * g + 4, s_], True, True, [t_KTs, t_qns], [pt_sc])
                  for half in range(2):
                      ACT(ex_s[:, half * 16:(half + 1) * 16], ps_sc[:, half * 512:half * 512 + 16], AF.Exp,
                          [pt_sc], [t_exs], scale=0.125)
                  DVE("tensor_tensor", [t_exs, t_Es], [t_PTs], out=PTs[:].rearrange("p (h c) -> p h c", h=2),
                      in0=ex_s[:].rearrange("p (h c) -> p h c", h=2),
                      in1=Es[:].rearrange("p (c h) -> p h c", h=2), op=ALU.mult)
                  ps_os, pt_os = next_ps()
                  for g in range(4):
                      for half in range(2):
                          MM(ps_os[half * 64:(half + 1) * 64, 4 * g:4 * g + 4], vc_s[bi_][:, g * 64:(g + 1) * 64],
                             PTs[:, half * 16 + 4 * g:half * 16 + 4 * g + 4], True, True, [t_vcs[bi_], t_PTs], [pt_os])
                          MM(ps_os[half * 64:(half + 1) * 64, 16 + 4 * g:16 + 4 * g + 4], ones64[:],
                             PTs[:, half * 16 + 4 * g:half * 16 + 4 * g + 4], True, True, [t_o64, t_PTs], [pt_os])
                  DVE("tensor_tensor", [pt_os, t_vn16], [t_sma], out=sm_a[:], in0=ps_os[:, 0:16], in1=vn16[:, :, s_], op=ALU.add)
                  DVE("tensor_tensor", [pt_os, t_pnew], [t_smb], out=sm_b[:], in0=ps_os[:, 16:32], in1=pnew[:, :, s_], op=ALU.add)
                  DVE("tensor_tensor", [t_smb, t_esl], [t_smb], out=sm_b[:], in0=sm_b[:], in1=esl[:], op=ALU.add)
                  DVE("reciprocal", [t_smb], [t_smb], out=sm_b[:], in_=sm_b[:])
                  DVE("tensor_tensor", [t_sma, t_smb], [t_mix], out=mixT[:, 0:16, 512 + s_], in0=sm_a[:], in1=sm_b[:], op=ALU.mult)
              OUT(o_ks[:, 0:127, :], ck[:, 1:128, :], [])
              OUT(o_vs[:, 0:127, :], cv[:, 1:128, :], [])
              OUT(o_ks[:, 127, :], ksn_tm[0:16, :], [t_ksn])
              OUT(o_vs[:, 127, :], vsn_tm[0:16, :], [t_vsn])
              OUT(o_kp, kp_tm[:], [t_kp])
              OUT(o_vp, vp_tm[:], [t_vp])
          P.barrier()

          if STAGE <= 6.7 and gi == 1:
              raise EarlyExit()
          rq = view(Rs, 0, [128, 640], BF16)
          rk = view(Rs, 1280, [128, 640], BF16)
          t_rq, t_rk = Tok(), Tok()
          rv_tm = view(Rs, 2560, [128, 6, 256], BF16)
          t_rv = Tok()
          sg = view(Rs, 5632, [128, 2, 640], BF16)
          t_sg = Tok()
          Xr = view(Rs, 8192, [128, 640], F32)
          Br = view(Rs, 10752, [128, 640], F32)
          t_Xr, t_Br = Tok(), Tok()
          AT = [view(Rs, 13312 + i * 256, [128, 128], BF16) for i in range(2)]
          qd = [view(Rs, 13824 + i * 256, [128, 128], BF16) for i in range(2)]
          ktm = [view(Rs, 14336 + i * 256, [128, 128], BF16) for i in range(2)]
          t_AT, t_qd, t_ktm = [Tok(), Tok()], [Tok(), Tok()], [Tok(), Tok()]
          sqo = view(Rs, 14848, [128, 256], BF16)
          t_sqo = Tok()
          rstd_o = view(Rs, 15360, [128, 128], F32)
          t_rso = Tok()
          to_ = view(Rs, 15872, [128, 256], F32)
          t_to = Tok()
          s0b = [view(Rs, 16896 + i * 1024, [128, 256], F32) for i in range(2)]
          t_s0 = [Tok(), Tok()]
          s1bf = [view(Rs, 18944 + i * 512, [128, 256], BF16) for i in range(2)]
          t_s1bf = [Tok(), Tok()]
          kmask = view(Rs, 19968, [128, 16, 128], BF16)
          t_kmask = Tok()
          ktm_s = view(Rs, 24064, [128, 128], BF16)
          t_ktms = Tok()

          qblocks = [(bi * 128, 128, gb) for bi, gb in enumerate(blocks) if bi * 128 >= q0]
          rblocks = list(qblocks)
          if nsmp:
              rblocks.append((512, 16, None))
          for h in range(8):
              wb, wt = load_w(w_in_v, RQ0 + h * 128)
              ps, pt, pcs = proj_fm(wb, wt, 32, hT_of, t_hT, q0, ncol)
              rotary(ps, pt, pcs, cosT, sinT, t_cs, 0, rq, t_rq, Xr, Br, t_Xr, t_Br)
              wb, wt = load_w(w_in_v, RK0 + h * 128)
              ps, pt, pcs = proj_fm(wb, wt, 32, hT_of, t_hT, q0, ncol)
              rotary(ps, pt, pcs, cosT, sinT, t_cs, 0, rk, t_rk, Xr, Br, t_Xr, t_Br)
              for dvc in range(2):
                  wb, wt = load_w(w_in_v, RV0 + h * 256 + dvc * 128)
                  for ri, (c0, n, gb) in enumerate(rblocks):
                      ps, pt = next_ps()
                      for k in range(32):
                          MM(ps[0:n, 0:128], hT[:, k, c0:c0 + n], wb[:, k, :], k == 0, k == 31, [t_hT, wt], [pt])
                      COPY(rv_tm[0:n, ri, dvc * 128:(dvc + 1) * 128], ps[0:n, 0:128], [pt], [t_rv])
              for dvc in range(2):
                  wb, wt = load_w(w_in_v, RG0 + h * 256 + dvc * 128)
                  ps, pt, pcs = proj_fm(wb, wt, 32, hT_of, t_hT, q0, ncol)
                  for pi, (a, b) in enumerate(pcs):
                      ACT(sg[:, dvc, a:b], ps[:, pi * 512:pi * 512 + (b - a)], AF.Silu, [pt], [t_sg])
              for ri, (c0, n, gb) in enumerate(qblocks):
                  qi0 = c0 - q0
                  bi_ = ri % 2
                  ps1, pt1 = next_ps()
                  MM(ps1[:, 0:128], rk[:, qi0:qi0 + 128], rq[:, qi0:qi0 + 128], True, True, [t_rk, t_rq], [pt1])
                  DVE("tensor_tensor", [pt1, t_dmask], [t_AT[bi_]], out=AT[bi_][:], in0=ps1[:, 0:128],
                      in1=dmaskT[:, h, :], op=ALU.mult)
                  DVE("tensor_tensor", [t_rq, t_qdec], [t_qd[bi_]], out=qd[bi_][:], in0=rq[:, qi0:qi0 + 128],
                      in1=qdec[:, h, :], op=ALU.mult)
                  psb = ps1[:, 512:1024].bitcast(BF16)
                  TR(psb[:, 0:128], rk[:, qi0:qi0 + 128], identb[:], [t_rk, t_identb], [pt1])
                  ACT(ktm[bi_][:], psb[:, 0:128], AF.Copy, [pt1, t_kdec], [t_ktm[bi_]], scale=kdec[:, h:h + 1])
                  ps_o, pt_o = next_ps()
                  for dvc in range(2):
                      MM(ps_o[:, dvc * 128:(dvc + 1) * 128], rv_tm[:, ri, dvc * 128:(dvc + 1) * 128], AT[bi_][:],
                         True, False, [t_rv, t_AT[bi_]], [pt_o])
                      MM(ps_o[:, dvc * 128:(dvc + 1) * 128], S_bf[:, h, dvc * 128:(dvc + 1) * 128], qd[bi_][:],
                         False, True, [t_Sbf[h], t_qd[bi_]], [pt_o])
                  ACT(sqo[:], ps_o[:, 0:256], AF.Square, [pt_o], [t_sqo])
                  MM(ps_o[:, 512:640], ones256[:], sqo[:, 0:128], True, False, [t_sqo, t_o256], [pt_o])
                  MM(ps_o[:, 512:640], ones256[:], sqo[:, 128:256], False, True, [t_sqo, t_o256], [pt_o])
                  RSTD(rstd_o[:], ps_o[:, 512:640], [pt_o], t_rso)
                  for dvc in range(2):
                      DVE("tensor_tensor", [pt_o, t_rso], [t_to], out=to_[:, dvc * 128:(dvc + 1) * 128],
                          in0=ps_o[:, dvc * 128:(dvc + 1) * 128], in1=rstd_o[:], op=ALU.mult)
                  q_lo = 126 if gb == 1 else 0
                  mc0 = c0 + q_lo - f0
                  DVE("tensor_tensor", [t_to, t_sg], [t_mix],
                      out=mixT[:, 16 + 2 * h:16 + 2 * h + 2, mc0:mc0 + 128 - q_lo],
                      in0=to_[:].rearrange("p (a b) -> p a b", a=2)[:, :, q_lo:128],
                      in1=sg[:, :, qi0 + q_lo:qi0 + 128], op=ALU.mult)
                  psS, ptS = PH[ri % 2], PHT[ri % 2]
                  MM(psS[:, 0:256], ktm[bi_][:], rv_tm[:, ri, :], True, True, [t_ktm[bi_], t_rv], [ptS])
                  DVE("scalar_tensor_tensor", [ptS, t_S[h]], [t_S[h]], out=S[:, h, :], in0=S[:, h, :],
                      scalar=float(g128[h]), in1=psS[:, 0:256], op0=ALU.mult, op1=ALU.add)
                  ACT(S_bf[:, h, :], S[:, h, :], AF.Copy, [t_S[h]], [t_Sbf[h]])
              if nsmp:
                  ri = len(qblocks)
                  qs0 = 512 - q0
                  cs_ = float(128.0 ** -0.5)
                  ps1, pt1 = next_ps()
                  psb = ps1[:, 0:512].bitcast(BF16)
                  TR(psb[0:16, 0:128], rk[:, qs0:qs0 + 16], identb[:], [t_rk, t_identb], [pt1])
                  ACT(ktm_s[0:16, :], psb[0:16, 0:128], AF.Copy, [pt1], [t_ktms], scale=cs_)
                  for s_ in range(16):
                      DVE("tensor_scalar", [t_ktms, t_identf], [t_kmask], out=kmask[0:16, s_, :], in0=ktm_s[0:16, :],
                          scalar1=identf[0:16, s_:s_ + 1], scalar2=None, op0=ALU.mult)
                  ps_os, pt_os = PH[0], PHT[0]
                  for s_ in range(16):
                      bi_ = s_ % 2
                      DMA("sp", s0b[bi_][:], sret[s_, h], writes=[t_s0[bi_]])
                      psS, ptS = next_ps()
                      MM(psS[:, 0:256], kmask[0:16, s_, :], rv_tm[0:16, ri, :], True, True, [t_kmask, t_rv], [ptS])
                      DVE("scalar_tensor_tensor", [ptS, t_s0[bi_]], [t_s0[bi_]], out=s0b[bi_][:], in0=s0b[bi_][:],
                          scalar=float(gam[h]), in1=psS[:, 0:256], op0=ALU.mult, op1=ALU.add)
                      OUT(o_rs[s_, h], s0b[bi_][:], [t_s0[bi_]])
                      ACT(s1bf[bi_][:], s0b[bi_][:], AF.Copy, [t_s0[bi_]], [t_s1bf[bi_]])
                      for dvc in range(2):
                          MM(ps_os[:, dvc * 16 + s_:dvc * 16 + s_ + 1], s1bf[bi_][:, dvc * 128:(dvc + 1) * 128],
                             rq[:, qs0 + s_:qs0 + s_ + 1], True, True, [t_s1bf[bi_], t_rq], [pt_os])
                  ACT(sqo[:, 0:32], ps_os[:, 0:32], AF.Square, [pt_os], [t_sqo])
                  MM(ps_os[:, 64:80], ones256[:], sqo[:, 0:16], True, False, [t_sqo, t_o256], [pt_os])
                  MM(ps_os[:, 64:80], ones256[:], sqo[:, 16:32], False, True, [t_sqo, t_o256], [pt_os])
                  RSTD(rstd_o[:, 0:16], ps_os[:, 64:80], [pt_os], t_rso)
                  for dvc in range(2):
                      DVE("tensor_tensor", [pt_os, t_rso], [t_to], out=to_[:, dvc * 16:(dvc + 1) * 16],
                          in0=ps_os[:, dvc * 16:(dvc + 1) * 16], in1=rstd_o[:, 0:16], op=ALU.mult)
                  DVE("tensor_tensor", [t_to, t_sg], [t_mix],
                      out=mixT[:, 16 + 2 * h:16 + 2 * h + 2, 512:528],
                      in0=to_[:, 0:32].rearrange("p (a b) -> p a b", a=2),
                      in1=sg[:, :, qs0:qs0 + 16], op=ALU.mult)
          if gi == 1:
              OUT(o_rp.rearrange("h k v -> k h v"), S[:], t_S)
          if dbg_mix is not None:
              OUT(dbg_mix[gi][:, :, 0:nf], mixT[:, :, 0:nf], [t_mix])
          def early(stage_no, g_at):
              return STAGE <= stage_no and gi == g_at
          if early(3, 0) or early(7, 1):
              raise EarlyExit()
          P.barrier()

          xs = [view(Rs, i * 2560, [128, 5, 128], F32) for i in range(2)]
          t_xs = [Tok(), Tok()]
          pcs_f = pieces(nf)
          nft = (nf + 127) // 128
          frow0 = row0 + f0
          for oc in range(32):
              wb, wt = load_w(w_out_v, oc * 128)
              xb, txb = xs[oc % 2], t_xs[oc % 2]
              nfull = nf // 128
              DMA("sp", xb[:, 0:nfull, :],
                  xm[frow0:frow0 + nfull * 128, oc * 128:(oc + 1) * 128].rearrange("(t p) c -> p t c", p=128),
                  writes=[txb])
              rem = nf - nfull * 128
              if rem:
                  DMA("sp", xb[0:rem, nfull, :], xm[frow0 + nfull * 128:frow0 + nf, oc * 128:(oc + 1) * 128],
                      writes=[txb])
              ps, pt = next_ps()

              def pcol(c):
                  for pi, (a, b) in enumerate(pcs_f):
                      if a <= c < b:
                          return pi * 512 + c - a
              for k in range(32):
                  for pi, (a, b) in enumerate(pcs_f):
                      MM(ps[:, pi * 512:pi * 512 + (b - a)], wb[:, k, :], mixT[:, k, a:b], k == 0, k == 31,
                         [wt, t_mix], [pt])
                  if k == 0:
                      for ti in range(nft):
                          n = min(128, nf - ti * 128)
                          c = pcol(ti * 128)
                          MM(ps[:, c:c + n], xb[0:n, ti, :], identf[0:n, 0:n], False, False, [txb, t_identf], [pt])
              for pi, (a, b) in enumerate(pcs_f):
                  COPY(xT_of(oc)[:, a:b], ps[:, pi * 512:pi * 512 + (b - a)], [pt], [t_xT])
          if early(4, 0):
              raise EarlyExit()
          P.barrier()

          h2T = view(Ra, 0, [128, 32, 528], BF16)
          t_h2 = Tok("h2T")
          sq = view(Rs, 0, [128, 32, 128], BF16)
          t_sq = Tok()
          rstd = view(Rs, 8192, [128, 528], F32)
          t_rstd = Tok()
          rmsnorm_fm(xT_of, t_xT, nf, g_ffn, t_gffn, lambda k: h2T[:, k, 0:nf], t_h2, sq, t_sq, rstd, t_rstd)
          P.barrier()
          aT = view(Rs, 0, [128, FCP, 528], BF16)
          t_aT = Tok("aT")
          u_sb = [view(Rs, 8448 + i * 2128, [128, 532], F32) for i in range(2)]
          t_u = [Tok(), Tok()]
          cgv = [view(Rs, 12704 + i * 2112, [128, 528], F32) for i in range(2)]
          t_c = [Tok(), Tok()]
          scs = [view(Rs, 16928 + i * 512, [128, 128], F32) for i in range(2)]
          t_scs = [Tok(), Tok()]
          scT = [view(Rs, 17952 + i * 128, [128, 32], F32) for i in range(2)]
          t_scT = [Tok(), Tok()]
          h2_of = lambda k: h2T[:, k, :]
          for typ in range(2):
              if gi == 0:
                  DVE("memset", [], [t_u[typ]], ap=u_sb[typ][:, 0:2], constant=0.0)
          fc0 = 0
          uo_cnt = 0
          while fc0 < NFC:
              npart = min(FCP, NFC - fc0)
              for fi in range(npart):
                  fc = fc0 + fi
                  for typ in range(2):
                      ch = typ * NFC + fc
                      wb, wt = load_w(w_up_v, ch * 128)
                      ps, pt, pcs = proj_fm(wb, wt, 32, h2_of, t_h2, 0, nf)
                      ub, tu = u_sb[typ], t_u[typ]
                      if gi == 1:
                          DVE("tensor_copy", [t_carry], [tu], out=ub[:, 0:2], in_=carry[:, ch, :])
                      for pi, (a, b) in enumerate(pcs):
                          ACT(ub[:, 2 + a:2 + b], ps[:, pi * 512:pi * 512 + (b - a)], AF.Copy, [pt], [tu])
                      if gi == 0:
                          DVE("tensor_copy", [tu], [t_carry], out=carry[:, ch, :], in_=ub[:, 2 + nf - 2:2 + nf])
                      cb, tcb = cgv[typ], t_c[typ]
                      ACT(cb[:, 0:nf], ub[:, 2:2 + nf], AF.Identity, [tu, t_convw, t_convb], [tcb],
                          scale=convw[:, 2, ch:ch + 1], bias=convb[:, ch:ch + 1])
                      DVE("scalar_tensor_tensor", [tu, tcb, t_convw], [tcb], out=cb[:, 0:nf], in0=ub[:, 1:1 + nf],
                          scalar=convw[:, 1, ch:ch + 1], in1=cb[:, 0:nf], op0=ALU.mult, op1=ALU.add)
                      DVE("scalar_tensor_tensor", [tu, tcb, t_convw], [tcb], out=cb[:, 0:nf], in0=ub[:, 0:nf],
                          scalar=convw[:, 0, ch:ch + 1], in1=cb[:, 0:nf], op0=ALU.mult, op1=ALU.add)
                      if gi == 1:
                          sb_, tsb = scs[typ], t_scs[typ]
                          DMA("sp", sb_[0:32, :], sconv[:, ch * 128:(ch + 1) * 128], writes=[tsb])
                          ps2, pt2 = next_ps()
                          TR(ps2[:, 0:32], sb_[0:32, :], identf[0:32, 0:32], [tsb, t_identf], [pt2])
                          COPY(scT[typ][:], ps2[:, 0:32], [pt2], [t_scT[typ]], eng="act")
                          sc3 = scT[typ][:].rearrange("p (s r) -> p s r", r=2)
                          ACT(cb[:, 512:528], ub[:, 514:530], AF.Identity, [tu, t_convw, t_convb], [tcb],
                              scale=convw[:, 2, ch:ch + 1], bias=convb[:, ch:ch + 1])
                          DVE("scalar_tensor_tensor", [t_scT[typ], tcb, t_convw], [tcb], out=cb[:, 512:528],
                              in0=sc3[:, :, 1], scalar=convw[:, 1, ch:ch + 1], in1=cb[:, 512:528],
                              op0=ALU.mult, op1=ALU.add)
                          DVE("scalar_tensor_tensor", [t_scT[typ], tcb, t_convw], [tcb], out=cb[:, 512:528],
                              in0=sc3[:, :, 0], scalar=convw[:, 0, ch:ch + 1], in1=cb[:, 512:528],
                              op0=ALU.mult, op1=ALU.add)
                          ps3, pt3 = next_ps()
                          TR(ps3[0:18, 0:128], ub[:, 2 + 510:2 + 528], identf[:], [tu, t_identf], [pt3])
                          j4 = fc % 4
                          COPY(ucp[typ][0:18, j4 * 128:(j4 + 1) * 128], ps3[0:18, 0:128], [pt3], [t_ucp[typ]], eng="act")
                          if j4 == 3 or fc == NFC - 1:
                              cbase = typ * FF + (fc // 4) * 512
                              w_ = (j4 + 1) * 128
                              OUT(o_cp[:, cbase:cbase + w_], ucp[typ][0:2, 0:w_], [t_ucp[typ]])
                              OUT(o_cs[:, 1, cbase:cbase + w_], ucp[typ][2:18, 0:w_], [t_ucp[typ]])
                  ACT(cgv[0][:, 0:nf], cgv[0][:, 0:nf], AF.Gelu, [t_c[0]], [t_c[0]])
                  DVE("tensor_tensor", [t_c[0], t_c[1]], [t_aT], out=aT[:, fi, 0:nf], in0=cgv[0][:, 0:nf],
                      in1=cgv[1][:, 0:nf], op=ALU.mult)
              for oc in range(32):
                  if oc % 4 == 0:
                      wb, wt = next_wb()
                      wbv = wb[:].rearrange("p k c -> p (k c)")[:, 0:npart * 512].rearrange("p (k c) -> p k c", c=512)
                      DMA("pool", wb[:].rearrange("p k c -> p (k c)")[:, 0:npart * 512],
                          w_down_v[oc // 4][:, fc0 * 512:(fc0 + npart) * 512], writes=[wt])
                  oj = oc % 4
                  ps, pt = next_ps()
                  for ki in range(npart):
                      for pi, (a, b) in enumerate(pcs_f):
                          MM(ps[:, pi * 512:pi * 512 + (b - a)], wbv[:, ki, oj * 128:(oj + 1) * 128], aT[:, ki, a:b],
                             ki == 0, ki == npart - 1, [wt, t_aT], [pt])
                  for pi, (a, b) in enumerate(pcs_f):
                      DVE("tensor_tensor", [pt, t_xT], [t_xT], out=xT_of(oc)[:, a:b], in0=xT_of(oc)[:, a:b],
                          in1=ps[:, pi * 512:pi * 512 + (b - a)], op=ALU.add)
              fc0 += npart
          if early(5, 0):
              raise EarlyExit()
          P.barrier()

          h3T = view(Ra, 0, [128, 32, 528], BF16)
          t_h3 = Tok("h3T")
          rmsnorm_fm(xT_of, t_xT, nf, g_ple, t_gple, lambda k: h3T[:, k, 0:nf], t_h3, sq, t_sq, rstd, t_rstd)
          P.barrier()
          pT = view(Rs, 0, [128, 2, 528], BF16)
          t_pT = Tok()
          p_tm = [view(Rs, 2112 + i * 1024, [128, 256], F32) for i in range(2)]
          t_ptm = [Tok(), Tok()]
          gate = view(Rs, 4160, [128, 528], F32)
          t_gate = Tok()
          prow0 = 0 if gi == 0 else 514
          for ti in range(nft):
              n = min(128, nf - ti * 128)
              pb, tpb = p_tm[ti % 2], t_ptm[ti % 2]
              DMA("sp", pb[0:n, :], pm[prow0 + ti * 128:prow0 + ti * 128 + n, :], writes=[tpb])
              ps, pt = next_ps()
              for c2 in range(2):
                  TR(ps[:, c2 * 128:c2 * 128 + n], pb[0:n, c2 * 128:(c2 + 1) * 128], identf[0:n, 0:n],
                     [tpb, t_identf], [pt])
              COPY(pT[:, :, ti * 128:ti * 128 + n], ps[:, 0:256].rearrange("p (c t) -> p c t", c=2)[:, :, 0:n],
                   [pt], [t_pT])
          h3_of = lambda k: h3T[:, k, :]
          for oc in range(32):
              wb, wt = load_w(w_gate_v, oc * 128)
              ps, pt, pcs = proj_fm(wb, wt, 32, h3_of, t_h3, 0, nf)
              for pi, (a, b) in enumerate(pcs):
                  ACT(gate[:, a:b], ps[:, pi * 512:pi * 512 + (b - a)], AF.Sigmoid, [pt], [t_gate])
              wb2, wt2 = load_w(w_proj_v, oc * 128, nk=2)
              ps2, pt2, pcs2 = proj_fm(wb2, wt2, 2, lambda k: pT[:, k, :], t_pT, 0, nf)
              for pi, (a, b) in enumerate(pcs2):
                  DVE("tensor_tensor", [pt2, t_gate], [t_gate], out=gate[:, a:b], in0=gate[:, a:b],
                      in1=ps2[:, pi * 512:pi * 512 + (b - a)], op=ALU.mult)
              DVE("tensor_tensor", [t_gate, t_xT], [t_xT], out=xT_of(oc)[:, 0:nf], in0=xT_of(oc)[:, 0:nf],
                  in1=gate[:, 0:nf], op=ALU.add)
          P.barrier()

          y_tm = view(Rs, 0, [128, 4096], F32)
          t_ytm = Tok()
          if gi == 0:
              otiles = [(2 + i * 128, 128, y_own[i * 128:(i + 1) * 128, :]) for i in range(4)]
          else:
              otiles = [(i * 128, 128, y_own[512 + i * 128:512 + (i + 1) * 128, :]) for i in range(4)]
              otiles.append((512, 16, y_smp))
          for (c0, n, dst) in otiles:
              for c4 in range(8):
                  ps, pt = next_ps()
                  for i in range(4):
                      k = c4 * 4 + i
                      TR(ps[0:n, i * 128:(i + 1) * 128], xT_of(k)[:, c0:c0 + n], identf[:], [t_xT, t_identf], [pt])
                  COPY(y_tm[0:n, c4 * 512:(c4 + 1) * 512], ps[0:n, 0:512], [pt], [t_ytm])
              OUT(dst, y_tm[0:n, :], [t_ytm])
          if early(6, 0):
              raise EarlyExit()

    try:
        _groups()
    except EarlyExit:
        P.op("sp", lambda e: e.nop(), reads=t_outs)
        P.emit()
        return nc, P
    OUT(o_cs[:, 0, :], sconv.rearrange("(s r) c -> s r c", r=2)[:, 1, :], [])
    P.op("sp", lambda e: e.nop(), reads=t_outs)
    P.emit()
    return nc, P


def kernel(x_prompt, x_sample, p_prompt, p_sample, cache_win_k, cache_win_v, state_ret, state_conv,
           rel_bias, g_mix, w_in, g_q, g_k, sinks, w_out, g_ffn, w_up, conv_w, conv_b, w_down,
           g_ple, w_ple_gate, w_ple_proj):
    f32 = np.float32
    consts, g128, gam, log_decay = host_consts()
    nc, P = build_nc(g128, gam)
    print("prog stats", P.stats, flush=True)
    if os.environ.get("MK_SITES"):
        import json
        json.dump({k: v for k, v in getattr(P, "names", {}).items()}, open("sites.json", "w"))

    def fm_vec(v):
        return np.ascontiguousarray(np.asarray(v, f32).reshape(32, 128).T)

    x_prompt = np.asarray(x_prompt, f32)
    shared = dict(consts)
    shared["identf2"] = consts["identf"]
    def blk(W, nk):
        W = np.asarray(W, f32)
        nb = W.shape[1] // 128
        return np.ascontiguousarray(W.reshape(nk, 128, nb, 128).transpose(2, 1, 0, 3)).reshape(nb, 128, nk * 128)
    shared["w_in"] = blk(w_in[0], 32)
    shared["w_out"] = blk(w_out[0], 32)
    shared["w_up"] = blk(w_up[0], 32)
    shared["w_gate"] = blk(w_ple_gate[0], 32)
    shared["w_proj"] = blk(w_ple_proj[0], 2)
    shared["w_down"] = np.ascontiguousarray(
        np.asarray(w_down[0], f32).reshape(NFC, 128, 8, 512).transpose(2, 1, 0, 3)).reshape(8, 128, NFC * 512)
    shared["g_mix"] = fm_vec(g_mix[0])
    shared["g_ffn"] = fm_vec(g_ffn[0])
    shared["g_ple"] = fm_vec(g_ple[0])
    shared["gq2"] = np.ascontiguousarray(np.concatenate([g_q[0], g_q[0]]).astype(f32).reshape(128, 1))
    shared["gk2"] = np.ascontiguousarray(np.concatenate([g_k[0], g_k[0]]).astype(f32).reshape(128, 1))
    cw = np.asarray(conv_w[0], f32)
    shared["convw"] = np.ascontiguousarray(cw.reshape(3, 2 * NFC, 128).transpose(2, 0, 1))
    shared["convb"] = np.ascontiguousarray(np.asarray(conv_b[0], f32).reshape(2 * NFC, 128).T)
    shared["relb"] = np.asarray(rel_bias, f32)
    sk = np.asarray(sinks[0], f32)
    sl = np.zeros((128, 16), f32)
    r0l = np.zeros((128, 16), f32)
    for g in range(4):
        for j in range(4):
            for half in range(2):
                sl[half * 64:(half + 1) * 64, 4 * g + j] = sk[8 * g + 2 * j + half]
    for c in range(16):
        for half in range(2):
            r0l[half * 64:(half + 1) * 64, c] = rel_bias[0, 2 * c + half]
    shared["sinkl"] = sl
    shared["relb0l"] = r0l

    in_maps = []
    for c in range(8):
        b, j = c // 4, c % 4
        t0 = 1024 * j
        m = dict(shared)
        xmr = np.zeros((1408, D), f32)
        if j > 0:
            xmr[0:256] = x_prompt[b, t0 - 256:t0]
        xmr[256:1280] = x_prompt[b, t0:t0 + 1024]
        xmr[1280:1296] = x_sample[16 * c:16 * c + 16, 0]
        m["xm"] = xmr
        xp = np.zeros((NPREV * 128, D), f32)
        npv = max(t0 - 128, 0)
        if npv > 0:
            xp[NPREV * 128 - npv:] = x_prompt[b, 0:npv]
        m["xprev"] = xp
        pmr = np.zeros((1042, 256), f32)
        pmr[2:1026] = p_prompt[0, b, t0:t0 + 1024]
        pmr[1026:1042] = p_sample[0, 16 * c:16 * c + 16, 0]
        m["pm"] = pmr
        m["ck"] = np.ascontiguousarray(np.asarray(cache_win_k[0, 16 * c:16 * c + 16], f32).reshape(16, 128, 256))
        m["cv"] = np.ascontiguousarray(np.asarray(cache_win_v[0, 16 * c:16 * c + 16], f32).reshape(16, 128, 256))
        m["sret"] = np.ascontiguousarray(np.asarray(state_ret[0, 16 * c:16 * c + 16], f32))
        m["sconv"] = np.ascontiguousarray(np.asarray(state_conv[0, 16 * c:16 * c + 16], f32).reshape(32, 2 * FF))
        posP = (t0 - 128 - NPREV * 128 + np.arange(NPREV * 128)).astype(np.int32)
        m["cosP"], m["sinP"] = rope_tables(posP)
        posA = (t0 - 128 + np.arange(640)).astype(np.int32)
        posB = np.concatenate([t0 + 512 + np.arange(512), np.full(16, 8192)]).astype(np.int32)
        m["cosA"], m["sinA"] = rope_tables(posA)
        m["cosB"], m["sinB"] = rope_tables(posB)
        m["hvalid"] = np.full((128, 64), 1.0 if j > 0 else 0.0, f32)
        in_maps.append(m)

    names = set()
    for alloc in nc.allocations:
        try:
            if alloc.kind == "ExternalInput":
                names.add(alloc.memorylocations[0].name)
        except Exception:
            pass
    if names:
        in_maps = [{k: v for k, v in m.items() if k in names} for m in in_maps]
    if os.environ.get("MK_CORES"):
        sel = [int(v) for v in os.environ["MK_CORES"].split(",")]
        res = run_bass_kernel_spmd(nc, [in_maps[i] for i in sel], core_ids=list(range(len(sel))))
        return dict(zip(sel, res.results))
    res = run_bass_kernel_spmd(nc, in_maps, core_ids=list(range(8)))
    R = res.results
    if STAGE < 99:
        return R

    yp = np.zeros((2, 4096, D), f32)
    ys = np.zeros((128, 1, D), f32)
    kp = np.zeros((1, 2, 128, 4, 64), f32)
    vp = np.zeros((1, 2, 128, 4, 64), f32)
    rp = np.zeros((1, 2, 8, 128, 256), f32)
    cp = np.zeros((1, 2, 2, 2 * FF), f32)
    ks = np.zeros((1, 128, 128, 4, 64), f32)
    vs = np.zeros((1, 128, 128, 4, 64), f32)
    rs = np.zeros((1, 128, 8, 128, 256), f32)
    cs = np.zeros((1, 128, 2, 2 * FF), f32)
    for c in range(8):
        b, j = c // 4, c % 4
        r = R[c]
        yp[b, 1024 * j:1024 * (j + 1)] = r["y_own"]
        ys[16 * c:16 * c + 16, 0] = r["y_smp"]
        if j == 3:
            kp[0, b] = r["o_kp"].reshape(128, 4, 64)
            vp[0, b] = r["o_vp"].reshape(128, 4, 64)
            rp[0, b] = r["o_rp"]
            cp[0, b] = r["o_cp"]
        ks[0, 16 * c:16 * c + 16] = r["o_ks"].reshape(16, 128, 4, 64)
        vs[0, 16 * c:16 * c + 16] = r["o_vs"].reshape(16, 128, 4, 64)
        rs[0, 16 * c:16 * c + 16] = r["o_rs"]
        cs[0, 16 * c:16 * c + 16] = r["o_cs"]
    return yp, ys, kp, vp, rp, cp, ks, vs, rs, cs
```

```python
import os
import math
import numpy as np
import concourse.bass as bass
import concourse.mybir as mybir
from concourse.bass_utils import run_bass_kernel_spmd

F32 = mybir.dt.float32
BF16 = mybir.dt.bfloat16
U8 = mybir.dt.uint8
AF = mybir.ActivationFunctionType
ALU = mybir.AluOpType

STAGE = float(os.environ.get("MK_STAGE", "99"))
SES = os.environ.get("MK_SES", "1") == "1"
SKIP = set(os.environ.get("MK_SKIP", "").split(","))

D = 4096
NKC = 32
AQ0, AK0, AV0, RQ0, RK0, RV0, RG0 = 0, 2048, 2304, 2560, 3584, 4608, 6656
INW = 8704
FF = 11008
NFC = 86
EPS = 1e-6
NPREV = 23
PREV_PASS = [6, 6, 6, 5]
GROUPS = [
    dict(name="A", row0=0, ncol=768, blocks=[0, 1, 2, 3, 4, 5], q0=128, f0=254, nsmp=0),
    dict(name="B", row0=768, ncol=528, blocks=[6, 7, 8, 9], q0=0, f0=0, nsmp=16),
]
FCP = 8


class Tok:
    __slots__ = ("name", "w", "r")

    def __init__(self, name=""):
        self.name = name
        self.w = None
        self.r = []


class Op:
    __slots__ = ("eng", "fn", "deps", "dma", "mark", "cnt", "sem", "semval", "prev_same_sem")

    def __init__(self, eng, fn, deps, dma):
        self.eng = eng
        self.fn = fn
        self.deps = deps
        self.dma = dma
        self.mark = False
        self.cnt = 0
        self.sem = None
        self.semval = 0
        self.prev_same_sem = None


class Prog:
    ENGS = ("pe", "act", "dve", "pool", "sp")

    def __init__(self, nc, n_dma_sems=8, same_engine_sync=True):
        self.nc = nc
        self.ops = []
        self.same_engine_sync = same_engine_sync
        self.n_dma_sems = n_dma_sems
        self.last = {e: None for e in self.ENGS}
        self.dma_since = []

    def op(self, eng, fn, reads=(), writes=(), dma=False, extra=()):
        idx = len(self.ops)
        deps = set(extra)
        for t in reads:
            if t.w is not None:
                deps.add(t.w)
        for t in writes:
            if t.w is not None:
                deps.add(t.w)
            deps.update(t.r)
        for t in reads:
            t.r.append(idx)
        for t in writes:
            t.w = idx
            t.r = []
        deps.discard(idx)
        self.ops.append(Op(eng, fn, deps, dma))
        if os.environ.get("MK_SITES"):
            import sys as _sys
            f = _sys._getframe(1)
            st = []
            while f is not None and len(st) < 5:
                st.append(f.f_lineno)
                f = f.f_back
            self.sites = getattr(self, "sites", {})
            self.sites[idx] = st
        self.last[eng] = idx
        if dma:
            self.dma_since.append(idx)
        return idx

    def barrier(self):
        deps = [v for v in self.last.values() if v is not None] + list(self.dma_since)
        self.dma_since = []
        for e in self.ENGS:
            self.op(e, lambda en: en.nop(), extra=deps)

    def emit(self):
        nc = self.nc
        ops = self.ops
        esem = {e: nc.alloc_semaphore(name=f"es_{e}") for e in self.ENGS}
        dsems = {e: [nc.alloc_semaphore(name=f"ds_{e}{i}") for i in range(self.n_dma_sems)]
                 for e in ("sp", "pool", "act")}
        dcount = {e: [0] * self.n_dma_sems for e in dsems}
        dlast = {e: [None] * self.n_dma_sems for e in dsems}
        dk = {e: 0 for e in dsems}
        for i, o in enumerate(ops):
            if o.dma:
                k = dk[o.eng] % self.n_dma_sems
                dk[o.eng] += 1
                dcount[o.eng][k] += 1
                o.sem = dsems[o.eng][k]
                o.semval = 16 * dcount[o.eng][k]
                o.prev_same_sem = dlast[o.eng][k]
                dlast[o.eng][k] = i
        waited = {}
        plan = []
        for i, o in enumerate(ops):
            wl = []
            byeng = {}
            dmaw = {}
            deps = set(o.deps)
            if o.dma and o.prev_same_sem is not None:
                deps.add(o.prev_same_sem)
            for d in deps:
                od = ops[d]
                if od.dma:
                    key = id(od.sem)
                    if key not in dmaw or dmaw[key][1] < od.semval:
                        dmaw[key] = (od.sem, od.semval)
                else:
                    if od.eng == o.eng and (o.eng == "pe" or not self.same_engine_sync):
                        continue
                    if od.eng not in byeng or byeng[od.eng] < d:
                        byeng[od.eng] = d
            for e, d in byeng.items():
                k = (o.eng, e)
                if waited.get(k, -1) >= d:
                    continue
                waited[k] = d
                ops[d].mark = True
                wl.append(("e", e, d))
            for key, (sem, val) in dmaw.items():
                k = (o.eng, key)
                if waited.get(k, -1) >= val:
                    continue
                waited[k] = val
                wl.append(("d", sem, val))
            plan.append(wl)
        cnt = {e: 0 for e in self.ENGS}
        for o in ops:
            if o.mark and not o.dma:
                cnt[o.eng] += 1
                o.cnt = cnt[o.eng]
        self.stats = dict(n_ops=len(ops), marks=dict(cnt),
                          per_eng={e: sum(1 for o in ops if o.eng == e) for e in self.ENGS})
        with nc.Block() as block:
            def run(engname):
                def body(e):
                    for i, o in enumerate(ops):
                        if o.eng != engname:
                            continue
                        for w in plan[i]:
                            if w[0] == "e":
                                e.wait_ge(esem[w[1]], ops[w[2]].cnt)
                            else:
                                e.wait_ge(w[1], w[2])
                        ins = o.fn(e)
                        if os.environ.get("MK_SITES"):
                            try:
                                self.names = getattr(self, "names", {})
                                self.names[ins.ins.name] = self.sites.get(i)
                            except Exception as ex:
                                pass
                        if o.dma:
                            ins.then_inc(o.sem, 16)
                        elif o.mark:
                            ins.then_inc(esem[engname], 1)
                return body
            block.tensor(run("pe"))
            block.scalar(run("act"))
            block.vector(run("dve"))
            block.gpsimd(run("pool"))
            block.sync(run("sp"))


class EarlyExit(Exception):
    pass


def pieces(n):
    if n <= 512:
        return [(0, n)]
    mid = ((n // 2) // 128) * 128
    if n - mid > 512:
        mid += 128
    assert mid <= 512 and n - mid <= 512
    return [(0, mid), (mid, n)]


def host_consts():
    c = {}
    c["identf"] = np.eye(128, dtype=np.float32)
    c["jrev"] = np.eye(128, dtype=np.float32)[::-1].copy()
    bd = np.zeros((128, 128), np.float32)
    bd[:64, :64] = 1.0
    bd[64:, 64:] = 1.0
    c["bd64"] = bd / 64.0
    c["bd1"] = bd.copy()
    c["ones4096"] = np.full((128, 128), 1.0 / 4096.0, np.float32)
    c["ones256"] = np.full((128, 128), 1.0 / 256.0, np.float32)
    c["ones64"] = np.ones((128, 64), np.float32)
    h = np.arange(8, dtype=np.float32)
    log_decay = np.log(1.0 - 2.0 ** (-5.0 - h)).astype(np.float32)
    idx = np.arange(128, dtype=np.float32)
    cs = np.float32(128.0 ** -0.5)
    diff = idx[None, :] - idx[:, None]
    dm = np.where(diff[None] >= 0, np.exp(diff[None] * log_decay[:, None, None]), 0.0).astype(np.float32)
    c["dmaskT"] = np.ascontiguousarray(np.transpose(dm, (1, 0, 2)) * cs).astype(np.float32)
    qd = np.exp((idx + 1.0)[None, :] * log_decay[:, None]).astype(np.float32)
    c["qdec"] = np.ascontiguousarray(np.broadcast_to(qd[None], (128, 8, 128))).astype(np.float32)
    kd = np.exp((127.0 - idx)[:, None] * log_decay[None]).astype(np.float32) * cs
    c["kdec"] = np.ascontiguousarray(kd).astype(np.float32)
    g128 = np.exp(np.float32(128.0) * log_decay).astype(np.float32)
    gam = np.exp(log_decay).astype(np.float32)
    r = np.arange(NPREV * 128, dtype=np.float32)
    kp = (np.exp((NPREV * 128 - 1.0 - r)[:, None] * log_decay[None]) * cs).astype(np.float32)
    c["kdecp"] = np.ascontiguousarray(kp.reshape(NPREV, 128, 8).transpose(1, 0, 2))
    def bucket(dist):
        n = np.maximum(dist, 0)
        nf = np.maximum(n, 1).astype(np.float32)
        large = 16 + (np.log(nf / np.float32(16)) / np.float32(math.log(128 / 16)) * np.float32(16)).astype(np.int32)
        return np.where(n < 16, n, np.minimum(large, 31))
    oh = np.zeros((32, 384), np.float32)
    for i in range(383):
        dist = i - 127
        if 0 <= dist <= 128:
            oh[bucket(np.array(dist)), i] = 1.0
    c["oh"] = oh
    ohs = np.zeros((32, 128), np.float32)
    for j in range(128):
        ohs[bucket(np.array(128 - j)), j] = 1.0
    c["ohs"] = ohs
    return c, g128, gam, log_decay


def rope_tables(pos):
    half = 64
    inv = (np.float32(10000.0) ** (-np.arange(half, dtype=np.float32) / np.float32(half))).astype(np.float32)
    ang = pos.astype(np.float32)[:, None] * inv[None]
    cos = np.cos(ang).astype(np.float32).T
    sin = np.sin(ang).astype(np.float32).T
    return (np.ascontiguousarray(np.concatenate([cos, cos], 0)),
            np.ascontiguousarray(np.concatenate([sin, sin], 0)))


def build_nc(g128, gam):
    nc = bass.Bass("TRN2", target_bir_lowering=False)
    P = Prog(nc, same_engine_sync=SES)

    def din(name, shape, dt=F32):
        return nc.dram_tensor(name, list(shape), dt, kind="ExternalInput").ap()

    def dout(name, shape, dt=F32):
        return nc.dram_tensor(name, list(shape), dt, kind="ExternalOutput").ap()

    def MM(out, lhsT, rhs, start, stop, reads, writes):
        P.op("pe", lambda e: e.matmul(out, lhsT=lhsT, rhs=rhs, start=start, stop=stop), reads, writes)

    def TR(out, in_, ident, reads, writes):
        P.op("pe", lambda e: e.transpose(out, in_, ident), reads, writes)

    def ACT(out, in_, func, reads, writes, **kw):
        P.op("act", lambda e: e.activation(out=out, in_=in_, func=func, **kw), reads, writes)

    def DVE(name, reads, writes, **kw):
        P.op("dve", lambda e: getattr(e, name)(**kw), reads, writes)

    def DMA(q, out, in_, reads=(), writes=()):
        P.op(q, lambda e: e.dma_start(out=out, in_=in_), reads, writes, dma=True)

    evac_rr = [0]

    def COPY(out, in_, reads, writes, eng=None):
        if eng is None:
            eng = "act" if evac_rr[0] % 2 == 0 else "dve"
            evac_rr[0] += 1
        if eng == "act":
            ACT(out, in_, AF.Copy, reads, writes)
        else:
            DVE("tensor_copy", reads, writes, out=out, in_=in_)

    xm = din("xm", [1408, D])
    xprev = din("xprev", [NPREV * 128, D])
    pm = din("pm", [1042, 256])
    w_in = din("w_in", [INW // 128, 128, 32 * 128])
    w_out = din("w_out", [32, 128, 32 * 128])
    w_up = din("w_up", [2 * NFC, 128, 32 * 128])
    w_down = din("w_down", [8, 128, NFC * 512])
    w_gate = din("w_gate", [32, 128, 32 * 128])
    w_proj = din("w_proj", [32, 128, 2 * 128])
    ck = din("ck", [16, 128, 256])
    cv = din("cv", [16, 128, 256])
    sret = din("sret", [16, 8, 128, 256])
    sconv = din("sconv", [32, 2 * FF])
    cosP_d = din("cosP", [128, NPREV * 128])
    sinP_d = din("sinP", [128, NPREV * 128])
    cosG_d = [din("cosA", [128, 640]), din("cosB", [128, 528])]
    sinG_d = [din("sinA", [128, 640]), din("sinB", [128, 528])]
    escr = nc.dram_tensor("escr", [32, 384], F32, kind="Internal").ap()

    y_own = dout("y_own", [1024, D])
    y_smp = dout("y_smp", [16, D])
    o_kp = dout("o_kp", [128, 256])
    o_vp = dout("o_vp", [128, 256])
    o_rp = dout("o_rp", [8, 128, 256])
    o_cp = dout("o_cp", [2, 2 * FF])
    o_ks = dout("o_ks", [16, 128, 256])
    o_vs = dout("o_vs", [16, 128, 256])
    o_rs = dout("o_rs", [16, 8, 128, 256])
    o_cs = dout("o_cs", [16, 2, 2 * FF])
    t_outs = []

    def OUT(out, in_, reads):
        t = Tok()
        t_outs.append(t)
        DMA("sp", out, in_, reads=reads, writes=[t])

    consts = {}

    def const(name, shape, dt=F32, src=None):
        d = src if src is not None else din(name, shape)
        t = nc.alloc_sbuf_tensor("c_" + name, list(shape), dt)
        tok = Tok(name)
        DMA("pool" if dt != F32 else "sp", t[:], d, writes=[tok])
        consts[name] = (t, tok)
        return t, tok

    identf, t_identf = const("identf", [128, 128])
    identb, t_identb = const("identb", [128, 128], BF16, src=din("identf2", [128, 128]))
    jrev, t_jrev = const("jrev", [128, 128])
    bd64, t_bd64 = const("bd64", [128, 128], BF16)
    bd1, t_bd1 = const("bd1", [128, 128], BF16)
    ones4096, t_o4096 = const("ones4096", [128, 128], BF16)
    ones256, t_o256 = const("ones256", [128, 128], BF16)
    ones64, t_o64 = const("ones64", [128, 64], BF16)
    hvalid, t_hvalid = const("hvalid", [128, 64], BF16)
    g_mix, t_gmix = const("g_mix", [128, 32])
    g_ffn, t_gffn = const("g_ffn", [128, 32])
    g_ple, t_gple = const("g_ple", [128, 32])
    gq2, t_gq2 = const("gq2", [128, 1])
    gk2, t_gk2 = const("gk2", [128, 1])
    convw, t_convw = const("convw", [128, 3, 2 * NFC])
    convb, t_convb = const("convb", [128, 2 * NFC])
    dmaskT, t_dmask = const("dmaskT", [128, 8, 128])
    qdec, t_qdec = const("qdec", [128, 8, 128])
    kdec, t_kdec = const("kdec", [128, 8])
    kdecp, t_kdecp = const("kdecp", [128, NPREV, 8])
    relb, t_relb = const("relb", [32, 32])
    oh, t_oh = const("oh", [32, 384])
    ohs, t_ohs = const("ohs", [32, 128])
    sinkl, t_sinkl = const("sinkl", [128, 16])
    relb0l, t_relb0l = const("relb0l", [128, 16])

    NPS = 3
    PS = [nc.alloc_psum_tensor(f"ps{i}", [128, 1024], F32) for i in range(NPS)]
    PST = [Tok(f"ps{i}") for i in range(NPS)]
    PH = [nc.alloc_psum_tensor(f"ph{i}", [128, 512], F32) for i in range(2)]
    PHT = [Tok(f"ph{i}") for i in range(2)]
    ps_rr = [0]

    def next_ps():
        i = ps_rr[0] % NPS
        ps_rr[0] += 1
        return PS[i], PST[i]

    RA_BYTES = 50688
    RB_BYTES = 50688
    RS_BYTES = 24576
    Ra = nc.alloc_sbuf_tensor("Ra", [128, RA_BYTES], U8)
    Rb = nc.alloc_sbuf_tensor("Rb", [128, RB_BYTES], U8)
    Rs = nc.alloc_sbuf_tensor("Rs", [128, RS_BYTES], U8)

    def view(arena, off, shape, dt):
        esz = 4 if dt == F32 else 2
        n = 1
        for s in shape[1:]:
            n *= s
        v = arena[:, off:off + n * esz].bitcast(dt)
        if len(shape) == 3:
            v = v.rearrange("p (a b) -> p a b", a=shape[1])
        elif len(shape) == 4:
            v = v.rearrange("p (a b c) -> p a b c", a=shape[1], b=shape[2])
        return v

    NWB = 3
    WB = [nc.alloc_sbuf_tensor(f"wb{i}", [128, 32, 128], BF16) for i in range(NWB)]
    WBT = [Tok(f"wb{i}") for i in range(NWB)]
    wb_rr = [0]

    def next_wb():
        i = wb_rr[0] % NWB
        wb_rr[0] += 1
        return WB[i], WBT[i]

    w_in_v, w_out_v, w_up_v, w_down_v, w_gate_v, w_proj_v = w_in, w_out, w_up, w_down, w_gate, w_proj

    def load_w(wv, col0, ncols=128, nk=32, k0=0, dup64=False):
        wb, wt = next_wb()
        b, off = col0 // 128, col0 % 128
        if dup64:
            src = wv[b].rearrange("p (k c) -> p k c", c=128)[:, 0:nk, off:off + 64]
            DMA("pool", wb[:, 0:nk, 0:64], src, writes=[wt])
            DMA("pool", wb[:, 0:nk, 64:128], src, writes=[wt])
        else:
            assert off == 0 and ncols == 128 and k0 == 0
            DMA("pool", wb[:].rearrange("p k c -> p (k c)")[:, 0:nk * 128], wv[b][:, 0:nk * 128], writes=[wt])
        return wb, wt

    kT_dup = nc.alloc_sbuf_tensor("kT_dup", [128, 4, 1296], BF16)
    t_kT = [Tok(f"kT{g}") for g in range(4)]
    V_tm = nc.alloc_sbuf_tensor("V_tm", [128, 10, 256], BF16)
    t_V = Tok("V_tm")
    S = nc.alloc_sbuf_tensor("S", [128, 8, 256], F32)
    S_bf = nc.alloc_sbuf_tensor("S_bf", [128, 8, 256], BF16)
    t_S = [Tok(f"S{h}") for h in range(8)]
    t_Sbf = [Tok(f"Sbf{h}") for h in range(8)]
    cosT = nc.alloc_sbuf_tensor("cosT", [128, 640], F32)
    sinT = nc.alloc_sbuf_tensor("sinT", [128, 640], F32)
    t_cs = Tok("cossin")
    Eg = nc.alloc_sbuf_tensor("Eg", [128, 8, 2, 128], BF16)
    t_Eg = Tok("Eg")
    Es = nc.alloc_sbuf_tensor("Es", [128, 32], F32)
    t_Es = Tok("Es")
    esl = nc.alloc_sbuf_tensor("esl", [128, 16], F32)
    t_esl = Tok("esl")
    e0l = nc.alloc_sbuf_tensor("e0l", [128, 16], F32)
    t_e0l = Tok("e0l")
    expb = nc.alloc_sbuf_tensor("expb", [32, 32], F32)
    t_expb = Tok("expb")
    carry = nc.alloc_sbuf_tensor("carry", [128, 2 * NFC, 2], F32)
    t_carry = Tok("carry")
    t_kp = Tok("kp_tm")
    t_ksn = Tok("ksn")
    t_vp = Tok("vp_tm")
    t_vsn = Tok("vsn")

    ACT(expb[:], relb[:], AF.Exp, [t_relb], [t_expb])
    ACT(esl[:], sinkl[:], AF.Exp, [t_sinkl], [t_esl])
    ACT(e0l[:], relb0l[:], AF.Exp, [t_relb0l], [t_e0l])
    ps, pt = next_ps()
    MM(ps[0:32, 0:384], expb[:], oh[:], True, True, [t_expb, t_oh], [pt])
    u_sb0 = view(Rs, 0, [128, 384], F32)
    t_u0 = Tok("u0")
    COPY(u_sb0[0:32, :], ps[0:32, 0:384], [pt], [t_u0], eng="dve")
    t_escr = Tok("escr")
    DMA("sp", escr, u_sb0[0:32, :], reads=[t_u0], writes=[t_escr])
    ps, pt = next_ps()
    MM(ps[:, 0:32], ohs[:], expb[:], True, True, [t_ohs, t_expb], [pt])
    COPY(Es[:], ps[:, 0:32], [pt], [t_Es], eng="dve")

    def RSTD(out, in_, reads, tok):
        ACT(out, in_, AF.Sqrt, reads, [tok], bias=EPS, scale=1.0)
        DVE("reciprocal", [tok], [tok], out=out, in_=out)

    def load_xT(src, n, x_tm, t_xtm, dstT, t_dst):
        DMA("sp", x_tm[0:n, :], src, writes=[t_xtm])
        for c4 in range(8):
            ps, pt = next_ps()
            for i in range(4):
                k = c4 * 4 + i
                TR(ps[:, i * 128:i * 128 + n], x_tm[0:n, k * 128:(k + 1) * 128], identf[0:n, 0:n],
                   [t_xtm, t_identf], [pt])
            COPY(dstT[:, c4 * 4:(c4 + 1) * 4, 0:n],
                 ps[:, 0:512].rearrange("p (c t) -> p c t", c=4)[:, :, 0:n], [pt], [t_dst])

    def rmsnorm_fm(xT_of, t_x, n, gvec, t_g, out_of, t_out, sq, t_sq, rstd, t_rstd, x3d=None, out3d=None,
                   parts=None):
        if x3d is not None:
            ACT(sq[:, :, 0:n], x3d, AF.Square, [t_x], [t_sq])
            ps, pt = next_ps()
            for k in range(32):
                MM(ps[:, 0:n], ones4096[:], sq[:, k, 0:n], k == 0, k == 31, [t_sq, t_o4096], [pt])
            RSTD(rstd[:, 0:n], ps[:, 0:n], [pt], t_rstd)
            DVE("tensor_tensor", [t_x, t_rstd], [t_x], out=x3d, in0=x3d,
                in1=rstd[:, 0:n].unsqueeze(1).to_broadcast([128, 32, n]), op=ALU.mult)
            DVE("tensor_tensor", [t_x, t_g], [t_out], out=out3d, in0=x3d,
                in1=gvec[:, 0:32].unsqueeze(2).to_broadcast([128, 32, n]), op=ALU.mult)
            return
        for c0 in range(0, n, 128):
            m = min(128, n - c0)
            if parts is not None:
                for (k0, k1, ap3) in parts:
                    ACT(sq[:, k0:k1, 0:m], ap3[:, :, c0:c0 + m], AF.Square, [t_x], [t_sq])
            else:
                for k in range(32):
                    ACT(sq[:, k, 0:m], xT_of(k)[:, c0:c0 + m], AF.Square, [t_x], [t_sq])
            ps, pt = next_ps()
            for k in range(32):
                MM(ps[:, 0:m], ones4096[:], sq[:, k, 0:m], k == 0, k == 31, [t_sq, t_o4096], [pt])
            RSTD(rstd[:, c0:c0 + m], ps[:, 0:m], [pt], t_rstd)
        for k in range(32):
            DVE("scalar_tensor_tensor", [t_x, t_rstd, t_g], [t_out], out=out_of(k), in0=xT_of(k)[:, 0:n],
                scalar=gvec[:, k:k + 1], in1=rstd[:, 0:n], op0=ALU.mult, op1=ALU.mult)

    def proj_fm(wb, wt, nk, rhs_of, t_rhs, lo, hi):
        ps, pt = next_ps()
        pcs = pieces(hi - lo)
        for k in range(nk):
            for pi, (a, b) in enumerate(pcs):
                MM(ps[:, pi * 512:pi * 512 + (b - a)], wb[:, k, :], rhs_of(k)[:, lo + a:lo + b],
                   k == 0, k == nk - 1, [wt, t_rhs], [pt])
        return ps, pt, pcs

    def rotary(ps, pt, pcs, cos, sin, t_tab, tc0, out, t_out, X, B, t_X, t_B):
        n = pcs[-1][1]
        for pi, (a, b) in enumerate(pcs):
            ACT(X[:, a:b], ps[:, pi * 512:pi * 512 + (b - a)], AF.Copy, [pt], [t_X])
        DVE("tensor_tensor", [t_X, t_tab], [t_B], out=B[0:64, 0:n], in0=X[64:128, 0:n],
            in1=sin[64:128, tc0:tc0 + n], op=ALU.mult)
        DVE("tensor_tensor", [t_X, t_tab], [t_B], out=B[64:128, 0:n], in0=X[0:64, 0:n],
            in1=sin[0:64, tc0:tc0 + n], op=ALU.mult)
        DVE("tensor_tensor", [t_X, t_tab], [t_X], out=X[:, 0:n], in0=X[:, 0:n],
            in1=cos[:, tc0:tc0 + n], op=ALU.mult)
        DVE("tensor_tensor", [t_X, t_B], [t_out], out=out[0:64, 0:n], in0=X[0:64, 0:n],
            in1=B[0:64, 0:n], op=ALU.subtract)
        DVE("tensor_tensor", [t_X, t_B], [t_out], out=out[64:128, 0:n], in0=X[64:128, 0:n],
            in1=B[64:128, 0:n], op=ALU.add)

    P.barrier()
    hTp = view(Rb, 0, [128, 32, 768], BF16)
    t_hTp = Tok("hTp")
    x_tm = view(Ra, 0, [128, 4096], F32)
    t_xtm = Tok("x_tm")
    xT_tmp = view(Ra, 16384, [128, 32, 128], F32)
    t_xTt = Tok("xT_tmp")
    cosP = view(Ra, 32768, [128, 768], F32)
    sinP = view(Ra, 32768 + 3072, [128, 768], F32)
    t_csP = Tok("csP")
    Xr = view(Ra, 32768 + 6144, [128, 768], F32)
    Br = view(Ra, 32768 + 9216, [128, 768], F32)
    t_Xr, t_Br = Tok("Xr"), Tok("Br")
    sq = view(Rs, 0, [128, 32, 128], BF16)
    t_sq = Tok("sq")
    rstd = view(Rs, 8192, [128, 528], F32)
    t_rstd = Tok("rstd")
    rkT = view(Rs, 10304, [128, 768], BF16)
    rvT = view(Rs, 11840, [128, 2, 768], BF16)
    t_rkT, t_rvT = Tok("rkT"), Tok("rvT")
    ktm = [view(Rs, 14912 + i * 256, [128, 128], BF16) for i in range(2)]
    vtm = [view(Rs, 15424 + i * 512, [128, 256], BF16) for i in range(2)]
    t_ktm = [Tok(), Tok()]
    t_vtm = [Tok(), Tok()]
    Sps_rr = [0]

    tile0 = 0
    for pi_, ntile in enumerate(PREV_PASS):
        ncolp = ntile * 128
        DMA("sp", cosP[:, 0:ncolp], cosP_d[:, tile0 * 128:tile0 * 128 + ncolp], writes=[t_csP])
        DMA("sp", sinP[:, 0:ncolp], sinP_d[:, tile0 * 128:tile0 * 128 + ncolp], writes=[t_csP])
        for ti in range(ntile):
            r0 = (tile0 + ti) * 128
            load_xT(xprev[r0:r0 + 128, :], 128, x_tm, t_xtm, xT_tmp, t_xTt)
            rmsnorm_fm(lambda k: xT_tmp[:, k, :], t_xTt, 128, g_mix, t_gmix,
                       lambda k, ti=ti: hTp[:, k, ti * 128:(ti + 1) * 128], t_hTp, sq, t_sq, rstd, t_rstd,
                       x3d=xT_tmp[:, :, :], out3d=hTp[:, :, ti * 128:(ti + 1) * 128])
        for h in range(8):
            wb, wt = load_w(w_in_v, RK0 + h * 128)
            ps, pt, pcs = proj_fm(wb, wt, 32, lambda k: hTp[:, k, :], t_hTp, 0, ncolp)
            rotary(ps, pt, pcs, cosP, sinP, t_csP, 0, rkT, t_rkT, Xr, Br, t_Xr, t_Br)
            for dvc in range(2):
                wb, wt = load_w(w_in_v, RV0 + h * 256 + dvc * 128)
                ps, pt, pcs = proj_fm(wb, wt, 32, lambda k: hTp[:, k, :], t_hTp, 0, ncolp)
                for pj, (a, b) in enumerate(pcs):
                    COPY(rvT[:, dvc, a:b], ps[:, pj * 512:pj * 512 + (b - a)], [pt], [t_rvT])
            psS, ptS = PH[h % 2], PHT[h % 2]
            for ti in range(ntile):
                bi = ti % 2
                ps, pt = next_ps()
                psb = ps[:, 0:512].bitcast(BF16)
                TR(psb[:, 0:128], rkT[:, ti * 128:(ti + 1) * 128], identb[:], [t_rkT, t_identb], [pt])
                ACT(ktm[bi][:], psb[:, 0:128], AF.Copy, [pt, t_kdecp], [t_ktm[bi]],
                    scale=kdecp[:, tile0 + ti, h:h + 1])
                for dvc in range(2):
                    TR(psb[:, 128 + dvc * 128:256 + dvc * 128], rvT[:, dvc, ti * 128:(ti + 1) * 128], identb[:],
                       [t_rvT, t_identb], [pt])
                COPY(vtm[bi][:], psb[:, 128:384], [pt], [t_vtm[bi]], eng="act")
                MM(psS[:, 0:256], ktm[bi][:], vtm[bi][:], ti == 0, ti == ntile - 1,
                   [t_ktm[bi], t_vtm[bi]], [ptS])
            if pi_ == 0:
                COPY(S[:, h, :], psS[:, 0:256], [ptS], [t_S[h]], eng="dve")
            else:
                DVE("tensor_tensor", [ptS, t_S[h]], [t_S[h]], out=S[:, h, :], in0=S[:, h, :],
                    in1=psS[:, 0:256], op=ALU.add)
        tile0 += ntile
    for h in range(8):
        ACT(S_bf[:, h, :], S[:, h, :], AF.Copy, [t_S[h]], [t_Sbf[h]])

    if STAGE <= 2:
        OUT(o_rp.rearrange("h k v -> k h v"), S[:], t_S)
        P.op("sp", lambda e: e.nop(), reads=t_outs)
        P.emit()
        return nc, P


    t_qns = Tok("qn_s")
    t_vnT = Tok("vnT")
    ucp = [nc.alloc_sbuf_tensor(f"ucp{i}", [18, 512], F32) for i in range(2)]
    t_ucp = [Tok("ucp0"), Tok("ucp1")]
    dbg_mix = [dout(f"dbg_mix{gi}", [128, 32, 528], BF16) for gi in range(2)] if os.environ.get("MK_DBG") else None

    def xT_of(k):
        if k < 24:
            return view(Rb, k * 2112, [128, 528], F32)
        return view(Ra, 33792 + (k - 24) * 2112, [128, 528], F32)
    t_xT = Tok("xT")
    xT_parts = [(0, 24, view(Rb, 0, [128, 24, 528], F32)), (24, 32, view(Ra, 33792, [128, 8, 528], F32))]

    def chk(code, cond=True):
        if cond and STAGE <= code:
            raise EarlyExit()

    def _groups():
      for gi, G in enumerate(GROUPS):
          ncol, q0, f0, nsmp, row0 = G["ncol"], G["q0"], G["f0"], G["nsmp"], G["row0"]
          nq = ncol - q0
          nf = ncol - f0
          gcol0 = row0
          blocks = G["blocks"]
          P.barrier()
          hT = view(Rb, 0, [128, 32, 768], BF16)
          t_hT = Tok("hT")
          x_tm2 = [view(Ra, 0, [128, 4096], F32), view(Ra, 16384, [128, 4096], F32)]
          t_xtm2 = [Tok(), Tok()]
          xT_tmp = view(Ra, 32768, [128, 32, 128], F32)
          t_xTt = Tok()
          sq = view(Rs, 0, [128, 32, 128], BF16)
          t_sq = Tok()
          rstd = view(Rs, 8192, [128, 528], F32)
          t_rstd = Tok()
          DMA("sp", cosT[:, 0:nq], cosG_d[gi], writes=[t_cs])
          DMA("sp", sinT[:, 0:nq], sinG_d[gi], writes=[t_cs])
          ntile = (ncol + 127) // 128
          if os.environ.get('MK_SKIP16'):
              ntile = ncol // 128
          for ti in range(ntile):
              n = 128
              load_xT(xm[row0 + ti * 128:row0 + ti * 128 + n, :], n, x_tm2[ti % 2], t_xtm2[ti % 2], xT_tmp, t_xTt)
              rmsnorm_fm(lambda k: xT_tmp[:, k, :], t_xTt, n, g_mix, t_gmix,
                         lambda k, ti=ti, n=n: hT[:, k, ti * 128:ti * 128 + n], t_hT, sq, t_sq, rstd, t_rstd,
                         x3d=xT_tmp[:, :, :], out3d=hT[:, :, ti * 128:ti * 128 + n])
          if STAGE <= 6.5 and gi == 1:
              raise EarlyExit()
          P.barrier()
          mixT = view(Ra, 0, [128, 32, 528], BF16)
          t_mix = Tok("mixT")
          kp_tm = view(Ra, 33792, [128, 256], F32)
          ksn_tm = view(Ra, 34816, [128, 256], F32)
          vp_tm = view(Ra, 35840, [128, 256], F32)
          vsn_tm = view(Ra, 36864, [128, 256], F32)
          qn_s = view(Ra, 37888, [128, 16, 16], BF16)
          vnT = view(Ra, 38400, [128, 4, 16], F32)
          hT_of = lambda k: hT[:, k, :]

          tm_blocks = [(bi * 128, 128, gb) for bi, gb in enumerate(blocks)]
          if nsmp and "tmv16" not in SKIP:
              tm_blocks.append((512, 16, None))
          for half in range(2):
              wb, wt = load_w(w_in_v, AV0 + half * 128)
              for (c0, n, gb) in tm_blocks:
                  ps, pt = next_ps()
                  for k in range(32):
                      MM(ps[0:n, 0:128], hT[:, k, c0:c0 + n], wb[:, k, :], k == 0, k == 31, [t_hT, wt], [pt])
                  if gb is not None:
                      ce = "act" if gb % 2 == 0 else "dve"
                      COPY(V_tm[0:n, gb, half * 128:(half + 1) * 128], ps[0:n, 0:128], [pt], [t_V], eng=ce)
                      if gb == 9:
                          COPY(vp_tm[:, half * 128:(half + 1) * 128], ps[:, 0:128], [pt], [t_vp], eng=ce)
                  else:
                      COPY(vsn_tm[0:16, half * 128:(half + 1) * 128], ps[0:16, 0:128], [pt], [t_vsn])

          chk(6.51, gi == 1)
          qn = view(Rs, 0, [128, 4, 640], BF16)
          t_qn = Tok("qn")
          sqn = view(Rs, 5120, [128, 768], BF16)
          t_sqn = Tok()
          rst2 = view(Rs, 6656, [128, 768], F32)
          t_rst2 = Tok()
          ex = [view(Rs, 9728 + i * 2048, [128, 512], F32) for i in range(2)]
          t_ex = [Tok(), Tok()]
          PT = view(Rs, 13824, [128, 2, 2, 512], BF16)
          t_PT = Tok("PT")
          rden = view(Rs, 17920, [128, 512], F32)
          t_rden = Tok()
          Hk = [view(Rs, 19968 + i * 1024, [128, 256], F32) for i in range(2)]
          t_Hk = [Tok(), Tok()]
          knf = view(Rs, 22016, [128, 144], F32)
          t_knf = Tok()

          def norm_qk(ps, pt, pcs, gvec, t_gv, out, t_out, extra=None):
              for pi, (a, b) in enumerate(pcs):
                  ACT(sqn[:, a:b], ps[:, pi * 512:pi * 512 + (b - a)], AF.Square, [pt], [t_sqn])
              ps2, pt2 = next_ps()
              for pi, (a, b) in enumerate(pcs):
                  MM(ps2[:, pi * 512:pi * 512 + (b - a)], bd64[:], sqn[:, a:b], True, True, [t_sqn, t_bd64], [pt2])
              for pi, (a, b) in enumerate(pcs):
                  RSTD(rst2[:, a:b], ps2[:, pi * 512:pi * 512 + (b - a)], [pt2], t_rst2)
              for pi, (a, b) in enumerate(pcs):
                  DVE("scalar_tensor_tensor", [pt, t_rst2, t_gv], [t_out], out=out[:, a:b],
                      in0=ps[:, pi * 512:pi * 512 + (b - a)], scalar=gvec[:, 0:1], in1=rst2[:, a:b],
                      op0=ALU.mult, op1=ALU.mult)
                  if extra is not None:
                      for (ea, eb, dst, t_dst) in extra:
                          lo, hi = max(a, ea), min(b, eb)
                          if lo < hi:
                              DVE("scalar_tensor_tensor", [pt, t_rst2, t_gv], [t_dst], out=dst[:, lo - ea:hi - ea],
                                  in0=ps[:, pi * 512 + lo - a:pi * 512 + hi - a], scalar=gvec[:, 0:1],
                                  in1=rst2[:, lo:hi], op0=ALU.mult, op1=ALU.mult)

          for g in range(4):
              wb, wt = load_w(w_in_v, AK0 + g * 64, dup64=True)
              ps, pt, pcs = proj_fm(wb, wt, 32, hT_of, t_hT, 0, ncol)
              extra = None
              if gi == 1:
                  extra = [(384, 528, knf, t_knf)]
              norm_qk(ps, pt, pcs, gk2, t_gk2, kT_dup[:, g, gcol0:gcol0 + ncol], t_kT[g], extra)
              if gi == 1 and "knf" not in SKIP:
                  ps, pt = next_ps()
                  TR(ps[:, 0:64], knf[0:64, 0:128], identf[0:64, 0:64], [t_knf, t_identf], [pt])
                  TR(ps[0:16, 64:128], knf[0:64, 128:144], identf[0:64, 0:64], [t_knf, t_identf], [pt])
                  COPY(kp_tm[:, g * 64:(g + 1) * 64], ps[:, 0:64], [pt], [t_kp], eng="act")
                  COPY(ksn_tm[0:16, g * 64:(g + 1) * 64], ps[0:16, 64:128], [pt], [t_ksn], eng="act")
              chk(6.52, gi == 1)
              for j in range(4):
                  wb, wt = load_w(w_in_v, AQ0 + (4 * g + j) * 128)
                  ps, pt, pcs = proj_fm(wb, wt, 32, hT_of, t_hT, q0, ncol)
                  norm_qk(ps, pt, pcs, gq2, t_gq2, qn[:, j, 0:nq], t_qn)
                  if nsmp and "qns" not in SKIP:
                      DVE("tensor_copy", [t_qn], [t_qns], out=qn_s[:, 4 * g + j, :], in_=qn[:, j, 512:528])
              if nsmp and "vnT" not in SKIP:
                  wb, wt = load_w(w_in_v, AV0 + g * 64, dup64=True)
                  ps, pt = next_ps()
                  for k in range(32):
                      MM(ps[:, 0:16], wb[:, k, :], hT[:, k, 512:528], k == 0, k == 31, [wt, t_hT], [pt])
                  COPY(vnT[:, g, :], ps[:, 0:16], [pt], [t_vnT])
              chk(6.53, gi == 1)
              for hh in range(8):
                  head = 8 * g + hh
                  hk, thk = Hk[hh % 2], t_Hk[hh % 2]
                  src = bass.AP(escr.tensor, head * 384, [[1, 128], [128, 2], [1, 128]])
                  DMA("sp", hk[:].rearrange("p (a b) -> p a b", a=2), src, reads=[t_escr], writes=[thk])
                  ps, pt = next_ps()
                  MM(ps[:, 0:256], jrev[:], hk[:], True, True, [thk, t_jrev], [pt])
                  ce = "act" if hh % 2 == 0 else "dve"
                  COPY(Eg[:, hh, 1, :], ps[:, 0:128], [pt], [t_Eg], eng=ce)
                  COPY(Eg[:, hh, 0, :], ps[:, 128:256], [pt], [t_Eg], eng=ce)
              chk(6.54, gi == 1)
              for bi, gb in enumerate(blocks):
                  c0 = bi * 128
                  if c0 < q0:
                      continue
                  qi0 = c0 - q0
                  kcols = (gcol0 + c0 - 128, gcol0 + c0)
                  ei = 0
                  for kbi in range(2):
                      for half in range(2):
                          ps, pt = next_ps()
                          MM(ps[:, 0:512].rearrange("p (j q) -> p j q", j=4),
                             kT_dup[half * 64:(half + 1) * 64, g, kcols[kbi]:kcols[kbi] + 128],
                             qn[half * 64:(half + 1) * 64, :, qi0:qi0 + 128], True, True, [t_kT[g], t_qn], [pt])
                          e_, te_ = ex[ei % 2], t_ex[ei % 2]
                          ei += 1
                          ACT(e_[:], ps[:, 0:512], AF.Exp, [pt], [te_], scale=0.125)
                          DVE("tensor_tensor", [te_, t_Eg], [t_PT],
                              out=PT[:, kbi, half, :].rearrange("p (j q) -> p j q", j=4),
                              in0=e_[:].rearrange("p (j q) -> p j q", j=4),
                              in1=Eg[:, half:8:2, kbi, :], op=ALU.mult)
                  ps_o, pt_o = next_ps()
                  for half in range(2):
                      for kbi in range(2):
                          vgb = gb - 1 + kbi
                          vo, tvo = (hvalid, t_hvalid) if vgb in (0, 1) else (ones64, t_o64)
                          MM(ps_o[half * 64:(half + 1) * 64, 0:512], V_tm[:, vgb, g * 64:(g + 1) * 64],
                             PT[:, kbi, half, :], kbi == 0, kbi == 1, [t_V, t_PT], [pt_o])
                      for kbi in range(2):
                          vgb = gb - 1 + kbi
                          vo, tvo = (hvalid, t_hvalid) if vgb in (0, 1) else (ones64, t_o64)
                          MM(ps_o[half * 64:(half + 1) * 64, 512:1024], vo[:],
                             PT[:, kbi, half, :], kbi == 0, kbi == 1, [tvo, t_PT], [pt_o])
                  for j in range(4):
                      DVE("tensor_scalar", [pt_o, t_esl], [t_rden], out=rden[:, j * 128:(j + 1) * 128],
                          in0=ps_o[:, 512 + j * 128:512 + (j + 1) * 128], scalar1=esl[:, 4 * g + j:4 * g + j + 1],
                          scalar2=None, op0=ALU.add)
                  DVE("reciprocal", [t_rden], [t_rden], out=rden[:], in_=rden[:])
                  q_lo = 126 if gb == 1 else 0
                  mc0 = c0 + q_lo - f0
                  DVE("tensor_tensor", [pt_o, t_rden], [t_mix],
                      out=mixT[:, 4 * g:4 * g + 4, mc0:mc0 + 128 - q_lo],
                      in0=ps_o[:, 0:512].rearrange("p (j q) -> p j q", j=4)[:, :, q_lo:128],
                      in1=rden[:].rearrange("p (j q) -> p j q", j=4)[:, :, q_lo:128], op=ALU.mult)

          if STAGE <= 6.6 and gi == 1:
              raise EarlyExit()
          if nsmp:
              P.barrier()
              kdup_s = [view(Rs, 0 + i * 1024, [128, 4, 2, 64], BF16) for i in range(2)]
              vc_s = [view(Rs, 2048 + i * 512, [128, 256], BF16) for i in range(2)]
              t_kds = [Tok(), Tok()]
              t_vcs = [Tok(), Tok()]
              KTs = view(Rs, 3072, [128, 4, 128], BF16)
              t_KTs = Tok()
              ex_s = view(Rs, 4096, [128, 32], F32)
              t_exs = Tok()
              PTs = view(Rs, 4224, [128, 32], BF16)
              t_PTs = Tok()
              prod = view(Rs, 4352, [128, 16, 16], BF16)
              t_prod = Tok()
              pnew = view(Rs, 4864, [128, 16, 16], F32)
              t_pnew = Tok()
              vn16 = view(Rs, 5888, [128, 16, 16], F32)
              t_vn16 = Tok()
              sm_a = view(Rs, 6912, [128, 16], F32)
              sm_b = view(Rs, 6976, [128, 16], F32)
              t_sma, t_smb = Tok(), Tok()
              for c in range(16):
                  g = c // 4
                  DVE("tensor_tensor", [t_qns, t_kT[g]], [t_prod], out=prod[:, c, :], in0=qn_s[:, c, :],
                      in1=kT_dup[:, g, gcol0 + 512:gcol0 + 528], op=ALU.mult)
                  DVE("tensor_copy", [t_vnT], [t_vn16], out=vn16[:, c, :], in_=vnT[:, g, :])
              ps, pt = next_ps()
              MM(ps[:, 0:256], bd1[:], prod[:].rearrange("p a b -> p (a b)"), True, True, [t_prod, t_bd1], [pt])
              ACT(pnew[:].rearrange("p a b -> p (a b)"), ps[:, 0:256], AF.Exp, [pt], [t_pnew], scale=0.125)
              DVE("tensor_tensor", [t_pnew, t_e0l], [t_pnew], out=pnew[:], in0=pnew[:],
                  in1=e0l[:].unsqueeze(2).to_broadcast([128, 16, 16]), op=ALU.mult)
              DVE("tensor_tensor", [t_pnew, t_vn16], [t_vn16], out=vn16[:], in0=vn16[:], in1=pnew[:], op=ALU.mult)
              for s_ in range(16):
                  bi_ = s_ % 2
                  DMA("pool", kdup_s[bi_][:, :, 0, :], ck[s_].rearrange("k (g d) -> k g d", g=4), writes=[t_kds[bi_]])
                  DMA("pool", kdup_s[bi_][:, :, 1, :], ck[s_].rearrange("k (g d) -> k g d", g=4), writes=[t_kds[bi_]])
                  DMA("pool", vc_s[bi_][:], cv[s_], writes=[t_vcs[bi_]])
                  ps, pt = next_ps()
                  psb = ps[:, 0:512].bitcast(BF16)
                  for g in range(4):
                      TR(psb[:, g * 128:(g + 1) * 128], kdup_s[bi_][:, g, :, :].rearrange("p a b -> p (a b)"),
                         identb[:], [t_kds[bi_], t_identb], [pt])
                  COPY(KTs[:].rearrange("p a b -> p (a b)"), psb[:, 0:512], [pt], [t_KTs])
                  ps_sc, pt_sc = next_ps()
                  for half in range(2):
                      for g in range(4):
                          MM(ps_sc[:, half * 512 + 4 * g:half * 512 + 4 * g + 4], KTs[half * 64:(half + 1) * 64, g, :],
                             qn_s[half * 64:(half + 1) * 64, 4 * g:4 * g + 4, s_], True, True, [t_KTs, t_qns], [pt_sc])
                  for half in range(2):
                      ACT(ex_s[:, half * 16:(half + 1) * 16], ps_sc[:, half * 512:half * 512 + 16], AF.Exp,
                          [pt_sc], [t_exs], scale=0.125)
                  DVE("tensor_tensor", [t_exs, t_Es], [t_PTs], out=PTs[:].rearrange("p (h c) -> p h c", h=2),
                      in0=ex_s[:].rearrange("p (h c) -> p h c", h=2),
                      in1=Es[:].rearrange("p (c h) -> p h c", h=2), op=ALU.mult)
                  ps_os, pt_os = next_ps()
                  for g in range(4):
                      for half in range(2):
                          MM(ps_os[half * 64:(half + 1) * 64, 4 * g:4 * g + 4], vc_s[bi_][:, g * 64:(g + 1) * 64],
                             PTs[:, half * 16 + 4 * g:half * 16 + 4 * g + 4], True, True, [t_vcs[bi_], t_PTs], [pt_os])
                          MM(ps_os[half * 64:(half + 1) * 64, 16 + 4 * g:16 + 4 * g + 4], ones64[:],
                             PTs[:, half * 16 + 4 * g:half * 16 + 4 * g + 4], True, True, [t_o64, t_PTs], [pt_os])
                  DVE("tensor_tensor", [pt_os, t_vn16], [t_sma], out=sm_a[:], in0=ps_os[:, 0:16], in1=vn16[:, :, s_], op=ALU.add)
                  DVE("tensor_tensor", [pt_os, t_pnew], [t_smb], out=sm_b[:], in0=ps_os[:, 16:32], in1=pnew[:, :, s_], op=ALU.add)
                  DVE("tensor_tensor", [t_smb, t_esl], [t_smb], out=sm_b[:], in0=sm_b[:], in1=esl[:], op=ALU.add)
                  DVE("reciprocal", [t_smb], [t_smb], out=sm_b[:], in_=sm_b[:])
                  DVE("tensor_tensor", [t_sma, t_smb], [t_mix], out=mixT[:, 0:16, 512 + s_], in0=sm_a[:], in1=sm_b[:], op=ALU.mult)
              OUT(o_ks[:, 0:127, :], ck[:, 1:128, :], [])
              OUT(o_vs[:, 0:127, :], cv[:, 1:128, :], [])
              OUT(o_ks[:, 127, :], ksn_tm[0:16, :], [t_ksn])
              OUT(o_vs[:, 127, :], vsn_tm[0:16, :], [t_vsn])
              OUT(o_kp, kp_tm[:], [t_kp])
              OUT(o_vp, vp_tm[:], [t_vp])
          P.barrier()

          if STAGE <= 6.7 and gi == 1:
              raise EarlyExit()
          rq = view(Rs, 0, [128, 640], BF16)
          rk = view(Rs, 1280, [128, 640], BF16)
          t_rq, t_rk = Tok(), Tok()
          rv_tm = view(Rs, 2560, [128, 6, 256], BF16)
          t_rv = Tok()
          sg = view(Rs, 5632, [128, 2, 640], BF16)
          t_sg = Tok()
          Xr = view(Rs, 8192, [128, 640], F32)
          Br = view(Rs, 10752, [128, 640], F32)
          t_Xr, t_Br = Tok(), Tok()
          AT = [view(Rs, 13312 + i * 256, [128, 128], BF16) for i in range(2)]
          qd = [view(Rs, 13824 + i * 256, [128, 128], BF16) for i in range(2)]
          ktm = [view(Rs, 14336 + i * 256, [128, 128], BF16) for i in range(2)]
          t_AT, t_qd, t_ktm = [Tok(), Tok()], [Tok(), Tok()], [Tok(), Tok()]
          sqo = view(Rs, 14848, [128, 256], BF16)
          t_sqo = Tok()
          rstd_o = view(Rs, 15360, [128, 128], F32)
          t_rso = Tok()
          to_ = view(Rs, 15872, [128, 256], F32)
          t_to = Tok()
          s0b = [view(Rs, 16896 + i * 1024, [128, 256], F32) for i in range(2)]
          t_s0 = [Tok(), Tok()]
          s1bf = [view(Rs, 18944 + i * 512, [128, 256], BF16) for i in range(2)]
          t_s1bf = [Tok(), Tok()]
          kmask = view(Rs, 19968, [128, 16, 128], BF16)
          t_kmask = Tok()
          ktm_s = view(Rs, 24064, [128, 128], BF16)
          t_ktms = Tok()

          qblocks = [(bi * 128, 128, gb) for bi, gb in enumerate(blocks) if bi * 128 >= q0]
          rblocks = list(qblocks)
          if nsmp:
              rblocks.append((512, 16, None))
          for h in range(8):
              wb, wt = load_w(w_in_v, RQ0 + h * 128)
              ps, pt, pcs = proj_fm(wb, wt, 32, hT_of, t_hT, q0, ncol)
              rotary(ps, pt, pcs, cosT, sinT, t_cs, 0, rq, t_rq, Xr, Br, t_Xr, t_Br)
              wb, wt = load_w(w_in_v, RK0 + h * 128)
              ps, pt, pcs = proj_fm(wb, wt, 32, hT_of, t_hT, q0, ncol)
              rotary(ps, pt, pcs, cosT, sinT, t_cs, 0, rk, t_rk, Xr, Br, t_Xr, t_Br)
              for dvc in range(2):
                  wb, wt = load_w(w_in_v, RV0 + h * 256 + dvc * 128)
                  for ri, (c0, n, gb) in enumerate(rblocks):
                      ps, pt = next_ps()
                      for k in range(32):
                          MM(ps[0:n, 0:128], hT[:, k, c0:c0 + n], wb[:, k, :], k == 0, k == 31, [t_hT, wt], [pt])
                      COPY(rv_tm[0:n, ri, dvc * 128:(dvc + 1) * 128], ps[0:n, 0:128], [pt], [t_rv])
              for dvc in range(2):
                  wb, wt = load_w(w_in_v, RG0 + h * 256 + dvc * 128)
                  ps, pt, pcs = proj_fm(wb, wt, 32, hT_of, t_hT, q0, ncol)
                  for pi, (a, b) in enumerate(pcs):
                      ACT(sg[:, dvc, a:b], ps[:, pi * 512:pi * 512 + (b - a)], AF.Silu, [pt], [t_sg])
              for ri, (c0, n, gb) in enumerate(qblocks):
                  qi0 = c0 - q0
                  bi_ = ri % 2
                  ps1, pt1 = next_ps()
                  MM(ps1[:, 0:128], rk[:, qi0:qi0 + 128], rq[:, qi0:qi0 + 128], True, True, [t_rk, t_rq], [pt1])
                  DVE("tensor_tensor", [pt1, t_dmask], [t_AT[bi_]], out=AT[bi_][:], in0=ps1[:, 0:128],
                      in1=dmaskT[:, h, :], op=ALU.mult)
                  DVE("tensor_tensor", [t_rq, t_qdec], [t_qd[bi_]], out=qd[bi_][:], in0=rq[:, qi0:qi0 + 128],
                      in1=qdec[:, h, :], op=ALU.mult)
                  psb = ps1[:, 512:1024].bitcast(BF16)
                  TR(psb[:, 0:128], rk[:, qi0:qi0 + 128], identb[:], [t_rk, t_identb], [pt1])
                  ACT(ktm[bi_][:], psb[:, 0:128], AF.Copy, [pt1, t_kdec], [t_ktm[bi_]], scale=kdec[:, h:h + 1])
                  ps_o, pt_o = next_ps()
                  for dvc in range(2):
                      MM(ps_o[:, dvc * 128:(dvc + 1) * 128], rv_tm[:, ri, dvc * 128:(dvc + 1) * 128], AT[bi_][:],
                         True, False, [t_rv, t_AT[bi_]], [pt_o])
                      MM(ps_o[:, dvc * 128:(dvc + 1) * 128], S_bf[:, h, dvc * 128:(dvc + 1) * 128], qd[bi_][:],
                         False, True, [t_Sbf[h], t_qd[bi_]], [pt_o])
                  ACT(sqo[:], ps_o[:, 0:256], AF.Square, [pt_o], [t_sqo])
                  MM(ps_o[:, 512:640], ones256[:], sqo[:, 0:128], True, False, [t_sqo, t_o256], [pt_o])
                  MM(ps_o[:, 512:640], ones256[:], sqo[:, 128:256], False, True, [t_sqo, t_o256], [pt_o])
                  RSTD(rstd_o[:], ps_o[:, 512:640], [pt_o], t_rso)
                  for dvc in range(2):
                      DVE("tensor_tensor", [pt_o, t_rso], [t_to], out=to_[:, dvc * 128:(dvc + 1) * 128],
                          in0=ps_o[:, dvc * 128:(dvc + 1) * 128], in1=rstd_o[:], op=ALU.mult)
                  q_lo = 126 if gb == 1 else 0
                  mc0 = c0 + q_lo - f0
                  DVE("tensor_tensor", [t_to, t_sg], [t_mix],
                      out=mixT[:, 16 + 2 * h:16 + 2 * h + 2, mc0:mc0 + 128 - q_lo],
                      in0=to_[:].rearrange("p (a b) -> p a b", a=2)[:, :, q_lo:128],
                      in1=sg[:, :, qi0 + q_lo:qi0 + 128], op=ALU.mult)
                  psS, ptS = PH[ri % 2], PHT[ri % 2]
                  MM(psS[:, 0:256], ktm[bi_][:], rv_tm[:, ri, :], True, True, [t_ktm[bi_], t_rv], [ptS])
                  DVE("scalar_tensor_tensor", [ptS, t_S[h]], [t_S[h]], out=S[:, h, :], in0=S[:, h, :],
                      scalar=float(g128[h]), in1=psS[:, 0:256], op0=ALU.mult, op1=ALU.add)
                  ACT(S_bf[:, h, :], S[:, h, :], AF.Copy, [t_S[h]], [t_Sbf[h]])
              if nsmp:
                  ri = len(qblocks)
                  qs0 = 512 - q0
                  cs_ = float(128.0 ** -0.5)
                  ps1, pt1 = next_ps()
                  psb = ps1[:, 0:512].bitcast(BF16)
                  TR(psb[0:16, 0:128], rk[:, qs0:qs0 + 16], identb[:], [t_rk, t_identb], [pt1])
                  ACT(ktm_s[0:16, :], psb[0:16, 0:128], AF.Copy, [pt1], [t_ktms], scale=cs_)
                  for s_ in range(16):
                      DVE("tensor_scalar", [t_ktms, t_identf], [t_kmask], out=kmask[0:16, s_, :], in0=ktm_s[0:16, :],
                          scalar1=identf[0:16, s_:s_ + 1], scalar2=None, op0=ALU.mult)
                  ps_os, pt_os = PH[0], PHT[0]
                  for s_ in range(16):
                      bi_ = s_ % 2
                      DMA("sp", s0b[bi_][:], sret[s_, h], writes=[t_s0[bi_]])
                      psS, ptS = next_ps()
                      MM(psS[:, 0:256], kmask[0:16, s_, :], rv_tm[0:16, ri, :], True, True, [t_kmask, t_rv], [ptS])
                      DVE("scalar_tensor_tensor", [ptS, t_s0[bi_]], [t_s0[bi_]], out=s0b[bi_][:], in0=s0b[bi_][:],
                          scalar=float(gam[h]), in1=psS[:, 0:256], op0=ALU.mult, op1=ALU.add)
                      OUT(o_rs[s_, h], s0b[bi_][:], [t_s0[bi_]])
                      ACT(s1bf[bi_][:], s0b[bi_][:], AF.Copy, [t_s0[bi_]], [t_s1bf[bi_]])
                      for dvc in range(2):
                          MM(ps_os[:, dvc * 16 + s_:dvc * 16 + s_ + 1], s1bf[bi_][:, dvc * 128:(dvc + 1) * 128],
                             rq[:, qs0 + s_:qs0 + s_ + 1], True, True, [t_s1bf[bi_], t_rq], [pt_os])
                  ACT(sqo[:, 0:32], ps_os[:, 0:32], AF.Square, [pt_os], [t_sqo])
                  MM(ps_os[:, 64:80], ones256[:], sqo[:, 0:16], True, False, [t_sqo, t_o256], [pt_os])
                  MM(ps_os[:, 64:80], ones256[:], sqo[:, 16:32], False, True, [t_sqo, t_o256], [pt_os])
                  RSTD(rstd_o[:, 0:16], ps_os[:, 64:80], [pt_os], t_rso)
                  for dvc in range(2):
                      DVE("tensor_tensor", [pt_os, t_rso], [t_to], out=to_[:, dvc * 16:(dvc + 1) * 16],
                          in0=ps_os[:, dvc * 16:(dvc + 1) * 16], in1=rstd_o[:, 0:16], op=ALU.mult)
                  DVE("tensor_tensor", [t_to, t_sg], [t_mix],
                      out=mixT[:, 16 + 2 * h:16 + 2 * h + 2, 512:528],
                      in0=to_[:, 0:32].rearrange("p (a b) -> p a b", a=2),
                      in1=sg[:, :, qs0:qs0 + 16], op=ALU.mult)
          if gi == 1:
              OUT(o_rp.rearrange("h k v -> k h v"), S[:], t_S)
          if dbg_mix is not None:
              OUT(dbg_mix[gi][:, :, 0:nf], mixT[:, :, 0:nf], [t_mix])
          def early(stage_no, g_at):
              return STAGE <= stage_no and gi == g_at
          if early(3, 0) or early(7, 1):
              raise EarlyExit()
          P.barrier()

          xs = [view(Rs, i * 2560, [128, 5, 128], F32) for i in range(2)]
          t_xs = [Tok(), Tok()]
          pcs_f = pieces(nf)
          nft = (nf + 127) // 128
          frow0 = row0 + f0
          for oc in range(32):
              wb, wt = load_w(w_out_v, oc * 128)
              xb, txb = xs[oc % 2], t_xs[oc % 2]
              nfull = nf // 128
              DMA("sp", xb[:, 0:nfull, :],
                  xm[frow0:frow0 + nfull * 128, oc * 128:(oc + 1) * 128].rearrange("(t p) c -> p t c", p=128),
                  writes=[txb])
              rem = nf - nfull * 128
              if rem:
                  DMA("sp", xb[0:rem, nfull, :], xm[frow0 + nfull * 128:frow0 + nf, oc * 128:(oc + 1) * 128],
                      writes=[txb])
              ps, pt = next_ps()

              def pcol(c):
                  for pi, (a, b) in enumerate(pcs_f):
                      if a <= c < b:
                          return pi * 512 + c - a
              for k in range(32):
                  for pi, (a, b) in enumerate(pcs_f):
                      MM(ps[:, pi * 512:pi * 512 + (b - a)], wb[:, k, :], mixT[:, k, a:b], k == 0, k == 31,
                         [wt, t_mix], [pt])
                  if k == 0:
                      for ti in range(nft):
                          n = min(128, nf - ti * 128)
                          c = pcol(ti * 128)
                          MM(ps[:, c:c + n], xb[0:n, ti, :], identf[0:n, 0:n], False, False, [txb, t_identf], [pt])
              for pi, (a, b) in enumerate(pcs_f):
                  COPY(xT_of(oc)[:, a:b], ps[:, pi * 512:pi * 512 + (b - a)], [pt], [t_xT])
          if early(4, 0):
              raise EarlyExit()
          P.barrier()

          h2T = view(Ra, 0, [128, 32, 528], BF16)
          t_h2 = Tok("h2T")
          sq = view(Rs, 0, [128, 32, 128], BF16)
          t_sq = Tok()
          rstd = view(Rs, 8192, [128, 528], F32)
          t_rstd = Tok()
          rmsnorm_fm(xT_of, t_xT, nf, g_ffn, t_gffn, lambda k: h2T[:, k, 0:nf], t_h2, sq, t_sq, rstd, t_rstd,
                     parts=xT_parts)
          P.barrier()
          aT = view(Rs, 0, [128, FCP, 528], BF16)
          t_aT = Tok("aT")
          u_sb = [view(Rs, 8448 + i * 2128, [128, 532], F32) for i in range(2)]
          t_u = [Tok(), Tok()]
          cgv = [view(Rs, 12704 + i * 2112, [128, 528], F32) for i in range(2)]
          t_c = [Tok(), Tok()]
          scs = [view(Rs, 16928 + i * 512, [128, 128], F32) for i in range(2)]
          t_scs = [Tok(), Tok()]
          scT = [view(Rs, 17952 + i * 128, [128, 32], F32) for i in range(2)]
          t_scT = [Tok(), Tok()]
          h2_of = lambda k: h2T[:, k, :]
          for typ in range(2):
              if gi == 0:
                  DVE("memset", [], [t_u[typ]], ap=u_sb[typ][:, 0:2], constant=0.0)
          fc0 = 0
          uo_cnt = 0
          while fc0 < NFC:
              npart = min(FCP, NFC - fc0)
              for fi in range(npart):
                  fc = fc0 + fi
                  for typ in range(2):
                      ch = typ * NFC + fc
                      wb, wt = load_w(w_up_v, ch * 128)
                      ps, pt, pcs = proj_fm(wb, wt, 32, h2_of, t_h2, 0, nf)
                      ub, tu = u_sb[typ], t_u[typ]
                      if gi == 1:
                          DVE("tensor_copy", [t_carry], [tu], out=ub[:, 0:2], in_=carry[:, ch, :])
                      for pi, (a, b) in enumerate(pcs):
                          ACT(ub[:, 2 + a:2 + b], ps[:, pi * 512:pi * 512 + (b - a)], AF.Copy, [pt], [tu])
                      if gi == 0:
                          DVE("tensor_copy", [tu], [t_carry], out=carry[:, ch, :], in_=ub[:, 2 + nf - 2:2 + nf])
                      cb, tcb = cgv[typ], t_c[typ]
                      ACT(cb[:, 0:nf], ub[:, 2:2 + nf], AF.Identity, [tu, t_convw, t_convb], [tcb],
                          scale=convw[:, 2, ch:ch + 1], bias=convb[:, ch:ch + 1])
                      DVE("scalar_tensor_tensor", [tu, tcb, t_convw], [tcb], out=cb[:, 0:nf], in0=ub[:, 1:1 + nf],
                          scalar=convw[:, 1, ch:ch + 1], in1=cb[:, 0:nf], op0=ALU.mult, op1=ALU.add)
                      DVE("scalar_tensor_tensor", [tu, tcb, t_convw], [tcb], out=cb[:, 0:nf], in0=ub[:, 0:nf],
                          scalar=convw[:, 0, ch:ch + 1], in1=cb[:, 0:nf], op0=ALU.mult, op1=ALU.add)
                      if gi == 1:
                          sb_, tsb = scs[typ], t_scs[typ]
                          DMA("sp", sb_[0:32, :], sconv[:, ch * 128:(ch + 1) * 128], writes=[tsb])
                          ps2, pt2 = next_ps()
                          TR(ps2[:, 0:32], sb_[0:32, :], identf[0:32, 0:32], [tsb, t_identf], [pt2])
                          COPY(scT[typ][:], ps2[:, 0:32], [pt2], [t_scT[typ]], eng="act")
                          sc3 = scT[typ][:].rearrange("p (s r) -> p s r", r=2)
                          ACT(cb[:, 512:528], ub[:, 514:530], AF.Identity, [tu, t_convw, t_convb], [tcb],
                              scale=convw[:, 2, ch:ch + 1], bias=convb[:, ch:ch + 1])
                          DVE("scalar_tensor_tensor", [t_scT[typ], tcb, t_convw], [tcb], out=cb[:, 512:528],
                              in0=sc3[:, :, 1], scalar=convw[:, 1, ch:ch + 1], in1=cb[:, 512:528],
                              op0=ALU.mult, op1=ALU.add)
                          DVE("scalar_tensor_tensor", [t_scT[typ], tcb, t_convw], [tcb], out=cb[:, 512:528],
                              in0=sc3[:, :, 0], scalar=convw[:, 0, ch:ch + 1], in1=cb[:, 512:528],
                              op0=ALU.mult, op1=ALU.add)
                          ps3, pt3 = next_ps()
                          TR(ps3[0:18, 0:128], ub[:, 2 + 510:2 + 528], identf[:], [tu, t_identf], [pt3])
                          j4 = fc % 4
                          COPY(ucp[typ][0:18, j4 * 128:(j4 + 1) * 128], ps3[0:18, 0:128], [pt3], [t_ucp[typ]], eng="act")
                          if j4 == 3 or fc == NFC - 1:
                              cbase = typ * FF + (fc // 4) * 512
                              w_ = (j4 + 1) * 128
                              OUT(o_cp[:, cbase:cbase + w_], ucp[typ][0:2, 0:w_], [t_ucp[typ]])
                              OUT(o_cs[:, 1, cbase:cbase + w_], ucp[typ][2:18, 0:w_], [t_ucp[typ]])
                  ACT(cgv[0][:, 0:nf], cgv[0][:, 0:nf], AF.Gelu, [t_c[0]], [t_c[0]])
                  DVE("tensor_tensor", [t_c[0], t_c[1]], [t_aT], out=aT[:, fi, 0:nf], in0=cgv[0][:, 0:nf],
                      in1=cgv[1][:, 0:nf], op=ALU.mult)
              for oc in range(32):
                  if oc % 4 == 0:
                      wb, wt = next_wb()
                      wbv = wb[:].rearrange("p k c -> p (k c)")[:, 0:npart * 512].rearrange("p (k c) -> p k c", c=512)
                      DMA("pool", wb[:].rearrange("p k c -> p (k c)")[:, 0:npart * 512],
                          w_down_v[oc // 4][:, fc0 * 512:(fc0 + npart) * 512], writes=[wt])
                  oj = oc % 4
                  ps, pt = next_ps()
                  for ki in range(npart):
                      for pi, (a, b) in enumerate(pcs_f):
                          MM(ps[:, pi * 512:pi * 512 + (b - a)], wbv[:, ki, oj * 128:(oj + 1) * 128], aT[:, ki, a:b],
                             ki == 0, ki == npart - 1, [wt, t_aT], [pt])
                  for pi, (a, b) in enumerate(pcs_f):
                      DVE("tensor_tensor", [pt, t_xT], [t_xT], out=xT_of(oc)[:, a:b], in0=xT_of(oc)[:, a:b],
                          in1=ps[:, pi * 512:pi * 512 + (b - a)], op=ALU.add)
              fc0 += npart
          if early(5, 0):
              raise EarlyExit()
          P.barrier()

          h3T = view(Ra, 0, [128, 32, 528], BF16)
          t_h3 = Tok("h3T")
          rmsnorm_fm(xT_of, t_xT, nf, g_ple, t_gple, lambda k: h3T[:, k, 0:nf], t_h3, sq, t_sq, rstd, t_rstd,
                     parts=xT_parts)
          P.barrier()
          pT = view(Rs, 0, [128, 2, 528], BF16)
          t_pT = Tok()
          p_tm = [view(Rs, 2112 + i * 1024, [128, 256], F32) for i in range(2)]
          t_ptm = [Tok(), Tok()]
          gate = view(Rs, 4160, [128, 528], F32)
          t_gate = Tok()
          prow0 = 0 if gi == 0 else 514
          for ti in range(nft):
              n = min(128, nf - ti * 128)
              pb, tpb = p_tm[ti % 2], t_ptm[ti % 2]
              DMA("sp", pb[0:n, :], pm[prow0 + ti * 128:prow0 + ti * 128 + n, :], writes=[tpb])
              ps, pt = next_ps()
              for c2 in range(2):
                  TR(ps[:, c2 * 128:c2 * 128 + n], pb[0:n, c2 * 128:(c2 + 1) * 128], identf[0:n, 0:n],
                     [tpb, t_identf], [pt])
              COPY(pT[:, :, ti * 128:ti * 128 + n], ps[:, 0:256].rearrange("p (c t) -> p c t", c=2)[:, :, 0:n],
                   [pt], [t_pT])
          h3_of = lambda k: h3T[:, k, :]
          for oc in range(32):
              wb, wt = load_w(w_gate_v, oc * 128)
              ps, pt, pcs = proj_fm(wb, wt, 32, h3_of, t_h3, 0, nf)
              for pi, (a, b) in enumerate(pcs):
                  ACT(gate[:, a:b], ps[:, pi * 512:pi * 512 + (b - a)], AF.Sigmoid, [pt], [t_gate])
              wb2, wt2 = load_w(w_proj_v, oc * 128, nk=2)
              ps2, pt2, pcs2 = proj_fm(wb2, wt2, 2, lambda k: pT[:, k, :], t_pT, 0, nf)
              for pi, (a, b) in enumerate(pcs2):
                  DVE("tensor_tensor", [pt2, t_gate], [t_gate], out=gate[:, a:b], in0=gate[:, a:b],
                      in1=ps2[:, pi * 512:pi * 512 + (b - a)], op=ALU.mult)
              DVE("tensor_tensor", [t_gate, t_xT], [t_xT], out=xT_of(oc)[:, 0:nf], in0=xT_of(oc)[:, 0:nf],
                  in1=gate[:, 0:nf], op=ALU.add)
          P.barrier()

          y_tm = view(Rs, 0, [128, 4096], F32)
          t_ytm = Tok()
          if gi == 0:
              otiles = [(2 + i * 128, 128, y_own[i * 128:(i + 1) * 128, :]) for i in range(4)]
          else:
              otiles = [(i * 128, 128, y_own[512 + i * 128:512 + (i + 1) * 128, :]) for i in range(4)]
              otiles.append((512, 16, y_smp))
          for (c0, n, dst) in otiles:
              for c4 in range(8):
                  ps, pt = next_ps()
                  for i in range(4):
                      k = c4 * 4 + i
                      TR(ps[0:n, i * 128:(i + 1) * 128], xT_of(k)[:, c0:c0 + n], identf[:], [t_xT, t_identf], [pt])
                  COPY(y_tm[0:n, c4 * 512:(c4 + 1) * 512], ps[0:n, 0:512], [pt], [t_ytm])
              OUT(dst, y_tm[0:n, :], [t_ytm])
          if early(6, 0):
              raise EarlyExit()

    try:
        _groups()
    except EarlyExit:
        P.op("sp", lambda e: e.nop(), reads=t_outs)
        P.emit()
        return nc, P
    OUT(o_cs[:, 0, :], sconv.rearrange("(s r) c -> s r c", r=2)[:, 1, :], [])
    P.op("sp", lambda e: e.nop(), reads=t_outs)
    P.emit()
    return nc, P


def kernel(x_prompt, x_sample, p_prompt, p_sample, cache_win_k, cache_win_v, state_ret, state_conv,
           rel_bias, g_mix, w_in, g_q, g_k, sinks, w_out, g_ffn, w_up, conv_w, conv_b, w_down,
           g_ple, w_ple_gate, w_ple_proj):
    f32 = np.float32
    consts, g128, gam, log_decay = host_consts()
    nc, P = build_nc(g128, gam)
    print("prog stats", P.stats, flush=True)
    if os.environ.get("MK_SITES"):
        import json
        json.dump({k: v for k, v in getattr(P, "names", {}).items()}, open("sites.json", "w"))

    def fm_vec(v):
        return np.ascontiguousarray(np.asarray(v, f32).reshape(32, 128).T)

    x_prompt = np.asarray(x_prompt, f32)
    shared = dict(consts)
    shared["identf2"] = consts["identf"]
    def blk(W, nk):
        W = np.asarray(W, f32)
        nb = W.shape[1] // 128
        return np.ascontiguousarray(W.reshape(nk, 128, nb, 128).transpose(2, 1, 0, 3)).reshape(nb, 128, nk * 128)
    shared["w_in"] = blk(w_in[0], 32)
    shared["w_out"] = blk(w_out[0], 32)
    shared["w_up"] = blk(w_up[0], 32)
    shared["w_gate"] = blk(w_ple_gate[0], 32)
    shared["w_proj"] = blk(w_ple_proj[0], 2)
    shared["w_down"] = np.ascontiguousarray(
        np.asarray(w_down[0], f32).reshape(NFC, 128, 8, 512).transpose(2, 1, 0, 3)).reshape(8, 128, NFC * 512)
    shared["g_mix"] = fm_vec(g_mix[0])
    shared["g_ffn"] = fm_vec(g_ffn[0])
    shared["g_ple"] = fm_vec(g_ple[0])
    shared["gq2"] = np.ascontiguousarray(np.concatenate([g_q[0], g_q[0]]).astype(f32).reshape(128, 1))
    shared["gk2"] = np.ascontiguousarray(np.concatenate([g_k[0], g_k[0]]).astype(f32).reshape(128, 1))
    cw = np.asarray(conv_w[0], f32)
    shared["convw"] = np.ascontiguousarray(cw.reshape(3, 2 * NFC, 128).transpose(2, 0, 1))
    shared["convb"] = np.ascontiguousarray(np.asarray(conv_b[0], f32).reshape(2 * NFC, 128).T)
    shared["relb"] = np.asarray(rel_bias, f32)
    sk = np.asarray(sinks[0], f32)
    sl = np.zeros((128, 16), f32)
    r0l = np.zeros((128, 16), f32)
    for g in range(4):
        for j in range(4):
            for half in range(2):
                sl[half * 64:(half + 1) * 64, 4 * g + j] = sk[8 * g + 2 * j + half]
    for c in range(16):
        for half in range(2):
            r0l[half * 64:(half + 1) * 64, c] = rel_bias[0, 2 * c + half]
    shared["sinkl"] = sl
    shared["relb0l"] = r0l

    in_maps = []
    for c in range(8):
        b, j = c // 4, c % 4
        t0 = 1024 * j
        m = dict(shared)
        xmr = np.zeros((1408, D), f32)
        if j > 0:
            xmr[0:256] = x_prompt[b, t0 - 256:t0]
        xmr[256:1280] = x_prompt[b, t0:t0 + 1024]
        xmr[1280:1296] = x_sample[16 * c:16 * c + 16, 0]
        m["xm"] = xmr
        xp = np.zeros((NPREV * 128, D), f32)
        npv = max(t0 - 128, 0)
        if npv > 0:
            xp[NPREV * 128 - npv:] = x_prompt[b, 0:npv]
        m["xprev"] = xp
        pmr = np.zeros((1042, 256), f32)
        pmr[2:1026] = p_prompt[0, b, t0:t0 + 1024]
        pmr[1026:1042] = p_sample[0, 16 * c:16 * c + 16, 0]
        m["pm"] = pmr
        m["ck"] = np.ascontiguousarray(np.asarray(cache_win_k[0, 16 * c:16 * c + 16], f32).reshape(16, 128, 256))
        m["cv"] = np.ascontiguousarray(np.asarray(cache_win_v[0, 16 * c:16 * c + 16], f32).reshape(16, 128, 256))
        m["sret"] = np.ascontiguousarray(np.asarray(state_ret[0, 16 * c:16 * c + 16], f32))
        m["sconv"] = np.ascontiguousarray(np.asarray(state_conv[0, 16 * c:16 * c + 16], f32).reshape(32, 2 * FF))
        posP = (t0 - 128 - NPREV * 128 + np.arange(NPREV * 128)).astype(np.int32)
        m["cosP"], m["sinP"] = rope_tables(posP)
        posA = (t0 - 128 + np.arange(640)).astype(np.int32)
        posB = np.concatenate([t0 + 512 + np.arange(512), np.full(16, 8192)]).astype(np.int32)
        m["cosA"], m["sinA"] = rope_tables(posA)
        m["cosB"], m["sinB"] = rope_tables(posB)
        m["hvalid"] = np.full((128, 64), 1.0 if j > 0 else 0.0, f32)
        in_maps.append(m)

    names = set()
    for alloc in nc.allocations:
        try:
            if alloc.kind == "ExternalInput":
                names.add(alloc.memorylocations[0].name)
        except Exception:
            pass
    if names:
        in_maps = [{k: v for k, v in m.items() if k in names} for m in in_maps]
    if os.environ.get("MK_CORES"):
        sel = [int(v) for v in os.environ["MK_CORES"].split(",")]
        res = run_bass_kernel_spmd(nc, [in_maps[i] for i in sel], core_ids=list(range(len(sel))))
        return dict(zip(sel, res.results))
    res = run_bass_kernel_spmd(nc, in_maps, core_ids=list(range(8)))
    R = res.results
    if STAGE < 99:
        return R

    yp = np.zeros((2, 4096, D), f32)
    ys = np.zeros((128, 1, D), f32)
    kp = np.zeros((1, 2, 128, 4, 64), f32)
    vp = np.zeros((1, 2, 128, 4, 64), f32)
    rp = np.zeros((1, 2, 8, 128, 256), f32)
    cp = np.zeros((1, 2, 2, 2 * FF), f32)
    ks = np.zeros((1, 128, 128, 4, 64), f32)
    vs = np.zeros((1, 128, 128, 4, 64), f32)
    rs = np.zeros((1, 128, 8, 128, 256), f32)
    cs = np.zeros((1, 128, 2, 2 * FF), f32)
    for c in range(8):
        b, j = c // 4, c % 4
        r = R[c]
        yp[b, 1024 * j:1024 * (j + 1)] = r["y_own"]
        ys[16 * c:16 * c + 16, 0] = r["y_smp"]
        if j == 3:
            kp[0, b] = r["o_kp"].reshape(128, 4, 64)
            vp[0, b] = r["o_vp"].reshape(128, 4, 64)
            rp[0, b] = r["o_rp"]
            cp[0, b] = r["o_cp"]
        ks[0, 16 * c:16 * c + 16] = r["o_ks"].reshape(16, 128, 4, 64)
        vs[0, 16 * c:16 * c + 16] = r["o_vs"].reshape(16, 128, 4, 64)
        rs[0, 16 * c:16 * c + 16] = r["o_rs"]
        cs[0, 16 * c:16 * c + 16] = r["o_cs"]
    return yp, ys, kp, vp, rp, cp, ks, vs, rs, cs
```
